# Optimizing a Trainium2 kernel written in Bass

```python
import math
import jax, jax.numpy as jnp
from jax import lax
import numpy as np

D_MODEL = 1024
BATCH = 4
SEQ = 8192
DEPTH = 2

D_MIX = D_MODEL
D_FF = 2816
NORM_EPS = 1e-6

POOL_WINDOWS = (2, 4, 8, 16)
POOL_GROUPS = len(POOL_WINDOWS)
POOL_GROUP_DIM = 64
POOL_WIDTH = POOL_GROUPS * POOL_GROUP_DIM

ATTN_HEADS = 8
HEAD_DIM = 64
ATTN_WIDTH = ATTN_HEADS * HEAD_DIM
Q_BLOCK = 128

CONV_WIDTH_CH = D_MIX - POOL_WIDTH - ATTN_WIDTH
CONV_KERNEL = 31

IN_COLS = POOL_WIDTH + 3 * ATTN_WIDTH + ATTN_HEADS + 2 * CONV_WIDTH_CH

kernel_name = "hymba_style_pool_fox_conformer_macaron"


def rms_norm(x, g):
    x32 = x.astype(jnp.float32)
    y = x32 * lax.rsqrt(jnp.mean(x32 * x32, axis=-1, keepdims=True) + NORM_EPS)
    return (y * g.astype(jnp.float32)).astype(x.dtype)


def layer_norm(x, g, b):
    x32 = x.astype(jnp.float32)
    mu = jnp.mean(x32, axis=-1, keepdims=True)
    xc = x32 - mu
    var = jnp.mean(xc * xc, axis=-1, keepdims=True)
    y = xc * lax.rsqrt(var + NORM_EPS)
    return (y * g.astype(jnp.float32) + b.astype(jnp.float32)).astype(x.dtype)


def swiglu(h, w_gate, w_up, w_down):
    return (jax.nn.silu(h @ w_gate) * (h @ w_up)) @ w_down


def causal_window_mean(u, w):
    S = u.shape[1]
    u32 = u.astype(jnp.float32)
    cs = jnp.cumsum(u32, axis=1)
    lagged = jnp.pad(cs, ((0, 0), (w, 0), (0, 0)))[:, :S]
    count = jnp.minimum(jnp.arange(S) + 1, w).astype(jnp.float32)
    return ((cs - lagged) / count[None, :, None]).astype(u.dtype)


def pool_mixer(u, pool_w, pool_scale):
    B, S, _ = u.shape
    ug = u.reshape(B, S, POOL_GROUPS, POOL_GROUP_DIM)
    pooled = jnp.stack(
        [causal_window_mean(ug[:, :, g], w) - ug[:, :, g] for g, w in enumerate(POOL_WINDOWS)],
        axis=2)
    mixed = jnp.einsum('bsgc,gcd->bsgd', pooled, pool_w)
    return mixed.reshape(B, S, POOL_WIDTH) * pool_scale


def forgetting_attention(q, k, v, z_f, forget_bias):
    B, S, H, Dh = q.shape
    n_blk = S // Q_BLOCK
    scale = 1.0 / math.sqrt(Dh)
    log_f = jax.nn.log_sigmoid(z_f.astype(jnp.float32) + forget_bias.astype(jnp.float32))
    F = jnp.cumsum(log_f, axis=1).transpose(0, 2, 1)
    qh = q.transpose(0, 2, 1, 3)
    kh = k.transpose(0, 2, 1, 3)
    vh = v.transpose(0, 2, 1, 3)
    q_blocks = qh.reshape(B, H, n_blk, Q_BLOCK, Dh).transpose(2, 0, 1, 3, 4)
    F_blocks = F.reshape(B, H, n_blk, Q_BLOCK).transpose(2, 0, 1, 3)
    k_pos = jnp.arange(S)

    def one_block(args):
        q_i, F_i, i = args
        s = jnp.einsum('bhqd,bhkd->bhqk', q_i, kh).astype(jnp.float32) * scale
        s = s + F_i[..., None] - F[:, :, None, :]
        q_pos = i * Q_BLOCK + jnp.arange(Q_BLOCK)
        mask = k_pos[None, :] <= q_pos[:, None]
        s = jnp.where(mask[None, None], s, -jnp.inf)
        p = jax.nn.softmax(s, axis=-1)
        return jnp.einsum('bhqk,bhkd->bhqd', p.astype(vh.dtype), vh)

    out = lax.map(one_block, (q_blocks, F_blocks, jnp.arange(n_blk)))
    return out.transpose(1, 0, 3, 2, 4).reshape(B, S, H * Dh)


def conformer_conv(h_glu, conv_w, conv_b, ln_g, ln_b):
    a, g = jnp.split(h_glu, 2, axis=-1)
    u = a * jax.nn.sigmoid(g)
    y = lax.conv_general_dilated(
        u, conv_w[:, None, :].astype(u.dtype), window_strides=(1,),
        padding=[(CONV_KERNEL - 1, 0)], dimension_numbers=('NWC', 'WIO', 'NWC'),
        feature_group_count=CONV_WIDTH_CH) + conv_b
    return jax.nn.silu(layer_norm(y, ln_g, ln_b))


def token_mixer(h, w_in, pool_w, pool_scale, forget_bias, conv_w, conv_b, conv_ln_g, conv_ln_b, w_out):
    B, S, _ = h.shape
    p = h @ w_in
    o = 0
    u_pool = p[..., o:o + POOL_WIDTH]; o += POOL_WIDTH
    q = p[..., o:o + ATTN_WIDTH]; o += ATTN_WIDTH
    k = p[..., o:o + ATTN_WIDTH]; o += ATTN_WIDTH
    v = p[..., o:o + ATTN_WIDTH]; o += ATTN_WIDTH
    z_f = p[..., o:o + ATTN_HEADS]; o += ATTN_HEADS
    h_glu = p[..., o:o + 2 * CONV_WIDTH_CH]
    shp = (B, S, ATTN_HEADS, HEAD_DIM)
    y_a = pool_mixer(u_pool, pool_w, pool_scale)
    y_b = forgetting_attention(q.reshape(shp), k.reshape(shp), v.reshape(shp), z_f, forget_bias)
    y_c = conformer_conv(h_glu, conv_w, conv_b, conv_ln_g, conv_ln_b)
    return jnp.concatenate([y_a, y_b, y_c], axis=-1) @ w_out


def setup_inputs(seed: int = 0) -> dict:
    key = jax.random.key(seed)
    ks = jax.random.split(key, 24)

    def nrm(k, shape, scale):
        return jax.random.normal(k, shape, jnp.float32) * scale

    def gain(k, shape):
        return 1.0 + 0.02 * jax.random.normal(k, shape, jnp.float32)

    L = DEPTH
    return {
        "x": nrm(ks[0], (BATCH, SEQ, D_MODEL), 1.0),
        "ffn1_norm": gain(ks[1], (L, D_MODEL)),
        "ffn1_w_gate": nrm(ks[2], (L, D_MODEL, D_FF), D_MODEL ** -0.5),
        "ffn1_w_up": nrm(ks[3], (L, D_MODEL, D_FF), D_MODEL ** -0.5),
        "ffn1_w_down": nrm(ks[4], (L, D_FF, D_MODEL), D_FF ** -0.5),
        "mix_norm": gain(ks[5], (L, D_MODEL)),
        "w_in": nrm(ks[6], (L, D_MODEL, IN_COLS), D_MODEL ** -0.5),
        "pool_w": nrm(ks[7], (L, POOL_GROUPS, POOL_GROUP_DIM, POOL_GROUP_DIM), POOL_GROUP_DIM ** -0.5),
        "pool_scale": gain(ks[8], (L, POOL_WIDTH)),
        "forget_bias": 2.0 + 0.1 * jax.random.normal(ks[9], (L, ATTN_HEADS), jnp.float32),
        "conv_w": nrm(ks[10], (L, CONV_KERNEL, CONV_WIDTH_CH), CONV_KERNEL ** -0.5),
        "conv_b": nrm(ks[11], (L, CONV_WIDTH_CH), 0.02),
        "conv_ln_g": gain(ks[12], (L, CONV_WIDTH_CH)),
        "conv_ln_b": nrm(ks[13], (L, CONV_WIDTH_CH), 0.02),
        "w_out": nrm(ks[14], (L, D_MIX, D_MODEL), D_MIX ** -0.5),
        "ffn2_norm": gain(ks[15], (L, D_MODEL)),
        "ffn2_w_gate": nrm(ks[16], (L, D_MODEL, D_FF), D_MODEL ** -0.5),
        "ffn2_w_up": nrm(ks[17], (L, D_MODEL, D_FF), D_MODEL ** -0.5),
        "ffn2_w_down": nrm(ks[18], (L, D_FF, D_MODEL), D_FF ** -0.5),
        "final_norm": gain(ks[19], (D_MODEL,)),
    }


def reference(x, ffn1_norm, ffn1_w_gate, ffn1_w_up, ffn1_w_down, mix_norm, w_in, pool_w, pool_scale,
              forget_bias, conv_w, conv_b, conv_ln_g, conv_ln_b, w_out, ffn2_norm, ffn2_w_gate,
              ffn2_w_up, ffn2_w_down, final_norm):
    for l in range(DEPTH):
        x = x + 0.5 * swiglu(rms_norm(x, ffn1_norm[l]), ffn1_w_gate[l], ffn1_w_up[l], ffn1_w_down[l])
        x = x + token_mixer(rms_norm(x, mix_norm[l]), w_in[l], pool_w[l], pool_scale[l], forget_bias[l],
                            conv_w[l], conv_b[l], conv_ln_g[l], conv_ln_b[l], w_out[l])
        x = x + 0.5 * swiglu(rms_norm(x, ffn2_norm[l]), ffn2_w_gate[l], ffn2_w_up[l], ffn2_w_down[l])
    return rms_norm(x, final_norm)
```

```python
import contextlib
import numpy as np
import concourse.bass as bass
import concourse.mybir as mybir
from concourse.bass_utils import run_bass_kernel_spmd

F32 = mybir.dt.float32
BF16 = mybir.dt.bfloat16
AF = mybir.ActivationFunctionType
ALU = mybir.AluOpType

NCORES = 8
S = 8192
SL = 4096
NB = 16
RX = 1544
GROUPS = [[0, 1], [2, 3], [4, 5], [6, 7]]
D = 1024
DFF = 2816
NL = 2
T = 512
NT = SL // T
KC = 8
FC = 22
H = 8
EPS = 1e-6
NEG = -30000.0
CVT = 4096
UPAD = 32


def wlayout():
    items = []
    for l in range(NL):
        for ffn in (1, 2):
            if ffn == 2:
                continue
            for f in range(FC):
                items.append((f"{l}.{ffn}.gu{f}", 2048))
            for c in range(KC):
                items.append((f"{l}.{ffn}.dn{c}", FC * 128))
        for c in range(2):
            items.append((f"{l}.pool{c}", 1024))
        for j in range(4):
            items.append((f"{l}.q{j}", 1024))
        for j in range(4):
            items.append((f"{l}.k{j}", 1024))
        for j in range(2):
            items.append((f"{l}.v{j}", 2048))
        items.append((f"{l}.zf", 64))
        for c in range(2):
            items.append((f"{l}.a{c}", 1024))
        for c in range(2):
            items.append((f"{l}.g{c}", 1024))
        items.append((f"{l}.pbd", 256))
        for c in range(KC):
            items.append((f"{l}.out{c}", 12 * 128))
        for f in range(FC):
            items.append((f"{l}.2.gu{f}", 2048))
        for c in range(KC):
            items.append((f"{l}.2.dn{c}", FC * 128))
    off = {}
    o = 0
    for n, w in items:
        off[n] = (o, w)
        o += w
    return items, off, o


WITEMS, WOFF, WTOT = wlayout()

PP_F1, PP_MX, PP_F2, PP_PS, PP_CB, PP_LG, PP_LB, PP_CW, PP_FB = 0, 8, 16, 24, 26, 28, 30, 32, 94
PPL = 96
PP_FIN = NL * PPL
PP_M0 = PP_FIN + 8
PP_M1 = PP_FIN + 9
PPW = PP_FIN + 10
CS_ID, CS_TRI, CS_INVW, CS_T0 = 0, 128, 256, 258
CSW = 258 + 32


def _colchunk(W, c0, n):
    return np.ascontiguousarray(W[:, c0:c0 + n].reshape(KC, 128, n).transpose(1, 0, 2)).reshape(128, KC * n)


def pack_weights(inp):
    Wf = np.zeros((128, WTOT), np.float32)

    def put(name, a):
        o, w = WOFF[name]
        assert a.shape == (128, w), (name, a.shape, w)
        Wf[:, o:o + w] = a

    for l in range(NL):
        for ffn in (1, 2):
            wg = inp[f"ffn{ffn}_w_gate"][l]
            wu = inp[f"ffn{ffn}_w_up"][l]
            wd = inp[f"ffn{ffn}_w_down"][l]
            for f in range(FC):
                put(f"{l}.{ffn}.gu{f}", np.concatenate([_colchunk(wg, f * 128, 128), _colchunk(wu, f * 128, 128)], axis=1))
            for c in range(KC):
                a = wd[:, c * 128:(c + 1) * 128].reshape(FC, 128, 128).transpose(1, 0, 2).reshape(128, FC * 128)
                put(f"{l}.{ffn}.dn{c}", a)
        wi = inp["w_in"][l]
        for c in range(2):
            put(f"{l}.pool{c}", _colchunk(wi, c * 128, 128))
        for j in range(4):
            put(f"{l}.q{j}", _colchunk(wi, 256 + j * 128, 128))
            put(f"{l}.k{j}", _colchunk(wi, 768 + j * 128, 128))
        for j in range(2):
            put(f"{l}.v{j}", _colchunk(wi, 1280 + j * 256, 256))
        put(f"{l}.zf", _colchunk(wi, 1792, 8))
        for c in range(2):
            put(f"{l}.a{c}", _colchunk(wi, 1800 + c * 128, 128))
            put(f"{l}.g{c}", _colchunk(wi, 2056 + c * 128, 128))
        pw = inp["pool_w"][l]
        bd = np.zeros((128, 2, 128), np.float32)
        for c in range(2):
            bd[0:64, c, 0:64] = pw[2 * c]
            bd[64:128, c, 64:128] = pw[2 * c + 1]
        put(f"{l}.pbd", bd.reshape(128, 256))
        wo = inp["w_out"][l]
        for c in range(KC):
            a = np.zeros((128, 12, 128), np.float32)
            for k in range(2):
                a[:, k, :] = wo[k * 128:(k + 1) * 128, c * 128:(c + 1) * 128]
                a[:, 10 + k, :] = wo[768 + k * 128:768 + (k + 1) * 128, c * 128:(c + 1) * 128]
            for h in range(H):
                a[0:64, 2 + h, :] = wo[256 + h * 64:256 + (h + 1) * 64, c * 128:(c + 1) * 128]
            put(f"{l}.out{c}", a.reshape(128, 12 * 128))
    return Wf


def pack_params(inp, rank):
    pp = np.zeros((128, PPW), np.float32)
    pp[:, PP_M0] = 1.0 if rank == 0 else 0.0
    pp[:, PP_M1] = 1.0 if rank == 1 else 0.0

    def colv(v, n):
        return np.ascontiguousarray(np.asarray(v).reshape(n, 128).T)

    for l in range(NL):
        b = l * PPL
        pp[:, b + PP_F1:b + PP_F1 + 8] = colv(inp["ffn1_norm"][l], 8)
        pp[:, b + PP_MX:b + PP_MX + 8] = colv(inp["mix_norm"][l], 8)
        pp[:, b + PP_F2:b + PP_F2 + 8] = colv(inp["ffn2_norm"][l], 8)
        pp[:, b + PP_PS:b + PP_PS + 2] = colv(inp["pool_scale"][l], 2)
        pp[:, b + PP_CB:b + PP_CB + 2] = colv(inp["conv_b"][l], 2)
        pp[:, b + PP_LG:b + PP_LG + 2] = colv(inp["conv_ln_g"][l], 2)
        pp[:, b + PP_LB:b + PP_LB + 2] = colv(inp["conv_ln_b"][l], 2)
        cw = np.asarray(inp["conv_w"][l])
        pp[:, b + PP_CW:b + PP_CW + 62] = cw.T.reshape(2, 128, 31).transpose(1, 0, 2).reshape(128, 62)
        pp[0:8, b + PP_FB] = np.asarray(inp["forget_bias"][l])
    pp[:, PP_FIN:PP_FIN + 8] = colv(inp["final_norm"], 8)
    return pp


def make_consts(rank):
    cs = np.zeros((128, CSW), np.float32)
    cs[:, CS_ID:CS_ID + 128] = np.eye(128, dtype=np.float32)
    s = np.arange(128)[:, None]
    t = np.arange(128)[None, :]
    cs[:, CS_TRI:CS_TRI + 128] = np.where(s > t, NEG, 0.0)
    wins = (2, 4, 8, 16)
    for c in range(2):
        for half in range(2):
            w = wins[2 * c + half]
            cs[half * 64:(half + 1) * 64, CS_INVW + c] = 1.0 / w
            for tt in range(16):
                cs[half * 64:(half + 1) * 64, CS_T0 + c * 16 + tt] = 1.0 / (min(tt + 1, w) if rank == 0 else w)
    return cs


def make_masks(rank):
    import ml_dtypes
    s = np.arange(128)[:, None]
    t = np.arange(512)[None, :]
    mk = np.zeros((128, 8, 512), np.float32)
    for j in range(4):
        tri = np.where((j * 128 + s) > t, NEG, 0.0)
        if rank == 0:
            mk[:, j, :] = tri
            mk[:, 4 + j, :] = NEG
        else:
            mk[:, j, :] = 0.0
            mk[:, 4 + j, :] = tri
    return mk.astype(ml_dtypes.bfloat16)


class Buf:
    __slots__ = ("name", "lw", "rd", "dsem", "dcnt")

    def __init__(self, name):
        self.name = name
        self.lw = None
        self.rd = {}
        self.dsem = None
        self.dcnt = 0


class Eng:
    def __init__(self, name, sem):
        self.name = name
        self.sem = sem
        self.tick = 0
        self.ops = []
        self.seen = {}


class Sched:
    def __init__(self):
        self.nsem = 0
        self.eng = {}
        for n in ("pe", "act", "dve", "pool", "sp"):
            self.eng[n] = Eng(n, self.newsem())
        self.bufs = []

    def newsem(self):
        self.nsem += 1
        return self.nsem - 1

    def buf(self, name):
        b = Buf(name)
        self.bufs.append(b)
        return b

    def _wait(self, eng, k, v):
        if eng.seen.get(k, 0) < v:
            eng.ops.append(("wait", k, v))
            eng.seen[k] = v

    def _deps(self, eng, reads, writes):
        w = {}

        def add(k, v):
            if v > w.get(k, 0):
                w[k] = v

        for b in reads:
            if b.lw:
                add(*b.lw)
        for b in writes:
            if b.lw:
                add(*b.lw)
            for k, v in b.rd.items():
                if k != eng.sem:
                    add(k, v)
        if eng.name == "pe":
            w.pop(eng.sem, None)
        for k, v in w.items():
            self._wait(eng, k, v)

    def op(self, en, fn, reads=(), writes=(), inc=True):
        eng = self.eng[en]
        self._deps(eng, reads, writes)
        if inc:
            eng.tick += 1
            eng.ops.append(("op", fn, eng.sem, 1))
            tick = eng.tick
        else:
            eng.ops.append(("op", fn, None, 0))
            tick = eng.tick + 1
        for b in reads:
            b.rd[eng.sem] = tick
        for b in writes:
            b.lw = (eng.sem, tick)
            b.rd = {}

    def dma(self, qn, pairs, reads=(), writes=()):
        q = self.eng[qn]
        self._deps(q, reads, writes)
        prim = writes[0] if writes else reads[0]
        if prim.dsem is None:
            prim.dsem = self.newsem()
        for (o, i) in pairs:
            prim.dcnt += 16
            q.ops.append(("op", (lambda e, o=o, i=i: e.dma_start(out=o, in_=i)), prim.dsem, 16))
        for b in reads:
            b.rd[prim.dsem] = prim.dcnt
        for b in writes:
            b.lw = (prim.dsem, prim.dcnt)
            b.rd = {}

    def collective(self, src_ap, dst_ap, bsrc, bdst):
        q = self.eng["pool"]
        self._deps(q, [bsrc], [bdst])
        if bdst.dsem is None:
            bdst.dsem = self.newsem()
        bdst.dcnt += 1
        q.ops.append(("op", (lambda e: e.collective_compute("AllGather", ALU.bypass, replica_groups=GROUPS,
                                                             ins=[src_ap], outs=[dst_ap])), bdst.dsem, 1))
        bsrc.rd[bdst.dsem] = bdst.dcnt
        bdst.lw = (bdst.dsem, bdst.dcnt)
        bdst.rd = {}

    def barrier(self):
        for e in self.eng.values():
            for e2 in self.eng.values():
                if e2.tick > 0:
                    self._wait(e, e2.sem, e2.tick)
            for b in self.bufs:
                if b.dsem is not None and b.dcnt > 0:
                    self._wait(e, b.dsem, b.dcnt)


def build_program(stop=None, debug=False):
    nc = bass.Bass("TRN2", target_bir_lowering=False)
    sc = Sched()
    es = contextlib.ExitStack()

    xin = nc.dram_tensor("xin", [KC, 128, SL], F32, kind="ExternalInput").ap()
    mkd = nc.dram_tensor("mk", [128, 8, T], BF16, kind="ExternalInput").ap()
    wf = nc.dram_tensor("wf", [128, WTOT], F32, kind="ExternalInput").ap()
    ppd = nc.dram_tensor("pp", [128, PPW], F32, kind="ExternalInput").ap()
    csd = nc.dram_tensor("cs", [128, CSW], F32, kind="ExternalInput").ap()
    yout = nc.dram_tensor("y", [KC, 128, SL], F32, kind="ExternalOutput").ap()
    wb = nc.dram_tensor("wb", [128, WTOT], BF16).ap()
    xs = nc.dram_tensor("xs", [KC, 128, SL], F32).ap()
    qs = nc.dram_tensor("qs", [H, 64, SL], BF16).ap()
    exs_t = [nc.dram_tensor(f"exs{i}", [RX, T], BF16) for i in range(NT)]
    exd_t = [nc.dram_tensor(f"exd{i}", [2 * RX, T], BF16) for i in range(NT)]
    exs = [t_.ap() for t_ in exs_t]
    exd = [t_.ap() for t_ in exd_t]
    lsrc_t = nc.dram_tensor("lsrc", [NT * 8, T], F32)
    ldst_t = nc.dram_tensor("ldst", [2 * NT * 8, T], F32)
    lsrc, ldst = lsrc_t.ap(), ldst_t.ap()

    def ex_k(ap, h):
        return ap[h * 64:(h + 1) * 64, :]

    def ex_v(ap):
        return ap[512:1032, :].rearrange("r c -> (r c)").rearrange("(h p k d) -> h p k d", h=H, p=128, k=4)

    def ex_u(ap, c):
        return ap[1032 + c * 128:1032 + (c + 1) * 128, :]

    def ex_up(ap, c):
        return ap[1288 + c * 128:1288 + (c + 1) * 128, :]

    def exd_rank(j, q):
        return exd[j][q * RX:(q + 1) * RX, :]
    gdrow = nc.dram_tensor("gdrow", [8, S + 32], F32).ap()
    gdtok = nc.dram_tensor("gdtok", [S + 32, 8], F32).ap()
    xqd = nc.dram_tensor("xqd", [3, H, T], BF16).ap()
    rdd = nc.dram_tensor("rdd", [2, T], F32).ap()

    def xview(ap, t):
        return ap.rearrange("c p t -> p c t")[:, :, t * T:(t + 1) * T]

    def sb(name, shape, dt):
        return es.enter_context(nc.sbuf_tensor(name, shape, dt))

    pp = sb("pp_sb", [128, PPW], F32)
    cs = sb("cs_sb", [128, CSW], F32)
    trib = sb("trib", [128, 128], BF16)
    identb = sb("identb", [128, 128], BF16)
    onesb = sb("onesb", [128, 128], BF16)
    onesf = sb("onesf", [128, 128], F32)
    ones8 = sb("ones8", [8, T], F32)
    epsc = sb("epsc", [128, 1], F32)
    zt = sb("zt", [128, 64], F32)
    ztb = sb("ztb", [128, 64], BF16)
    nfb = sb("nfb", [8, 2], F32)
    convdiag = sb("convdiag", [128, 2, 31, 128], BF16)
    poolbd = sb("poolbd", [128, 2, 128], BF16)
    gkall = sb("gkall", [128, S // 128, 8], F32)
    NX = 2
    xring = [sb(f"xr{i}", [128, KC, T], F32) for i in range(NX)]
    NW = 4
    wring = [sb(f"wr{i}", [128, FC * 128], BF16) for i in range(NW)]
    ARENA16 = 63 * 1024
    arena = sb("arena", [128, ARENA16], BF16)
    cv = {"off": 0}

    def carve(shape, dt):
        n = int(np.prod(shape[1:]))
        n16 = n * (2 if dt == F32 else 1)
        o = cv["off"]
        cv["off"] = o + (n16 + 15) // 16 * 16
        assert cv["off"] <= ARENA16, (cv["off"], ARENA16)
        v = arena[0:shape[0], o:o + n16]
        if dt == F32:
            v = v.bitcast(F32)
        if len(shape) == 3:
            v = v.rearrange("p (a b) -> p a b", a=shape[1])
        elif len(shape) == 4:
            v = v.rearrange("p (a b c) -> p a b c", a=shape[1], b=shape[2])
        return v

    cv["off"] = 0
    hT = carve([128, KC, T], BF16)
    G = carve([128, FC, T], BF16)
    sq = carve([128, KC, T], BF16)
    rr = carve([128, T], F32)
    sgs = [carve([128, T], BF16) for i in range(2)]
    NSTG = 4
    stg = [carve([128, T], BF16) for i in range(NSTG)]
    vst = carve([128, H, 4, 65], BF16)
    sgm = carve([128, T], F32)
    ex8 = carve([8, T], F32)
    lt8 = carve([8, T], F32)
    gts = [carve([8, T + 1], F32) for i in range(2)]
    cin = [carve([128, CVT], F32) for i in range(2)]
    cout = [carve([128, CVT], BF16) for i in range(2)]
    cv["off"] = 0
    NKP = 3
    kps = [carve([128, 2048], BF16) for i in range(NKP)]
    vps = [carve([128, 16, 128], BF16) for i in range(NKP)]
    qaugs = [carve([128, H, T], BF16) for i in range(2)]
    NP = 3
    pts = [carve([128, T], BF16) for i in range(NP)]
    grow = carve([8, T + 1], F32)
    growB = carve([8, T + 1], F32)
    grefbc = carve([128, 8], F32)
    grefB = carve([128, 8], F32)
    mk = carve([128, 8, T], BF16)
    haloA = carve([128, 2, 32], BF16)
    haloB = carve([128, 2, 32], BF16)
    phaloA = carve([128, 2, 16], BF16)
    phaloB = carve([128, 2, 16], BF16)
    biasTs = [carve([128, S // 128, 8], F32) for i in range(2)]
    _o = cv["off"]
    r1ts = [carve([128, T], F32) for i in range(2)]
    cv["off"] = _o
    xq = carve([8, T], F32)
    xr1 = carve([8, T], F32)
    x32 = carve([8, T], F32)
    xhi = carve([8, T], BF16)
    xmid = carve([8, T], BF16)
    xlo = carve([8, T], BF16)
    mixPs = [carve([128, 2, T], BF16) for i in range(2)]
    mixCs = [carve([128, 2, T], BF16) for i in range(2)]
    ybTs = [carve([64, H, T], BF16) for i in range(2)]
    uh = carve([128, 2, T + 32], BF16)
    uph = carve([128, 2, T + 16], BF16)
    sA = carve([128, 2, T + 16], F32)
    sB = carve([128, 2, T + 16], F32)
    pooledb = carve([128, 2, T], BF16)
    t0tmp = carve([128, 2, 16], F32)
    yconv = carve([128, 2, T], F32)
    ysq = sA[:, :, 0:T]
    mstat = carve([128, T], F32)
    msq = carve([128, T], F32)
    rstd = carve([128, T], F32)
    tmpn = carve([128, T], F32)
    tmpn2 = carve([128, T], F32)
    bcss = [carve([64, T], F32) for i in range(2)]

    psum = [es.enter_context(nc.psum_tensor(f"ps{i}", [128, T], F32)) for i in range(8)]

    B = sc.buf
    b_pp, b_cs, b_trib, b_identb, b_ones, b_zt, b_nfb = B("pp"), B("cs"), B("trib"), B("identb"), B("ones"), B("zt"), B("nfb")
    b_convdiag, b_poolbd, b_gkall = B("convdiag"), B("poolbd"), B("gkall")
    b_x = [[B(f"x{i}.{c}") for c in range(KC)] for i in range(NX)]
    b_w = [B(f"w{i}") for i in range(NW)]
    b_hT = [B(f"hT{c}") for c in range(KC)]
    b_G = [B(f"G{f}") for f in range(FC)]
    b_sq = [B(f"sq{c}") for c in range(KC)]
    b_rr = B("rr")
    b_sg = [B(f"sg{i}") for i in range(2)]
    b_stg = [B(f"stg{i}") for i in range(NSTG)]
    b_vst, b_sgm, b_ex8, b_lt8 = B("vst"), B("sgm"), B("ex8"), B("lt8")
    b_gt = [B("gt0"), B("gt1")]
    b_kp = [B(f"kp{i}") for i in range(NKP)]
    b_vp = [B(f"vp{i}") for i in range(NKP)]
    b_qaugs, b_grow, b_grefbc, b_biasTs = [B("qaug0"), B("qaug1")], B("grow"), B("grefbc"), [B("biasT0"), B("biasT1")]
    b_pt = [B(f"pt{i}") for i in range(NP)]
    b_xq, b_xr1, b_x32, b_xhi, b_xmid, b_xlo = B("xq"), B("xr1"), B("x32"), B("xhi"), B("xmid"), B("xlo")
    b_mixPs = [[B(f"mixP{i}{c}") for c in range(2)] for i in range(2)]
    b_mixCs = [[B(f"mixC{i}{c}") for c in range(2)] for i in range(2)]
    b_ybTs = [[B(f"ybT{i}{h}") for h in range(H)] for i in range(2)]
    b_uh, b_uph, b_sA, b_sB, b_pooledb, b_t0tmp = B("uh"), B("uph"), B("sA"), B("sB"), B("pooledb"), B("t0tmp")
    b_yconv, b_ysq_unused, b_mstat, b_msq, b_rstd, b_tmpn, b_r1t, b_bcs = (B("yconv"), B("ysq"), B("mstat"), B("msq"),
                                                                    B("rstd"), B("tmpn"), [B("r1t0"), B("r1t1")], [B("bcs0"), B("bcs1")])
    b_tmpn2 = B("tmpn2")
    b_growB, b_grefB, b_halo = B("growB"), B("grefB"), B("halo")
    b_rdd = B("rdd")
    b_ysq = b_sA
    b_cin = [B("cin0"), B("cin1")]
    b_cout = [B("cout0"), B("cout1")]
    b_ps = [B(f"ps{i}") for i in range(8)]
    NBLK = (WTOT + CVT - 1) // CVT
    EARLY_COLS = WOFF["0.pbd"][0]
    NEARLY = (EARLY_COLS + CVT - 1) // CVT
    PER_TILE = (NBLK - NEARLY + NT - 1) // NT
    def blk_group(bi):
        return 0 if bi < NEARLY else 1 + (bi - NEARLY) // PER_TILE
    b_wbg = [B(f"wb{g}") for g in range(2 + (NBLK - NEARLY) // PER_TILE)]
    def wb_bufs(o, w):
        return sorted({blk_group(bi) for bi in range(o // CVT, (o + w - 1) // CVT + 1)})
    b_xs = [B(f"xs{t}") for t in range(NT)]
    b_qs = [B(f"qs{t}") for t in range(NT)]
    b_exs = [B(f"exs{t}") for t in range(NT)]
    b_exd = [B(f"exd{t}") for t in range(NT)]
    b_lsrc, b_ldst, b_gd, b_mk = B("lsrc"), B("ldst"), B("gd"), B("mk")
    b_pad = B("pad")
    b_y = B("y")
    b_xqd = B("xqd")

    class Rot:
        def __init__(self, idx):
            self.idx = idx
            self.i = 0
            self.held = set()

        def get(self, hold=False):
            while True:
                k = self.idx[self.i % len(self.idx)]
                self.i += 1
                if k not in self.held:
                    break
            if hold:
                self.held.add(k)
            return psum[k], b_ps[k]

        def release(self, bps):
            self.held.discard(b_ps.index(bps))

    psr = Rot([0, 1, 2, 3, 4, 5])
    pso = Rot([6, 7])
    rot = {"stg": 0, "alt": 0, "kp": 0, "pt": 0, "sg": 0}

    def alt_eng():
        rot["alt"] += 1
        return "act" if rot["alt"] % 2 else "dve"

    def rsqrt_eps(out, in_, reads, writes):
        sc.op("act", lambda e: e.activation(out=out, in_=in_, func=AF.Sqrt, bias=epsc[:, 0:1], scale=1.0), reads=list(reads) + [b_ones], writes=writes)
        sc.op("dve", lambda e: e.reciprocal(out=out, in_=out), reads=writes, writes=writes)

    def copy_op(en, out, in_, reads, writes):
        if en == "act":
            sc.op("act", lambda e: e.activation(out=out, in_=in_, func=AF.Copy), reads, writes)
        else:
            sc.op(en, lambda e: e.tensor_copy(out=out, in_=in_), reads, writes)

    wstate = {"issued": 0, "used": 0, "order": []}

    def wplan(name):
        wstate["order"].append(name)

    def wissue_upto(n):
        while wstate["issued"] < min(n, len(wstate["order"])):
            i = wstate["issued"]
            o, w = WOFF[wstate["order"][i]]
            slot = i % NW
            sc.dma("sp", [(wring[slot][:, 0:w], wb[:, o:o + w])], reads=[b_wbg[g] for g in wb_bufs(o, w)], writes=[b_w[slot]])
            wstate["issued"] += 1

    def wget(name):
        i = wstate["used"]
        assert wstate["order"][i] == name, (wstate["order"][i], name)
        wissue_upto(i + NW - 1)
        wstate["used"] += 1
        slot = i % NW
        return wring[slot], b_w[slot]

    def plan_ffn(l, ffn):
        for f in range(FC):
            wplan(f"{l}.{ffn}.gu{f}")
        for c in range(KC):
            wplan(f"{l}.{ffn}.dn{c}")

    def plan_mixin(l):
        for j in range(4):
            wplan(f"{l}.q{j}")
        for j in range(4):
            wplan(f"{l}.k{j}")
        for j in range(2):
            wplan(f"{l}.v{j}")
        wplan(f"{l}.zf")
        for c in range(2):
            wplan(f"{l}.pool{c}")
        for c in range(2):
            wplan(f"{l}.a{c}")
            wplan(f"{l}.g{c}")

    for l in range(NL):
        wplan(f"{l}.pbd")
        for t in range(NT):
            if l > 0:
                plan_ffn(l - 1, 2)
            plan_ffn(l, 1)
            plan_mixin(l)
        for t in range(NT):
            for c in range(KC):
                wplan(f"{l}.out{c}")
    for t in range(NT):
        plan_ffn(NL - 1, 2)

    sc.dma("sp", [(pp[:, :], ppd)], writes=[b_pp])
    sc.dma("sp", [(cs[:, :], csd)], writes=[b_cs])
    sc.op("dve", lambda e: e.memset(zt[:, :], 0.0), writes=[b_zt])
    sc.op("dve", lambda e: e.memset(ztb[:, :], 0.0), writes=[b_zt])
    sc.op("dve", lambda e: e.memset(onesb[:, :], 1.0), writes=[b_ones])
    sc.op("dve", lambda e: e.memset(onesf[:, :], 1.0), writes=[b_ones])
    sc.op("dve", lambda e: e.memset(ones8[:, :], 1.0), writes=[b_ones])
    sc.op("dve", lambda e: e.memset(epsc[:, :], EPS), writes=[b_ones])
    sc.op("dve", lambda e: e.tensor_copy(out=trib[:, :], in_=cs[:, CS_TRI:CS_TRI + 128]), reads=[b_cs], writes=[b_trib])
    sc.op("dve", lambda e: e.tensor_copy(out=identb[:, :], in_=cs[:, CS_ID:CS_ID + 128]), reads=[b_cs], writes=[b_identb])
    for l in range(NL):
        sc.op("dve", lambda e, l=l: e.tensor_scalar(out=nfb[:, l:l + 1], in0=pp[0:8, l * PPL + PP_FB:l * PPL + PP_FB + 1],
                                                     scalar1=-1.0, scalar2=None, op0=ALU.mult), reads=[b_pp], writes=[b_nfb])
    sc.dma("pool", [(gdrow[:, 0:1], zt[0:8, 0:1]), (gdtok[0:1, :], zt[0:1, 0:8])], reads=[b_zt], writes=[b_pad])

    cvt_engs = ["dve", "act"]

    cvt_state = {"next": 0, "loaded": 0}

    def cvt_load(i):
        c0 = i * CVT
        w = min(CVT, WTOT - c0)
        sc.dma("sp", [(cin[i % 2][:, 0:w], wf[:, c0:c0 + w])], writes=[b_cin[i % 2]])

    def cvt_conv(i):
        c0 = i * CVT
        w = min(CVT, WTOT - c0)
        k = i % 2
        copy_op(cvt_engs[i % 2], cout[k][:, 0:w], cin[k][:, 0:w], [b_cin[k]], [b_cout[k]])
        sc.dma("pool", [(wb[:, c0:c0 + w], cout[k][:, 0:w])], reads=[b_cout[k]], writes=[b_wbg[blk_group(i)]])

    def convert_some(n, limit=None):
        lim = NBLK if limit is None else limit
        for _ in range(n):
            i = cvt_state["next"]
            if i >= lim:
                return
            while cvt_state["loaded"] < min(i + 2, NBLK):
                cvt_load(cvt_state["loaded"])
                cvt_state["loaded"] += 1
            cvt_conv(i)
            cvt_state["next"] += 1

    convert_some(NEARLY, NEARLY)

    def rmsnorm_to_hT(xt, bx, gcol):
        ps, bps = psr.get()
        for c in range(KC):
            sc.op("act", lambda e, c=c: e.activation(out=sq[:, c, :], in_=xt[:, c, :], func=AF.Square, scale=1.0 / 32.0),
                  reads=[bx[c]], writes=[b_sq[c]])
        for c in range(KC):
            sc.op("pe", lambda e, c=c: e.matmul(ps[:, :], onesb[:, :], sq[:, c, :], start=(c == 0), stop=(c == KC - 1)),
                  reads=[b_sq[c], b_ones], writes=[bps], inc=(c == KC - 1))
        rsqrt_eps(rr[:, :], ps[:, :], [bps], [b_rr])
        for c in range(KC):
            sc.op("dve", lambda e, c=c: e.scalar_tensor_tensor(out=hT[:, c, :], in0=xt[:, c, :], scalar=pp[:, gcol + c:gcol + c + 1],
                                                                in1=rr[:, :], op0=ALU.mult, op1=ALU.mult),
                  reads=[bx[c], b_rr, b_pp], writes=[b_hT[c]])

    dbgflag = {"first": debug}

    def ffn(l, which, xt, bx, mid=None):
        gcol = l * PPL + (PP_F1 if which == 1 else PP_F2)
        rmsnorm_to_hT(xt, bx, gcol)
        first = False
        for f in range(FC):
            wt, bw = wget(f"{l}.{which}.gu{f}")
            wv = wt[:, 0:2048].rearrange("p (g k n) -> p g k n", g=2, k=KC)
            psg, bpg = psr.get()
            psu, bpu = psr.get()
            for k in range(KC):
                sc.op("pe", lambda e, k=k, psg=psg, wv=wv: e.matmul(psg[:, :], wv[:, 0, k, :], hT[:, k, :], start=(k == 0), stop=(k == KC - 1)),
                      reads=[bw, b_hT[k]], writes=[bpg], inc=(k == KC - 1))
            for k in range(KC):
                sc.op("pe", lambda e, k=k, psu=psu, wv=wv: e.matmul(psu[:, :], wv[:, 1, k, :], hT[:, k, :], start=(k == 0), stop=(k == KC - 1)),
                      reads=[bw, b_hT[k]], writes=[bpu], inc=(k == KC - 1))
            si = rot["sg"] % 2
            rot["sg"] += 1
            sc.op("act", lambda e, psg=psg, si=si: e.activation(out=sgs[si][:, :], in_=psg[:, :], func=AF.Silu),
                  reads=[bpg], writes=[b_sg[si]])
            sc.op("dve", lambda e, f=f, psu=psu, si=si: e.tensor_tensor(out=G[:, f, :], in0=psu[:, :], in1=sgs[si][:, :], op=ALU.mult),
                  reads=[bpu, b_sg[si]], writes=[b_G[f]])
            if l == 0 and which == 1 and f % 3 == 2:
                convert_some(1)
        if mid is not None:
            mid()
        for c in range(KC):
            wt, bw = wget(f"{l}.{which}.dn{c}")
            wv = wt[:, 0:FC * 128].rearrange("p (k n) -> p k n", k=FC)
            psd, bpd = psr.get()
            for f in range(FC):
                sc.op("pe", lambda e, f=f, psd=psd, wv=wv: e.matmul(psd[:, :], wv[:, f, :], G[:, f, :], start=(f == 0), stop=(f == FC - 1)),
                      reads=[bw, b_G[f]], writes=[bpd], inc=(f == FC - 1))
            sc.op("dve", lambda e, c=c, psd=psd: e.scalar_tensor_tensor(out=xt[:, c, :], in0=psd[:, :], scalar=0.5, in1=xt[:, c, :],
                                                                        op0=ALU.mult, op1=ALU.add),
                  reads=[bpd, bx[c]], writes=[bx[c]])

    def next_stg():
        i = rot["stg"] % NSTG
        rot["stg"] += 1
        return stg[i], b_stg[i]

    def mixin(l, t, xt, bx):
        base = l * PPL
        rmsnorm_to_hT(xt, bx, base + PP_MX)
        t0, t1 = t * T, (t + 1) * T
        for which in ("q", "k"):
            for j in range(4):
                wt, bw = wget(f"{l}.{which}{j}")
                wv = wt[:, 0:1024].rearrange("p (k n) -> p k n", k=KC)
                ps, bps = psr.get()
                for k in range(KC):
                    sc.op("pe", lambda e, k=k, ps=ps, wv=wv: e.matmul(ps[:, :], wv[:, k, :], hT[:, k, :], start=(k == 0), stop=(k == KC - 1)),
                          reads=[bw, b_hT[k]], writes=[bps], inc=(k == KC - 1))
                st, bst = next_stg()
                if which == "q":
                    sc.op("act", lambda e, ps=ps, st=st: e.activation(out=st[:, :], in_=ps[:, :], func=AF.Copy, scale=0.125),
                          reads=[bps], writes=[bst])
                    sc.dma("pool", [(qs[2 * j + hh, :, t0:t1], st[hh * 64:(hh + 1) * 64, :]) for hh in range(2)], reads=[bst], writes=[b_qs[t]])
                else:
                    sc.op("dve", lambda e, ps=ps, st=st: e.tensor_copy(out=st[:, :], in_=ps[:, :]), reads=[bps], writes=[bst])
                    sc.dma("pool", [(ex_k(exs[t], 2 * j + hh), st[hh * 64:(hh + 1) * 64, :]) for hh in range(2)], reads=[bst], writes=[b_exs[t]])
        for j in range(2):
            wt, bw = wget(f"{l}.v{j}")
            wv = wt[:, 0:2048].rearrange("p (k n) -> p k n", k=KC)
            for s in range(4):
                ps, bps = psr.get()
                for k in range(KC):
                    sc.op("pe", lambda e, k=k, ps=ps, wv=wv, s=s: e.matmul(ps[:, 0:256], hT[:, k, s * 128:(s + 1) * 128], wv[:, k, :],
                                                                           start=(k == 0), stop=(k == KC - 1)),
                          reads=[bw, b_hT[k]], writes=[bps], inc=(k == KC - 1))
                en = alt_eng()
                copy_op(en, vst[:, 4 * j:4 * j + 4, s, 0:64], ps[:, 0:256].rearrange("p (h d) -> p h d", h=4), [bps], [b_vst])
        sc.dma("pool", [(ex_v(exs[t]).rearrange("h p k d -> p h k d"), vst[:, :, :, :])], reads=[b_vst], writes=[b_exs[t]])
        wt, bw = wget(f"{l}.zf")
        wv = wt[:, 0:64].rearrange("p (k n) -> p k n", k=KC)
        ps, bps = psr.get()
        for k in range(KC):
            sc.op("pe", lambda e, k=k, ps=ps, wv=wv: e.matmul(ps[0:8, :], wv[:, k, :], hT[:, k, :], start=(k == 0), stop=(k == KC - 1)),
                  reads=[bw, b_hT[k]], writes=[bps], inc=(k == KC - 1))
        sc.op("act", lambda e, ps=ps: e.activation(out=ex8[:, :], in_=ps[0:8, :], func=AF.Exp, bias=nfb[:, l:l + 1], scale=-1.0),
              reads=[bps, b_nfb], writes=[b_ex8])
        sc.op("act", lambda e: e.activation(out=lt8[:, :], in_=ex8[:, :], func=AF.Ln, bias=1.0, scale=1.0), reads=[b_ex8], writes=[b_lt8])
        sc.dma("pool", [(lsrc[t * 8:(t + 1) * 8, :], lt8[:, :])], reads=[b_lt8], writes=[b_lsrc])
        for c in range(2):
            wt, bw = wget(f"{l}.pool{c}")
            wv = wt[:, 0:1024].rearrange("p (k n) -> p k n", k=KC)
            ps, bps = psr.get()
            for k in range(KC):
                sc.op("pe", lambda e, k=k, ps=ps, wv=wv: e.matmul(ps[:, :], wv[:, k, :], hT[:, k, :], start=(k == 0), stop=(k == KC - 1)),
                      reads=[bw, b_hT[k]], writes=[bps], inc=(k == KC - 1))
            st, bst = next_stg()
            copy_op("act", st[:, :], ps[:, :], [bps], [bst])
            sc.dma("pool", [(ex_up(exs[t], c), st[:, :])], reads=[bst], writes=[b_exs[t]])
        for c in range(2):
            wta, bwa = wget(f"{l}.a{c}")
            wva = wta[:, 0:1024].rearrange("p (k n) -> p k n", k=KC)
            psa, bpa = psr.get()
            for k in range(KC):
                sc.op("pe", lambda e, k=k, psa=psa, wva=wva: e.matmul(psa[:, :], wva[:, k, :], hT[:, k, :], start=(k == 0), stop=(k == KC - 1)),
                      reads=[bwa, b_hT[k]], writes=[bpa], inc=(k == KC - 1))
            wtg, bwg = wget(f"{l}.g{c}")
            wvg = wtg[:, 0:1024].rearrange("p (k n) -> p k n", k=KC)
            psg, bpg = psr.get()
            for k in range(KC):
                sc.op("pe", lambda e, k=k, psg=psg, wvg=wvg: e.matmul(psg[:, :], wvg[:, k, :], hT[:, k, :], start=(k == 0), stop=(k == KC - 1)),
                      reads=[bwg, b_hT[k]], writes=[bpg], inc=(k == KC - 1))
            sc.op("act", lambda e, psg=psg: e.activation(out=sgm[:, :], in_=psg[:, :], func=AF.Sigmoid), reads=[bpg], writes=[b_sgm])
            st, bst = next_stg()
            sc.op("dve", lambda e, psa=psa, st=st: e.tensor_tensor(out=st[:, :], in0=psa[:, :], in1=sgm[:, :], op=ALU.mult),
                  reads=[bpa, b_sgm], writes=[bst])
            sc.dma("pool", [(ex_u(exs[t], c), st[:, :])], reads=[bst], writes=[b_exs[t]])

    def load_x(src_ap, src_bufs, t):
        slot = load_x.n % NX
        load_x.n += 1
        sc.dma("sp", [(xring[slot][:, :, :], xview(src_ap, t))], reads=src_bufs, writes=b_x[slot])
        return xring[slot], b_x[slot]
    load_x.n = 0

    def loop_a(l):
        src = xin if l == 0 else xs
        sc.op("dve", lambda e: e.memset(vst[:, :, :, :], 1.0), writes=[b_vst])
        layer_setup_b(l)
        nxt = load_x(src, [] if l == 0 else [b_xs[0]], 0)
        for t in range(NT):
            xt, bx = nxt
            if t + 1 < NT:
                nxt = load_x(src, [] if l == 0 else [b_xs[t + 1]], t + 1)
            if l > 0:
                ffn(l - 1, 2, xt, bx)
            ffn(l, 1, xt, bx)
            sc.dma("pool", [(xview(xs, t), xt[:, :, :])], reads=bx, writes=[b_xs[t]])
            mixin(l, t, xt, bx)
            if l == 0:
                convert_some(PER_TILE - 7)
            sc.collective(exs_t[t].ap().opt(), exd_t[t].ap().opt(), b_exs[t], b_exd[t])
        sc.collective(lsrc_t.ap().opt(), ldst_t.ap().opt(), b_lsrc, b_ldst)
        lall = ldst.rearrange("(q j h) t -> h j q t", q=2, j=NT, h=8)
        for hh in range(2):
            sc.dma("sp", [(cin[hh][0:8, :].rearrange("h (j q t) -> h j q t", j=NT // 2, q=2)[:, :, q_, :],
                           lall[:, (NT // 2) * hh:(NT // 2) * (hh + 1), q_, :]) for q_ in range(2)],
                   reads=[b_ldst], writes=[b_cin[hh]])
        for g in range(NB):
            hh, bi = g // 8, g % 8
            seg = cin[hh][0:8, bi * T:(bi + 1) * T]
            if g == 0:
                sc.op("dve", lambda e, seg=seg: e.tensor_tensor_scan(out=seg, data0=ones8[:, :], data1=seg, initial=0.0, op0=ALU.mult, op1=ALU.add),
                      reads=[b_cin[hh], b_ones], writes=[b_cin[hh]])
            else:
                ph, pb = (g - 1) // 8, (g - 1) % 8
                carry = cin[ph][0:8, (pb + 1) * T - 1:(pb + 1) * T]
                sc.op("dve", lambda e, seg=seg, carry=carry: e.tensor_tensor_scan(out=seg, data0=ones8[:, :], data1=seg, initial=carry,
                                                                                op0=ALU.mult, op1=ALU.add),
                      reads=[b_cin[hh], b_cin[ph], b_ones], writes=[b_cin[hh]])
        sc.dma("pool", [(gdrow[:, 1 + hh * 4096:1 + (hh + 1) * 4096], cin[hh][0:8, :]) for hh in range(2)], reads=b_cin, writes=[b_gd])
        ps, bps = psr.get()
        for kt in range(S // 128):
            hh, off = kt // 32, (kt % 32) * 128
            sc.op("pe", lambda e, kt=kt, hh=hh, off=off: e.transpose(ps[:, kt * 8:(kt + 1) * 8], cin[hh][0:8, off:off + 128], cs[0:8, CS_ID:CS_ID + 8]),
                  reads=[b_cin[hh], b_cs], writes=[bps], inc=(kt == S // 128 - 1))
        sc.op("dve", lambda e: e.tensor_copy(out=gkall[:, :, :], in_=ps[:, :].rearrange("p (k h) -> p k h", h=8)), reads=[bps], writes=[b_gkall])
        sc.dma("pool", [(gdtok[1:1 + S, :].rearrange("(k p) h -> p k h", p=128), gkall[:, :, :])], reads=[b_gkall], writes=[b_gd])

    def layer_setup_b(l):
        base = l * PPL
        wt, bw = wget(f"{l}.pbd")
        sc.op("dve", lambda e: e.tensor_copy(out=poolbd[:, :, :], in_=wt[:, 0:256].rearrange("p (c n) -> p c n", c=2)),
              reads=[bw], writes=[b_poolbd])
        for c in range(2):
            for j in range(31):
                col = base + PP_CW + c * 31 + j
                sc.op("dve", lambda e, c=c, j=j, col=col: e.tensor_scalar(out=convdiag[:, c, j, :], in0=cs[:, CS_ID:CS_ID + 128],
                                                                            scalar1=pp[:, col:col + 1], scalar2=None, op0=ALU.mult),
                      reads=[b_cs, b_pp], writes=[b_convdiag])

    def prologue_stages(l, i):
        base = l * PPL
        par = i % 2
        t0, t1 = i * T, (i + 1) * T
        nkt = 8 * (i + 1)
        qaug, b_qaug, biasT, b_biasT = qaugs[par], b_qaugs[par], biasTs[par], b_biasTs[par]
        mixP, mixC, b_mixP, b_mixC = mixPs[par], mixCs[par], b_mixPs[par], b_mixCs[par]
        W = T + 16
        st = {}
        m0c = pp[:, PP_M0:PP_M0 + 1]
        m1c = pp[:, PP_M1:PP_M1 + 1]

        def s0():
            gA, gB = 2 * i, 2 * i + 1
            sc.dma("sp", [(qaug[0:64, :, :], qs.rearrange("h d t -> d h t")[:, :, t0:t1])], reads=[b_qs[i]], writes=[b_qaug])
            sc.dma("sp", [(grow[:, :], gdrow[:, gA * T:gA * T + T + 1])], reads=[b_gd, b_pad], writes=[b_grow])
            sc.dma("sp", [(growB[:, :], gdrow[:, gB * T:gB * T + T + 1])], reads=[b_gd, b_pad], writes=[b_growB])
            sc.dma("sp", [(grefbc[:, :], gdtok[gA * T:gA * T + 1, :].partition_broadcast(128).squeeze(1))], reads=[b_gd, b_pad], writes=[b_grefbc])
            sc.dma("sp", [(grefB[:, :], gdtok[gB * T:gB * T + 1, :].partition_broadcast(128).squeeze(1))], reads=[b_gd, b_pad], writes=[b_grefB])
            sc.op("dve", lambda e: e.tensor_scalar(out=growB[:, :], in0=growB[:, :], scalar1=m1c[0:8, :], scalar2=None, op0=ALU.mult),
                  reads=[b_growB, b_pp], writes=[b_growB])
            sc.op("dve", lambda e: e.scalar_tensor_tensor(out=grow[:, :], in0=grow[:, :], scalar=m0c[0:8, :], in1=growB[:, :], op0=ALU.mult, op1=ALU.add),
                  reads=[b_grow, b_growB, b_pp], writes=[b_grow])
            sc.op("dve", lambda e: e.tensor_scalar(out=grefB[:, :], in0=grefB[:, :], scalar1=m1c, scalar2=None, op0=ALU.mult),
                  reads=[b_grefB, b_pp], writes=[b_grefB])
            sc.op("dve", lambda e: e.scalar_tensor_tensor(out=grefbc[:, :], in0=grefbc[:, :], scalar=m0c, in1=grefB[:, :], op0=ALU.mult, op1=ALU.add),
                  reads=[b_grefbc, b_grefB, b_pp], writes=[b_grefbc])
            sc.dma("sp", [(uph[:, c, 16:T + 16], ex_up(exs[i], c)) for c in range(2)] + [(uh[:, c, 32:T + 32], ex_u(exs[i], c)) for c in range(2)],
                   reads=[b_exs[i]], writes=[b_uph, b_uh])
            if i > 0:
                prv = exd_rank(i - 1, 1)
                sc.dma("sp", [(haloA[:, c, :], ex_u(prv, c)[:, T - 32:T]) for c in range(2)] + [(phaloA[:, c, :], ex_up(prv, c)[:, T - 16:T]) for c in range(2)],
                       reads=[b_exd[i - 1]], writes=[b_halo])
            else:
                sc.op("dve", lambda e: e.memset(haloA[:, :, :], 0.0), writes=[b_halo])
                sc.op("dve", lambda e: e.memset(phaloA[:, :, :], 0.0), writes=[b_halo])
            cur0 = exd_rank(i, 0)
            sc.dma("sp", [(haloB[:, c, :], ex_u(cur0, c)[:, T - 32:T]) for c in range(2)] + [(phaloB[:, c, :], ex_up(cur0, c)[:, T - 16:T]) for c in range(2)],
                   reads=[b_exd[i]], writes=[b_halo])
            for (hA, hB, dst, w_) in ((haloA, haloB, uh, 32), (phaloA, phaloB, uph, 16)):
                sc.op("dve", lambda e, hB=hB: e.tensor_scalar(out=hB[:, :, :], in0=hB[:, :, :], scalar1=m1c, scalar2=None, op0=ALU.mult),
                      reads=[b_halo, b_pp], writes=[b_halo])
                sc.op("dve", lambda e, hA=hA, hB=hB, dst=dst, w_=w_: e.scalar_tensor_tensor(out=dst[:, :, 0:w_], in0=hA[:, :, :], scalar=m0c, in1=hB[:, :, :],
                                                                                          op0=ALU.mult, op1=ALU.add),
                      reads=[b_halo, b_pp], writes=[b_uph, b_uh])
            sc.op("dve", lambda e: e.tensor_scalar(out=xq[:, :], in0=grow[:, 1:T + 1], scalar1=grow[:, 0:1], scalar2=-1.0, op0=ALU.subtract, op1=ALU.mult),
                  reads=[b_grow], writes=[b_xq])
            sc.op("dve", lambda e: e.tensor_copy(out=xhi[:, :], in_=xq[:, :]), reads=[b_xq], writes=[b_xhi])
            sc.op("dve", lambda e: e.tensor_copy(out=x32[:, :], in_=xhi[:, :]), reads=[b_xhi], writes=[b_x32])
            sc.op("dve", lambda e: e.tensor_tensor(out=xr1[:, :], in0=xq[:, :], in1=x32[:, :], op=ALU.subtract), reads=[b_xq, b_x32], writes=[b_xr1])
            sc.op("dve", lambda e: e.tensor_copy(out=xmid[:, :], in_=xr1[:, :]), reads=[b_xr1], writes=[b_xmid])
            sc.op("dve", lambda e: e.tensor_copy(out=x32[:, :], in_=xmid[:, :]), reads=[b_xmid], writes=[b_x32])
            sc.op("dve", lambda e: e.tensor_tensor(out=xr1[:, :], in0=xr1[:, :], in1=x32[:, :], op=ALU.subtract), reads=[b_xr1, b_x32], writes=[b_xr1])
            sc.op("dve", lambda e: e.tensor_copy(out=xlo[:, :], in_=xr1[:, :]), reads=[b_xr1], writes=[b_xlo])
            sc.dma("pool", [(xqd[0, :, :], xhi[:, :]), (xqd[1, :, :], xmid[:, :]), (xqd[2, :, :], xlo[:, :])],
                   reads=[b_xhi, b_xmid, b_xlo], writes=[b_xqd])
            sc.dma("pool", [(qaug[64:67, :, :], xqd[:, :, :])], reads=[b_xqd], writes=[b_qaug])
            sc.op("dve", lambda e: e.tensor_tensor(out=biasT[:, 0:nkt, :], in0=gkall[:, 0:nkt, :],
                                                   in1=grefbc[:, :].unsqueeze(1).to_broadcast([128, nkt, 8]), op=ALU.subtract),
                  reads=[b_gkall, b_grefbc], writes=[b_biasT])
            sc.op("dve", lambda e: e.tensor_tensor(out=sA[:, :, 1:W], in0=uph[:, :, 1:W], in1=uph[:, :, 0:W - 1], op=ALU.add),
                  reads=[b_uph], writes=[b_sA])
            sc.op("dve", lambda e: e.tensor_tensor(out=sB[:, :, 3:W], in0=sA[:, :, 3:W], in1=sA[:, :, 1:W - 2], op=ALU.add),
                  reads=[b_sA], writes=[b_sB])
            sc.op("dve", lambda e: e.tensor_tensor(out=sA[:, 1, 7:W], in0=sB[:, 1, 7:W], in1=sB[:, 1, 3:W - 4], op=ALU.add),
                  reads=[b_sB], writes=[b_sA])
            sc.op("dve", lambda e: e.tensor_tensor(out=sB[64:128, 1, 15:W], in0=sA[64:128, 1, 15:W], in1=sA[64:128, 1, 7:W - 8], op=ALU.add),
                  reads=[b_sA], writes=[b_sB])
            srcs = [(sA, 0, 0), (sB, 0, 1), (sA, 1, 0), (sB, 1, 1)]
            for (stile, c, half) in srcs:
                p0, p1 = half * 64, (half + 1) * 64
                sc.op("dve", lambda e, stile=stile, c=c, p0=p0, p1=p1: e.scalar_tensor_tensor(
                    out=pooledb[p0:p1, c, :], in0=stile[p0:p1, c, 16:W], scalar=cs[p0:p1, CS_INVW + c:CS_INVW + c + 1],
                    in1=uph[p0:p1, c, 16:W], op0=ALU.mult, op1=ALU.subtract), reads=[b_sA, b_sB, b_uph, b_cs], writes=[b_pooledb])
            if i == 0:
                for (stile, c, half) in srcs:
                    p0, p1 = half * 64, (half + 1) * 64
                    sc.op("dve", lambda e, stile=stile, c=c, p0=p0, p1=p1: e.tensor_tensor(
                        out=t0tmp[p0:p1, c, :], in0=stile[p0:p1, c, 16:32], in1=cs[p0:p1, CS_T0 + c * 16:CS_T0 + (c + 1) * 16], op=ALU.mult),
                        reads=[b_sA, b_sB, b_cs], writes=[b_t0tmp])
                    sc.op("dve", lambda e, c=c, p0=p0, p1=p1: e.tensor_tensor(
                        out=pooledb[p0:p1, c, 0:16], in0=t0tmp[p0:p1, c, :], in1=uph[p0:p1, c, 16:32], op=ALU.subtract),
                        reads=[b_t0tmp, b_uph], writes=[b_pooledb])

        def s1():
            for c in range(2):
                ps, bps = psr.get()
                sc.op("pe", lambda e, c=c, ps=ps: e.matmul(ps[:, :], poolbd[:, c, :], pooledb[:, c, :], start=True, stop=True),
                      reads=[b_poolbd, b_pooledb], writes=[bps])
                col = base + PP_PS + c
                sc.op("act", lambda e, c=c, ps=ps, col=col: e.activation(out=mixP[:, c, :], in_=ps[:, :], func=AF.Identity, scale=pp[:, col:col + 1]),
                      reads=[bps, b_pp], writes=[b_mixP[c]])
            st["pc"] = []
            for c in range(2):
                ps, bps = psr.get(hold=True)
                for j in range(31):
                    sc.op("pe", lambda e, c=c, j=j, ps=ps: e.matmul(ps[:, :], convdiag[:, c, j, :], uh[:, c, 2 + j:2 + j + T], start=(j == 0), stop=(j == 30)),
                          reads=[b_convdiag, b_uh], writes=[bps], inc=(j == 30))
                st["pc"].append((ps, bps))

        def s2():
            for c in range(2):
                ps, bps = st["pc"][c]
                col = base + PP_CB + c
                sc.op("act", lambda e, c=c, ps=ps, col=col: e.activation(out=yconv[:, c, :], in_=ps[:, :], func=AF.Identity, bias=pp[:, col:col + 1], scale=1.0),
                      reads=[bps, b_pp], writes=[b_yconv])
                psr.release(bps)
            sc.op("dve", lambda e: e.tensor_tensor(out=ysq[:, :, :], in0=yconv[:, :, :], in1=yconv[:, :, :], op=ALU.mult), reads=[b_yconv], writes=[b_ysq])

        def s3():
            ps1, bp1 = psr.get(hold=True)
            ps2, bp2 = psr.get(hold=True)
            for c in range(2):
                sc.op("pe", lambda e, c=c: e.matmul(ps1[:, :], onesf[:, :], yconv[:, c, :], start=(c == 0), stop=(c == 1)),
                      reads=[b_ones, b_yconv], writes=[bp1], inc=(c == 1))
            for c in range(2):
                sc.op("pe", lambda e, c=c: e.matmul(ps2[:, :], onesf[:, :], ysq[:, c, :], start=(c == 0), stop=(c == 1)),
                      reads=[b_ones, b_ysq], writes=[bp2], inc=(c == 1))
            st["ln"] = (ps1, bp1, ps2, bp2)

        def s4():
            ps1, bp1, ps2, bp2 = st["ln"]
            sc.op("dve", lambda e: e.tensor_scalar(out=mstat[:, :], in0=ps1[:, :], scalar1=1.0 / 256.0, scalar2=None, op0=ALU.mult), reads=[bp1], writes=[b_mstat])
            sc.op("dve", lambda e: e.tensor_tensor(out=msq[:, :], in0=mstat[:, :], in1=mstat[:, :], op=ALU.mult), reads=[b_mstat], writes=[b_msq])
            sc.op("dve", lambda e: e.scalar_tensor_tensor(out=rstd[:, :], in0=ps2[:, :], scalar=1.0 / 256.0, in1=msq[:, :], op0=ALU.mult, op1=ALU.subtract),
                  reads=[bp2, b_msq], writes=[b_rstd])
            psr.release(bp1)
            psr.release(bp2)
            for c, (tt, bt) in enumerate(((tmpn, b_tmpn), (tmpn2, b_tmpn2))):
                sc.op("dve", lambda e, c=c, tt=tt: e.tensor_tensor(out=tt[:, :], in0=yconv[:, c, :], in1=mstat[:, :], op=ALU.subtract),
                      reads=[b_yconv, b_mstat], writes=[bt])

        def s5():
            rsqrt_eps(rstd[:, :], rstd[:, :], [b_rstd], [b_rstd])
            for c, (tt, bt) in enumerate(((tmpn, b_tmpn), (tmpn2, b_tmpn2))):
                sc.op("dve", lambda e, tt=tt: e.tensor_tensor(out=tt[:, :], in0=tt[:, :], in1=rstd[:, :], op=ALU.mult), reads=[bt, b_rstd], writes=[bt])
            for c, (tt, bt) in enumerate(((tmpn, b_tmpn), (tmpn2, b_tmpn2))):
                cg, cb = base + PP_LG + c, base + PP_LB + c
                sc.op("act", lambda e, c=c, cg=cg, cb=cb, tt=tt: e.activation(out=mixC[:, c, :], in_=tt[:, :], func=AF.Silu, bias=pp[:, cb:cb + 1], scale=pp[:, cg:cg + 1]),
                      reads=[bt, b_pp], writes=[b_mixC[c]])

        return [s0, s1, s2, s3, s4, s5]

    LOOKAHEAD = 2

    def attention_layer(l, slot_hooks):
        work_all = []
        for i in range(NT):
            nkt = 8 * (i + 1)
            npc = (nkt + 15) // 16
            for h in range(H):
                for pc in range(npc):
                    work_all.append((i, h, pc))
        loaded = {"n": 0}

        def issue_loads(upto):
            while loaded["n"] < min(upto, len(work_all)):
                n = loaded["n"]
                i, h, pc = work_all[n]
                nkt = 8 * (i + 1)
                nk = min(16, nkt - pc * 16)
                slot = n % NKP
                kpairs, vpairs, rds = [], [], []
                for jj in range(nk // 8):
                    j = 2 * pc + jj
                    rds.append(b_exd[j])
                    both = exd[j].rearrange("(q r) c -> r q c", q=2)
                    kpairs.append((kps[slot][0:64, jj * 1024:(jj + 1) * 1024].rearrange("p (q c) -> p q c", q=2), both[h * 64:(h + 1) * 64, :, :]))
                    for q_ in range(2):
                        vsrc = ex_v(exd_rank(j, q_))[h]
                        vpairs.append((vps[slot][:, jj * 8 + q_ * 4:jj * 8 + q_ * 4 + 4, 0:65], vsrc))
                sc.dma("sp", kpairs, reads=rds, writes=[b_kp[slot]])
                sc.dma("sp", vpairs, reads=rds, writes=[b_vp[slot]])
                loaded["n"] += 1

        def qk(i, h, g, j, kp, bkp, vp, bvp, qaug, b_qaug):
            jd = g - 8 * i
            c0 = 0
            pss, bpss = psr.get()
            if jd >= 0:
                sc.op("pe", lambda e: e.matmul(pss[:, :], kp[:, j * 128:(j + 1) * 128], qaug[:, h, :], start=True, stop=False),
                      reads=[bkp, b_qaug], writes=[bpss], inc=False)
                sc.op("pe", lambda e: e.matmul(pss[:, :], identb[:, :], mk[:, jd, :], start=False, stop=True),
                      reads=[b_identb, b_mk], writes=[bpss])
            else:
                sc.op("pe", lambda e: e.matmul(pss[:, :], kp[:, j * 128:(j + 1) * 128], qaug[:, h, :], start=True, stop=True),
                      reads=[bkp, b_qaug], writes=[bpss])
            return (j, g, c0, pss, bpss, vp, bvp)

        def exp_pv(item, h, nkt, biasT, b_biasT, pso_t, bpo):
            j, g, c0, pss, bpss, vp, bvp = item
            pi = rot["pt"] % NP
            rot["pt"] += 1
            pt, bpt = pts[pi], b_pt[pi]
            sc.op("act", lambda e: e.activation(out=pt[:, c0:T], in_=pss[:, c0:T], func=AF.Exp, bias=biasT[:, g, h:h + 1], scale=1.0),
                  reads=[bpss, b_biasT], writes=[bpt])
            sc.op("pe", lambda e: e.matmul(pso_t[:, c0:T], vp[:, j, :], pt[:, c0:T], start=(g == 0), stop=(g == nkt - 1), skip_group_check=True),
                  reads=[bvp, bpt], writes=[bpo], inc=(g == nkt - 1))

        def fin_a(h, pso_t, bpo):
            k = h % 2
            sc.op("dve", lambda e: e.reciprocal(out=r1ts[k][64:65, :], in_=pso_t[64:65, :]), reads=[bpo], writes=[b_r1t[k]])
            sc.dma("pool", [(rdd[k:k + 1, :], r1ts[k][64:65, :])], reads=[b_r1t[k]], writes=[b_rdd])
            sc.dma("pool", [(bcss[k][:, :], rdd[k:k + 1, :].partition_broadcast(64).squeeze(1))], reads=[b_rdd], writes=[b_bcs[k]])

        def fin_b(i, h, pso_t, bpo):
            k = h % 2
            yb, byb = ybTs[i % 2], b_ybTs[i % 2]
            sc.op("dve", lambda e: e.tensor_tensor(out=yb[:, h, :], in0=pso_t[0:64, :], in1=bcss[k][:, :], op=ALU.mult),
                  reads=[bpo, b_bcs[k]], writes=[byb[h]])

        pending_fin = None
        n = 0
        for i in range(NT):
            par = i % 2
            qaug, b_qaug, biasT, b_biasT = qaugs[par], b_qaugs[par], biasTs[par], b_biasTs[par]
            nkt = 8 * (i + 1)
            npc = (nkt + 15) // 16
            hooks = slot_hooks(i)
            for h in range(H):
                cur_o = pso.get()
                pend = []
                cnt = 0
                for pc in range(npc):
                    issue_loads(n + 2)
                    slot = n % NKP
                    n += 1
                    nk = min(16, nkt - pc * 16)
                    for j in range(nk):
                        pend.append(qk(i, h, pc * 16 + j, j, kps[slot], b_kp[slot], vps[slot], b_vp[slot], qaug, b_qaug))
                        if len(pend) > LOOKAHEAD:
                            exp_pv(pend.pop(0), h, nkt, biasT, b_biasT, cur_o[0], cur_o[1])
                        cnt += 1
                        if cnt == min(nkt, 8) and pending_fin is not None:
                            fin_b(*pending_fin)
                            pending_fin = None
                while pend:
                    exp_pv(pend.pop(0), h, nkt, biasT, b_biasT, cur_o[0], cur_o[1])
                if pending_fin is not None:
                    fin_b(*pending_fin)
                fin_a(h, cur_o[0], cur_o[1])
                pending_fin = (i, h, cur_o[0], cur_o[1])
                if h in hooks:
                    hooks[h]()
            if "end" in hooks:
                hooks["end"]()
        fin_b(*pending_fin)

    def w_out(l, i, xt, bx):
        par = i % 2
        mixP, mixC, b_mixP, b_mixC = mixPs[par], mixCs[par], b_mixPs[par], b_mixCs[par]
        yb, byb = ybTs[par], b_ybTs[par]
        for c in range(KC):
            wt, bw = wget(f"{l}.out{c}")
            wv = wt[:, 0:1536].rearrange("p (k n) -> p k n", k=12)
            ps, bps = psr.get()
            ops = []
            for k in range(2):
                ops.append((wv[:, k, :], mixP[:, k, :], b_mixP[k]))
            for h in range(H):
                ops.append((wv[0:64, 2 + h, :], yb[:, h, :], byb[h]))
            for k in range(2):
                ops.append((wv[:, 10 + k, :], mixC[:, k, :], b_mixC[k]))
            for n, (lh, rh, br) in enumerate(ops):
                sc.op("pe", lambda e, lh=lh, rh=rh, n=n, ps=ps: e.matmul(ps[:, :], lh, rh, start=(n == 0), stop=(n == 11)),
                      reads=[bw, br], writes=[bps], inc=(n == 11))
            sc.op("dve", lambda e, c=c, ps=ps: e.tensor_tensor(out=xt[:, c, :], in0=ps[:, :], in1=xt[:, c, :], op=ALU.add),
                  reads=[bps, bx[c]], writes=[bx[c]])

    def loop_b(l):
        for i in range(NKP):
            sc.op("dve", lambda e, i=i: e.memset(kps[i][64:128, :], 0.0), writes=[b_kp[i]])
            sc.op("dve", lambda e, i=i: e.memset(kps[i][64:67, :], 1.0), writes=[b_kp[i]])
            sc.op("dve", lambda e, i=i: e.memset(vps[i][:, :, :], 1.0), writes=[b_vp[i]])
        for i in range(2):
            sc.op("dve", lambda e, i=i: e.memset(qaugs[i][64:128, :, :], 0.0), writes=[b_qaugs[i]])
        sc.dma("sp", [(mk[:, :, :], mkd)], writes=[b_mk])
        xtiles = {}
        xtiles[0] = load_x(xs, [b_xs[0]], 0)
        for s in prologue_stages(l, 0):
            s()

        def finish_slot(i):
            xt, bx = xtiles.pop(i)
            w_out(l, i, xt, bx)
            sc.dma("pool", [(xview(xs, i), xt[:, :, :])], reads=bx, writes=[b_xs[i]])

        def slot_hooks(i):
            hooks = {}
            stages = prologue_stages(l, i + 1) if i + 1 < NT else None

            def h0():
                if i > 0:
                    finish_slot(i - 1)
                if i + 1 < NT:
                    stages[0]()
            hooks[0] = h0
            if stages is not None:
                def h2():
                    xtiles[i + 1] = load_x(xs, [b_xs[i + 1]], i + 1)
                hooks.update({1: stages[1], 2: h2, 3: stages[2], 4: stages[3], 6: stages[4], "end": stages[5]})
            return hooks

        attention_layer(l, slot_hooks)
        finish_slot(NT - 1)

    def final_norm(xt, bx):
        ps, bps = psr.get()
        for c in range(KC):
            sc.op("act", lambda e, c=c: e.activation(out=sq[:, c, :], in_=xt[:, c, :], func=AF.Square, scale=1.0 / 32.0),
                  reads=[bx[c]], writes=[b_sq[c]])
        for c in range(KC):
            sc.op("pe", lambda e, c=c: e.matmul(ps[:, :], onesb[:, :], sq[:, c, :], start=(c == 0), stop=(c == KC - 1)),
                  reads=[b_sq[c], b_ones], writes=[bps], inc=(c == KC - 1))
        rsqrt_eps(rr[:, :], ps[:, :], [bps], [b_rr])
        for c in range(KC):
            sc.op("dve", lambda e, c=c: e.scalar_tensor_tensor(out=xt[:, c, :], in0=xt[:, c, :], scalar=pp[:, PP_FIN + c:PP_FIN + c + 1],
                                                                in1=rr[:, :], op0=ALU.mult, op1=ALU.mult),
                  reads=[bx[c], b_rr, b_pp], writes=[bx[c]])

    def loop_c():
        l = NL - 1
        cur = {"nxt": load_x(xs, [b_xs[0]], 0)}
        for t in range(NT):
            xt, bx = cur["nxt"]

            def mid(t=t):
                if t + 1 < NT:
                    cur["nxt"] = load_x(xs, [b_xs[t + 1]], t + 1)
            ffn(l, 2, xt, bx, mid=mid)
            final_norm(xt, bx)
            sc.dma("pool", [(xview(yout, t), xt[:, :, :])], reads=bx, writes=[b_y])

    def dump():
        sc.barrier()
        b_dbg = B("dbg")
        pairs = [(yout[c], xs[c]) for c in range(KC)]
        sc.dma("pool", pairs, writes=[b_dbg])
        sc.barrier()

    stage = 0
    done = False
    for l in range(NL):
        loop_a(l)
        sc.barrier()
        stage += 1
        if stop == stage:
            dump(); done = True; break
        loop_b(l)
        sc.barrier()
        stage += 1
        if stop == stage:
            dump(); done = True; break
    if not done:
        loop_c()
        sc.barrier()

    with es:
        sems = [es.enter_context(nc.semaphore(f"s{i}")) for i in range(sc.nsem)]
        es.enter_context(nc.allow_non_contiguous_dma(reason="tiny per-row scalars"))
        block = es.enter_context(nc.Block())

        def emit(eng_name):
            def body(e):
                for rec in sc.eng[eng_name].ops:
                    if rec[0] == "wait":
                        e.wait_ge(sems[rec[1]], rec[2])
                    else:
                        ins = rec[1](e)
                        if rec[2] is not None:
                            ins.then_inc(sems[rec[2]], rec[3])
            return body

        block.sync(emit("sp"))
        block.tensor(emit("pe"))
        block.scalar(emit("act"))
        block.vector(emit("dve"))
        block.gpsimd(emit("pool"))
    return nc


_CACHE = {}


def kernel(**inputs):
    inp = {k: np.asarray(v) for k, v in inputs.items()}
    x = inp["x"].astype(np.float32, copy=False)
    if "nc" not in _CACHE:
        _CACHE["nc"] = build_program()
    nc = _CACHE["nc"]
    Wf = pack_weights(inp)
    pps = [pack_params(inp, r) for r in range(2)]
    css = [make_consts(r) for r in range(2)]
    mks = [make_masks(r) for r in range(2)]
    in_maps = []
    for c in range(NCORES):
        b, r = c // 2, c % 2
        xl = x[b].reshape(NB, T, D)[r::2].reshape(SL, D)
        xT = np.ascontiguousarray(xl.T).reshape(KC, 128, SL)
        in_maps.append({"xin": xT, "wf": Wf, "pp": pps[r], "cs": css[r], "mk": mks[r]})
    res = run_bass_kernel_spmd(nc, in_maps, core_ids=list(range(NCORES)))
    out = np.empty((NCORES // 2, S, D), np.float32)
    for c in range(NCORES):
        b, r = c // 2, c % 2
        yl = res.results[c]["y"].reshape(D, SL).T.reshape(NT, T, D)
        out[b].reshape(NB, T, D)[r::2] = yl
    return out
```

```python
import contextlib
import numpy as np
import concourse.bass as bass
import concourse.mybir as mybir
from concourse.bass_utils import run_bass_kernel_spmd

F32 = mybir.dt.float32
BF16 = mybir.dt.bfloat16
AF = mybir.ActivationFunctionType
ALU = mybir.AluOpType

NCORES = 8
S = 8192
SL = 4096
NB = 16
RX = 1544
GROUPS = [[0, 1], [2, 3], [4, 5], [6, 7]]
D = 1024
DFF = 2816
NL = 2
T = 512
NT = SL // T
KC = 8
FC = 22
H = 8
EPS = 1e-6
NEG = -30000.0
CVT = 4096
UPAD = 32


def wlayout():
    items = []
    for l in range(NL):
        for ffn in (1, 2):
            if ffn == 2:
                continue
            for f in range(FC):
                items.append((f"{l}.{ffn}.gu{f}", 2048))
            for c in range(KC):
                items.append((f"{l}.{ffn}.dn{c}", FC * 128))
        for c in range(2):
            items.append((f"{l}.pool{c}", 1024))
        for j in range(4):
            items.append((f"{l}.q{j}", 1024))
        for j in range(4):
            items.append((f"{l}.k{j}", 1024))
        for j in range(2):
            items.append((f"{l}.v{j}", 2048))
        items.append((f"{l}.zf", 64))
        for c in range(2):
            items.append((f"{l}.a{c}", 1024))
        for c in range(2):
            items.append((f"{l}.g{c}", 1024))
        items.append((f"{l}.pbd", 256))
        for c in range(KC):
            items.append((f"{l}.out{c}", 12 * 128))
        for f in range(FC):
            items.append((f"{l}.2.gu{f}", 2048))
        for c in range(KC):
            items.append((f"{l}.2.dn{c}", FC * 128))
    off = {}
    o = 0
    for n, w in items:
        off[n] = (o, w)
        o += w
    return items, off, o


WITEMS, WOFF, WTOT = wlayout()

PP_F1, PP_MX, PP_F2, PP_PS, PP_CB, PP_LG, PP_LB, PP_CW, PP_FB = 0, 8, 16, 24, 26, 28, 30, 32, 94
PPL = 96
PP_FIN = NL * PPL
PP_M0 = PP_FIN + 8
PP_M1 = PP_FIN + 9
PP_M0N = PP_FIN + 10
PPW = PP_FIN + 11
CS_ID, CS_TRI, CS_INVW, CS_T0 = 0, 128, 256, 258
CSW = 258 + 32


def _colchunk(W, c0, n):
    return np.ascontiguousarray(W[:, c0:c0 + n].reshape(KC, 128, n).transpose(1, 0, 2)).reshape(128, KC * n)


def pack_weights(inp):
    Wf = np.zeros((128, WTOT), np.float32)

    def put(name, a):
        o, w = WOFF[name]
        assert a.shape == (128, w), (name, a.shape, w)
        Wf[:, o:o + w] = a

    for l in range(NL):
        for ffn in (1, 2):
            wg = inp[f"ffn{ffn}_w_gate"][l]
            wu = inp[f"ffn{ffn}_w_up"][l]
            wd = inp[f"ffn{ffn}_w_down"][l]
            for f in range(FC):
                put(f"{l}.{ffn}.gu{f}", np.concatenate([_colchunk(wg, f * 128, 128), _colchunk(wu, f * 128, 128)], axis=1))
            for c in range(KC):
                a = wd[:, c * 128:(c + 1) * 128].reshape(FC, 128, 128).transpose(1, 0, 2).reshape(128, FC * 128)
                put(f"{l}.{ffn}.dn{c}", a)
        wi = inp["w_in"][l]
        for c in range(2):
            put(f"{l}.pool{c}", _colchunk(wi, c * 128, 128))
        for j in range(4):
            put(f"{l}.q{j}", _colchunk(wi, 256 + j * 128, 128))
            put(f"{l}.k{j}", _colchunk(wi, 768 + j * 128, 128))
        for j in range(2):
            put(f"{l}.v{j}", _colchunk(wi, 1280 + j * 256, 256))
        put(f"{l}.zf", _colchunk(wi, 1792, 8))
        for c in range(2):
            put(f"{l}.a{c}", _colchunk(wi, 1800 + c * 128, 128))
            put(f"{l}.g{c}", _colchunk(wi, 2056 + c * 128, 128))
        pw = inp["pool_w"][l]
        bd = np.zeros((128, 2, 128), np.float32)
        for c in range(2):
            bd[0:64, c, 0:64] = pw[2 * c]
            bd[64:128, c, 64:128] = pw[2 * c + 1]
        put(f"{l}.pbd", bd.reshape(128, 256))
        wo = inp["w_out"][l]
        for c in range(KC):
            a = np.zeros((128, 12, 128), np.float32)
            for k in range(2):
                a[:, k, :] = wo[k * 128:(k + 1) * 128, c * 128:(c + 1) * 128]
                a[:, 10 + k, :] = wo[768 + k * 128:768 + (k + 1) * 128, c * 128:(c + 1) * 128]
            for h in range(H):
                a[0:64, 2 + h, :] = wo[256 + h * 64:256 + (h + 1) * 64, c * 128:(c + 1) * 128]
            put(f"{l}.out{c}", a.reshape(128, 12 * 128))
    return Wf


def pack_params(inp, rank):
    pp = np.zeros((128, PPW), np.float32)
    pp[:, PP_M0] = 1.0 if rank == 0 else 0.0
    pp[:, PP_M1] = 1.0 if rank == 1 else 0.0
    pp[:, PP_M0N] = NEG if rank == 0 else 0.0

    def colv(v, n):
        return np.ascontiguousarray(np.asarray(v).reshape(n, 128).T)

    for l in range(NL):
        b = l * PPL
        pp[:, b + PP_F1:b + PP_F1 + 8] = colv(inp["ffn1_norm"][l], 8)
        pp[:, b + PP_MX:b + PP_MX + 8] = colv(inp["mix_norm"][l], 8)
        pp[:, b + PP_F2:b + PP_F2 + 8] = colv(inp["ffn2_norm"][l], 8)
        pp[:, b + PP_PS:b + PP_PS + 2] = colv(inp["pool_scale"][l], 2)
        pp[:, b + PP_CB:b + PP_CB + 2] = colv(inp["conv_b"][l], 2)
        pp[:, b + PP_LG:b + PP_LG + 2] = colv(inp["conv_ln_g"][l], 2)
        pp[:, b + PP_LB:b + PP_LB + 2] = colv(inp["conv_ln_b"][l], 2)
        cw = np.asarray(inp["conv_w"][l])
        pp[:, b + PP_CW:b + PP_CW + 62] = cw.T.reshape(2, 128, 31).transpose(1, 0, 2).reshape(128, 62)
        pp[0:8, b + PP_FB] = np.asarray(inp["forget_bias"][l])
    pp[:, PP_FIN:PP_FIN + 8] = colv(inp["final_norm"], 8)
    return pp


def make_consts(rank):
    cs = np.zeros((128, CSW), np.float32)
    cs[:, CS_ID:CS_ID + 128] = np.eye(128, dtype=np.float32)
    s = np.arange(128)[:, None]
    t = np.arange(128)[None, :]
    cs[:, CS_TRI:CS_TRI + 128] = np.where(s > t, NEG, 0.0)
    wins = (2, 4, 8, 16)
    for c in range(2):
        for half in range(2):
            w = wins[2 * c + half]
            cs[half * 64:(half + 1) * 64, CS_INVW + c] = 1.0 / w
            for tt in range(16):
                cs[half * 64:(half + 1) * 64, CS_T0 + c * 16 + tt] = 1.0 / (min(tt + 1, w) if rank == 0 else w)
    return cs


def make_masks(rank):
    import ml_dtypes
    s = np.arange(128)[:, None]
    t = np.arange(512)[None, :]
    mk = np.zeros((128, 8, 512), np.float32)
    for j in range(4):
        tri = np.where((j * 128 + s) > t, NEG, 0.0)
        if rank == 0:
            mk[:, j, :] = tri
            mk[:, 4 + j, :] = NEG
        else:
            mk[:, j, :] = 0.0
            mk[:, 4 + j, :] = tri
    return mk.astype(ml_dtypes.bfloat16)


class Buf:
    __slots__ = ("name", "lw", "rd", "dsem", "dcnt")

    def __init__(self, name):
        self.name = name
        self.lw = None
        self.rd = {}
        self.dsem = None
        self.dcnt = 0


class Eng:
    def __init__(self, name, sem):
        self.name = name
        self.sem = sem
        self.tick = 0
        self.ops = []
        self.seen = {}


class Sched:
    def __init__(self):
        self.nsem = 0
        self.eng = {}
        for n in ("pe", "act", "dve", "pool", "sp"):
            self.eng[n] = Eng(n, self.newsem())
        self.bufs = []

    def newsem(self):
        self.nsem += 1
        return self.nsem - 1

    def buf(self, name):
        b = Buf(name)
        self.bufs.append(b)
        return b

    def _wait(self, eng, k, v):
        if eng.seen.get(k, 0) < v:
            eng.ops.append(("wait", k, v))
            eng.seen[k] = v

    def _deps(self, eng, reads, writes):
        w = {}

        def add(k, v):
            if v > w.get(k, 0):
                w[k] = v

        for b in reads:
            if b.lw:
                add(*b.lw)
        for b in writes:
            if b.lw:
                add(*b.lw)
            for k, v in b.rd.items():
                if k != eng.sem:
                    add(k, v)
        if eng.name == "pe":
            w.pop(eng.sem, None)
        for k, v in w.items():
            self._wait(eng, k, v)

    def op(self, en, fn, reads=(), writes=(), inc=True):
        eng = self.eng[en]
        self._deps(eng, reads, writes)
        if inc:
            eng.tick += 1
            eng.ops.append(("op", fn, eng.sem, 1))
            tick = eng.tick
        else:
            eng.ops.append(("op", fn, None, 0))
            tick = eng.tick + 1
        for b in reads:
            b.rd[eng.sem] = tick
        for b in writes:
            b.lw = (eng.sem, tick)
            b.rd = {}

    def dma(self, qn, pairs, reads=(), writes=()):
        q = self.eng[qn]
        self._deps(q, reads, writes)
        prim = writes[0] if writes else reads[0]
        if prim.dsem is None:
            prim.dsem = self.newsem()
        for (o, i) in pairs:
            prim.dcnt += 16
            q.ops.append(("op", (lambda e, o=o, i=i: e.dma_start(out=o, in_=i)), prim.dsem, 16))
        for b in reads:
            b.rd[prim.dsem] = prim.dcnt
        for b in writes:
            b.lw = (prim.dsem, prim.dcnt)
            b.rd = {}

    def collective(self, src_ap, dst_ap, bsrc, bdst):
        q = self.eng["pool"]
        self._deps(q, [bsrc], [bdst])
        if bdst.dsem is None:
            bdst.dsem = self.newsem()
        bdst.dcnt += 1
        q.ops.append(("op", (lambda e: e.collective_compute("AllGather", ALU.bypass, replica_groups=GROUPS,
                                                             ins=[src_ap], outs=[dst_ap])), bdst.dsem, 1))
        bsrc.rd[bdst.dsem] = bdst.dcnt
        bdst.lw = (bdst.dsem, bdst.dcnt)
        bdst.rd = {}

    def barrier(self):
        for e in self.eng.values():
            for e2 in self.eng.values():
                if e2.tick > 0:
                    self._wait(e, e2.sem, e2.tick)
            for b in self.bufs:
                if b.dsem is not None and b.dcnt > 0:
                    self._wait(e, b.dsem, b.dcnt)


def build_program(stop=None, debug=False):
    nc = bass.Bass("TRN2", target_bir_lowering=False)
    sc = Sched()
    es = contextlib.ExitStack()

    xin = nc.dram_tensor("xin", [KC, 128, SL], F32, kind="ExternalInput").ap()
    mkd = nc.dram_tensor("mk", [128, 8, T], BF16, kind="ExternalInput").ap()
    wf = nc.dram_tensor("wf", [128, WTOT], F32, kind="ExternalInput").ap()
    ppd = nc.dram_tensor("pp", [128, PPW], F32, kind="ExternalInput").ap()
    csd = nc.dram_tensor("cs", [128, CSW], F32, kind="ExternalInput").ap()
    yout = nc.dram_tensor("y", [KC, 128, SL], F32, kind="ExternalOutput").ap()
    wb = nc.dram_tensor("wb", [128, WTOT], BF16).ap()
    xs = nc.dram_tensor("xs", [KC, 128, SL], F32).ap()
    qs = nc.dram_tensor("qs", [H, 64, SL], BF16).ap()
    exs_t = [nc.dram_tensor(f"exs{i}", [RX, T], BF16) for i in range(NT)]
    exd_t = [nc.dram_tensor(f"exd{i}", [2 * RX, T], BF16) for i in range(NT)]
    exs = [t_.ap() for t_ in exs_t]
    exd = [t_.ap() for t_ in exd_t]
    lsrc_t = nc.dram_tensor("lsrc", [NT * 8, T], F32)
    ldst_t = nc.dram_tensor("ldst", [2 * NT * 8, T], F32)
    lsrc, ldst = lsrc_t.ap(), ldst_t.ap()

    def ex_k(ap, h):
        return ap[h * 64:(h + 1) * 64, :]

    def ex_v(ap):
        return ap[512:1032, :].rearrange("r c -> (r c)").rearrange("(h p k d) -> h p k d", h=H, p=128, k=4)

    def ex_u(ap, c):
        return ap[1032 + c * 128:1032 + (c + 1) * 128, :]

    def ex_up(ap, c):
        return ap[1288 + c * 128:1288 + (c + 1) * 128, :]

    def exd_rank(j, q):
        return exd[j][q * RX:(q + 1) * RX, :]
    gdrow = nc.dram_tensor("gdrow", [8, S + 32], F32).ap()
    gdtok = nc.dram_tensor("gdtok", [S + 32, 8], F32).ap()
    xqd = nc.dram_tensor("xqd", [3, H, T], BF16).ap()
    rdd = nc.dram_tensor("rdd", [2, T], F32).ap()

    def xview(ap, t):
        return ap.rearrange("c p t -> p c t")[:, :, t * T:(t + 1) * T]

    def sb(name, shape, dt):
        return es.enter_context(nc.sbuf_tensor(name, shape, dt))

    pp = sb("pp_sb", [128, PPW], F32)
    cs = sb("cs_sb", [128, CSW], F32)
    trib = sb("trib", [128, 128], BF16)
    identb = sb("identb", [128, 128], BF16)
    onesb = sb("onesb", [128, 128], BF16)
    onesf = sb("onesf", [128, 128], F32)
    ones8 = sb("ones8", [8, T], F32)
    epsc = sb("epsc", [128, 1], F32)
    zt = sb("zt", [128, 64], F32)
    ztb = sb("ztb", [128, 64], BF16)
    nfb = sb("nfb", [8, 2], F32)
    convdiag = sb("convdiag", [128, 2, 31, 128], BF16)
    poolbd = sb("poolbd", [128, 2, 128], BF16)
    gkall = sb("gkall", [128, S // 128, 8], F32)
    NX = 2
    xring = [sb(f"xr{i}", [128, KC, T], F32) for i in range(NX)]
    NW = 4
    wring = [sb(f"wr{i}", [128, FC * 128], BF16) for i in range(NW)]
    ARENA16 = 63 * 1024
    arena = sb("arena", [128, ARENA16], BF16)
    cv = {"off": 0}

    def carve(shape, dt):
        n = int(np.prod(shape[1:]))
        n16 = n * (2 if dt == F32 else 1)
        o = cv["off"]
        cv["off"] = o + (n16 + 15) // 16 * 16
        assert cv["off"] <= ARENA16, (cv["off"], ARENA16)
        v = arena[0:shape[0], o:o + n16]
        if dt == F32:
            v = v.bitcast(F32)
        if len(shape) == 3:
            v = v.rearrange("p (a b) -> p a b", a=shape[1])
        elif len(shape) == 4:
            v = v.rearrange("p (a b c) -> p a b c", a=shape[1], b=shape[2])
        return v

    cv["off"] = 0
    hT = carve([128, KC, T], BF16)
    G = carve([128, FC, T], BF16)
    sq = carve([128, KC, T], BF16)
    rr = carve([128, T], F32)
    sgs = [carve([128, T], BF16) for i in range(2)]
    NSTG = 4
    stg = [carve([128, T], BF16) for i in range(NSTG)]
    vst = carve([128, H, 4, 65], BF16)
    sgm = carve([128, T], F32)
    ex8 = carve([8, T], F32)
    lt8 = carve([8, T], F32)
    gts = [carve([8, T + 1], F32) for i in range(2)]
    cin = [carve([128, CVT], F32) for i in range(2)]
    cout = [carve([128, CVT], BF16) for i in range(2)]
    cv["off"] = 0
    NKP = 3
    kps = [carve([128, 2048], BF16) for i in range(NKP)]
    vps = [carve([128, 16, 128], BF16) for i in range(NKP)]
    qaugs = [carve([128, H, T], BF16) for i in range(2)]
    NP = 3
    pts = [carve([128, T], BF16) for i in range(NP)]
    grow = carve([8, T + 1], F32)
    growB = carve([8, T + 1], F32)
    grefbc = carve([128, 8], F32)
    grefB = carve([128, 8], F32)
    mk = carve([128, 8, T], BF16)
    haloA = carve([128, 2, 32], BF16)
    haloB = carve([128, 2, 32], BF16)
    phaloA = carve([128, 2, 16], BF16)
    phaloB = carve([128, 2, 16], BF16)
    biasTs = [carve([128, S // 128, 8], F32) for i in range(2)]
    bowns = [carve([128, 4, 8], F32) for i in range(2)]
    _o = cv["off"]
    r1ts = [carve([128, T], F32) for i in range(2)]
    cv["off"] = _o
    xq = carve([8, T], F32)
    xr1 = carve([8, T], F32)
    x32 = carve([8, T], F32)
    xhi = carve([8, T], BF16)
    xmid = carve([8, T], BF16)
    xlo = carve([8, T], BF16)
    mixPs = [carve([128, 2, T], BF16) for i in range(2)]
    mixCs = [carve([128, 2, T], BF16) for i in range(2)]
    ybTs = [carve([64, H, T], BF16) for i in range(2)]
    uh = carve([128, 2, T + 32], BF16)
    uph = carve([128, 2, T + 16], BF16)
    sA = carve([128, 2, T + 16], F32)
    sB = carve([128, 2, T + 16], F32)
    pooledb = carve([128, 2, T], BF16)
    t0tmp = carve([128, 2, 16], F32)
    yconv = carve([128, 2, T], F32)
    ysq = sA[:, :, 0:T]
    mstat = carve([128, T], F32)
    msq = carve([128, T], F32)
    rstd = carve([128, T], F32)
    tmpn = carve([128, T], F32)
    tmpn2 = carve([128, T], F32)
    bcss = [carve([64, T], F32) for i in range(2)]

    psum = [es.enter_context(nc.psum_tensor(f"ps{i}", [128, T], F32)) for i in range(8)]

    B = sc.buf
    b_pp, b_cs, b_trib, b_identb, b_ones, b_zt, b_nfb = B("pp"), B("cs"), B("trib"), B("identb"), B("ones"), B("zt"), B("nfb")
    b_convdiag, b_poolbd, b_gkall = B("convdiag"), B("poolbd"), B("gkall")
    b_x = [[B(f"x{i}.{c}") for c in range(KC)] for i in range(NX)]
    b_w = [B(f"w{i}") for i in range(NW)]
    b_hT = [B(f"hT{c}") for c in range(KC)]
    b_G = [B(f"G{f}") for f in range(FC)]
    b_sq = [B(f"sq{c}") for c in range(KC)]
    b_rr = B("rr")
    b_sg = [B(f"sg{i}") for i in range(2)]
    b_stg = [B(f"stg{i}") for i in range(NSTG)]
    b_vst, b_sgm, b_ex8, b_lt8 = B("vst"), B("sgm"), B("ex8"), B("lt8")
    b_gt = [B("gt0"), B("gt1")]
    b_kp = [B(f"kp{i}") for i in range(NKP)]
    b_vp = [B(f"vp{i}") for i in range(NKP)]
    b_qaugs, b_grow, b_grefbc, b_biasTs = [B("qaug0"), B("qaug1")], B("grow"), B("grefbc"), [B("biasT0"), B("biasT1")]
    b_pt = [B(f"pt{i}") for i in range(NP)]
    b_xq, b_xr1, b_x32, b_xhi, b_xmid, b_xlo = B("xq"), B("xr1"), B("x32"), B("xhi"), B("xmid"), B("xlo")
    b_mixPs = [[B(f"mixP{i}{c}") for c in range(2)] for i in range(2)]
    b_mixCs = [[B(f"mixC{i}{c}") for c in range(2)] for i in range(2)]
    b_ybTs = [[B(f"ybT{i}{h}") for h in range(H)] for i in range(2)]
    b_uh, b_uph, b_sA, b_sB, b_pooledb, b_t0tmp = B("uh"), B("uph"), B("sA"), B("sB"), B("pooledb"), B("t0tmp")
    b_yconv, b_ysq_unused, b_mstat, b_msq, b_rstd, b_tmpn, b_r1t, b_bcs = (B("yconv"), B("ysq"), B("mstat"), B("msq"),
                                                                    B("rstd"), B("tmpn"), [B("r1t0"), B("r1t1")], [B("bcs0"), B("bcs1")])
    b_tmpn2 = B("tmpn2")
    b_growB, b_grefB, b_halo = B("growB"), B("grefB"), B("halo")
    b_bowns = [B("bown0"), B("bown1")]
    b_rdd = B("rdd")
    b_ysq = b_sA
    b_cin = [B("cin0"), B("cin1")]
    b_cout = [B("cout0"), B("cout1")]
    b_ps = [B(f"ps{i}") for i in range(8)]
    NBLK = (WTOT + CVT - 1) // CVT
    EARLY_COLS = WOFF["0.pbd"][0]
    NEARLY = (EARLY_COLS + CVT - 1) // CVT
    PER_TILE = (NBLK - NEARLY + NT - 1) // NT
    def blk_group(bi):
        return 0 if bi < NEARLY else 1 + (bi - NEARLY) // PER_TILE
    b_wbg = [B(f"wb{g}") for g in range(2 + (NBLK - NEARLY) // PER_TILE)]
    def wb_bufs(o, w):
        return sorted({blk_group(bi) for bi in range(o // CVT, (o + w - 1) // CVT + 1)})
    b_xs = [B(f"xs{t}") for t in range(NT)]
    b_qs = [B(f"qs{t}") for t in range(NT)]
    b_exs = [B(f"exs{t}") for t in range(NT)]
    b_exd = [B(f"exd{t}") for t in range(NT)]
    b_lsrc, b_ldst, b_gd, b_mk = B("lsrc"), B("ldst"), B("gd"), B("mk")
    b_pad = B("pad")
    b_y = B("y")
    b_xqd = B("xqd")

    class Rot:
        def __init__(self, idx):
            self.idx = idx
            self.i = 0
            self.held = set()

        def get(self, hold=False):
            while True:
                k = self.idx[self.i % len(self.idx)]
                self.i += 1
                if k not in self.held:
                    break
            if hold:
                self.held.add(k)
            return psum[k], b_ps[k]

        def release(self, bps):
            self.held.discard(b_ps.index(bps))

    psr = Rot([0, 1, 2, 3, 4, 5])
    pso = Rot([6, 7])
    rot = {"stg": 0, "alt": 0, "kp": 0, "pt": 0, "sg": 0}

    def alt_eng():
        rot["alt"] += 1
        return "act" if rot["alt"] % 2 else "dve"

    def rsqrt_eps(out, in_, reads, writes):
        sc.op("act", lambda e: e.activation(out=out, in_=in_, func=AF.Sqrt, bias=epsc[:, 0:1], scale=1.0), reads=list(reads) + [b_ones], writes=writes)
        sc.op("dve", lambda e: e.reciprocal(out=out, in_=out), reads=writes, writes=writes)

    def copy_op(en, out, in_, reads, writes):
        if en == "act":
            sc.op("act", lambda e: e.activation(out=out, in_=in_, func=AF.Copy), reads, writes)
        else:
            sc.op(en, lambda e: e.tensor_copy(out=out, in_=in_), reads, writes)

    wstate = {"issued": 0, "used": 0, "order": []}

    def wplan(name):
        wstate["order"].append(name)

    def wissue_upto(n):
        while wstate["issued"] < min(n, len(wstate["order"])):
            i = wstate["issued"]
            o, w = WOFF[wstate["order"][i]]
            slot = i % NW
            sc.dma("sp", [(wring[slot][:, 0:w], wb[:, o:o + w])], reads=[b_wbg[g] for g in wb_bufs(o, w)], writes=[b_w[slot]])
            wstate["issued"] += 1

    def wget(name):
        i = wstate["used"]
        assert wstate["order"][i] == name, (wstate["order"][i], name)
        wissue_upto(i + NW - 1)
        wstate["used"] += 1
        slot = i % NW
        return wring[slot], b_w[slot]

    def plan_ffn(l, ffn):
        for f in range(FC):
            wplan(f"{l}.{ffn}.gu{f}")
        for c in range(KC):
            wplan(f"{l}.{ffn}.dn{c}")

    def plan_mixin(l):
        for j in range(4):
            wplan(f"{l}.q{j}")
        for j in range(4):
            wplan(f"{l}.k{j}")
        for j in range(2):
            wplan(f"{l}.v{j}")
        wplan(f"{l}.zf")
        for c in range(2):
            wplan(f"{l}.pool{c}")
        for c in range(2):
            wplan(f"{l}.a{c}")
            wplan(f"{l}.g{c}")

    for l in range(NL):
        wplan(f"{l}.pbd")
        for t in range(NT):
            if l > 0:
                plan_ffn(l - 1, 2)
            plan_ffn(l, 1)
            plan_mixin(l)
        for t in range(NT):
            for c in range(KC):
                wplan(f"{l}.out{c}")
    for t in range(NT):
        plan_ffn(NL - 1, 2)

    sc.dma("sp", [(pp[:, :], ppd)], writes=[b_pp])
    sc.dma("sp", [(cs[:, :], csd)], writes=[b_cs])
    sc.op("dve", lambda e: e.memset(zt[:, :], 0.0), writes=[b_zt])
    sc.op("dve", lambda e: e.memset(ztb[:, :], 0.0), writes=[b_zt])
    sc.op("dve", lambda e: e.memset(onesb[:, :], 1.0), writes=[b_ones])
    sc.op("dve", lambda e: e.memset(onesf[:, :], 1.0), writes=[b_ones])
    sc.op("dve", lambda e: e.memset(ones8[:, :], 1.0), writes=[b_ones])
    sc.op("dve", lambda e: e.memset(epsc[:, :], EPS), writes=[b_ones])
    sc.op("dve", lambda e: e.tensor_copy(out=trib[:, :], in_=cs[:, CS_TRI:CS_TRI + 128]), reads=[b_cs], writes=[b_trib])
    sc.op("dve", lambda e: e.tensor_copy(out=identb[:, :], in_=cs[:, CS_ID:CS_ID + 128]), reads=[b_cs], writes=[b_identb])
    for l in range(NL):
        sc.op("dve", lambda e, l=l: e.tensor_scalar(out=nfb[:, l:l + 1], in0=pp[0:8, l * PPL + PP_FB:l * PPL + PP_FB + 1],
                                                     scalar1=-1.0, scalar2=None, op0=ALU.mult), reads=[b_pp], writes=[b_nfb])
    sc.dma("pool", [(gdrow[:, 0:1], zt[0:8, 0:1]), (gdtok[0:1, :], zt[0:1, 0:8])], reads=[b_zt], writes=[b_pad])

    cvt_engs = ["dve", "act"]

    cvt_state = {"next": 0, "loaded": 0}

    def cvt_load(i):
        c0 = i * CVT
        w = min(CVT, WTOT - c0)
        sc.dma("sp", [(cin[i % 2][:, 0:w], wf[:, c0:c0 + w])], writes=[b_cin[i % 2]])

    def cvt_conv(i):
        c0 = i * CVT
        w = min(CVT, WTOT - c0)
        k = i % 2
        copy_op(cvt_engs[i % 2], cout[k][:, 0:w], cin[k][:, 0:w], [b_cin[k]], [b_cout[k]])
        sc.dma("pool", [(wb[:, c0:c0 + w], cout[k][:, 0:w])], reads=[b_cout[k]], writes=[b_wbg[blk_group(i)]])

    def convert_some(n, limit=None):
        lim = NBLK if limit is None else limit
        for _ in range(n):
            i = cvt_state["next"]
            if i >= lim:
                return
            while cvt_state["loaded"] < min(i + 2, NBLK):
                cvt_load(cvt_state["loaded"])
                cvt_state["loaded"] += 1
            cvt_conv(i)
            cvt_state["next"] += 1

    convert_some(NEARLY, NEARLY)

    def rmsnorm_to_hT(xt, bx, gcol):
        ps, bps = psr.get()
        for c in range(KC):
            sc.op("act", lambda e, c=c: e.activation(out=sq[:, c, :], in_=xt[:, c, :], func=AF.Square, scale=1.0 / 32.0),
                  reads=[bx[c]], writes=[b_sq[c]])
        for c in range(KC):
            sc.op("pe", lambda e, c=c: e.matmul(ps[:, :], onesb[:, :], sq[:, c, :], start=(c == 0), stop=(c == KC - 1)),
                  reads=[b_sq[c], b_ones], writes=[bps], inc=(c == KC - 1))
        rsqrt_eps(rr[:, :], ps[:, :], [bps], [b_rr])
        for c in range(KC):
            sc.op("dve", lambda e, c=c: e.scalar_tensor_tensor(out=hT[:, c, :], in0=xt[:, c, :], scalar=pp[:, gcol + c:gcol + c + 1],
                                                                in1=rr[:, :], op0=ALU.mult, op1=ALU.mult),
                  reads=[bx[c], b_rr, b_pp], writes=[b_hT[c]])

    dbgflag = {"first": debug}

    def ffn(l, which, xt, bx, mid=None):
        gcol = l * PPL + (PP_F1 if which == 1 else PP_F2)
        rmsnorm_to_hT(xt, bx, gcol)
        first = False
        for f in range(FC):
            wt, bw = wget(f"{l}.{which}.gu{f}")
            wv = wt[:, 0:2048].rearrange("p (g k n) -> p g k n", g=2, k=KC)
            psg, bpg = psr.get()
            psu, bpu = psr.get()
            for k in range(KC):
                sc.op("pe", lambda e, k=k, psg=psg, wv=wv: e.matmul(psg[:, :], wv[:, 0, k, :], hT[:, k, :], start=(k == 0), stop=(k == KC - 1)),
                      reads=[bw, b_hT[k]], writes=[bpg], inc=(k == KC - 1))
            for k in range(KC):
                sc.op("pe", lambda e, k=k, psu=psu, wv=wv: e.matmul(psu[:, :], wv[:, 1, k, :], hT[:, k, :], start=(k == 0), stop=(k == KC - 1)),
                      reads=[bw, b_hT[k]], writes=[bpu], inc=(k == KC - 1))
            si = rot["sg"] % 2
            rot["sg"] += 1
            sc.op("act", lambda e, psg=psg, si=si: e.activation(out=sgs[si][:, :], in_=psg[:, :], func=AF.Silu),
                  reads=[bpg], writes=[b_sg[si]])
            sc.op("dve", lambda e, f=f, psu=psu, si=si: e.tensor_tensor(out=G[:, f, :], in0=psu[:, :], in1=sgs[si][:, :], op=ALU.mult),
                  reads=[bpu, b_sg[si]], writes=[b_G[f]])
            if l == 0 and which == 1 and f % 3 == 2:
                convert_some(1)
        if mid is not None:
            mid()
        for c in range(KC):
            wt, bw = wget(f"{l}.{which}.dn{c}")
            wv = wt[:, 0:FC * 128].rearrange("p (k n) -> p k n", k=FC)
            psd, bpd = psr.get()
            for f in range(FC):
                sc.op("pe", lambda e, f=f, psd=psd, wv=wv: e.matmul(psd[:, :], wv[:, f, :], G[:, f, :], start=(f == 0), stop=(f == FC - 1)),
                      reads=[bw, b_G[f]], writes=[bpd], inc=(f == FC - 1))
            sc.op("dve", lambda e, c=c, psd=psd: e.scalar_tensor_tensor(out=xt[:, c, :], in0=psd[:, :], scalar=0.5, in1=xt[:, c, :],
                                                                        op0=ALU.mult, op1=ALU.add),
                  reads=[bpd, bx[c]], writes=[bx[c]])

    def next_stg():
        i = rot["stg"] % NSTG
        rot["stg"] += 1
        return stg[i], b_stg[i]

    def mixin(l, t, xt, bx):
        base = l * PPL
        rmsnorm_to_hT(xt, bx, base + PP_MX)
        t0, t1 = t * T, (t + 1) * T
        for which in ("q", "k"):
            for j in range(4):
                wt, bw = wget(f"{l}.{which}{j}")
                wv = wt[:, 0:1024].rearrange("p (k n) -> p k n", k=KC)
                ps, bps = psr.get()
                for k in range(KC):
                    sc.op("pe", lambda e, k=k, ps=ps, wv=wv: e.matmul(ps[:, :], wv[:, k, :], hT[:, k, :], start=(k == 0), stop=(k == KC - 1)),
                          reads=[bw, b_hT[k]], writes=[bps], inc=(k == KC - 1))
                st, bst = next_stg()
                if which == "q":
                    sc.op("act", lambda e, ps=ps, st=st: e.activation(out=st[:, :], in_=ps[:, :], func=AF.Copy, scale=0.125),
                          reads=[bps], writes=[bst])
                    sc.dma("pool", [(qs[2 * j + hh, :, t0:t1], st[hh * 64:(hh + 1) * 64, :]) for hh in range(2)], reads=[bst], writes=[b_qs[t]])
                else:
                    sc.op("dve", lambda e, ps=ps, st=st: e.tensor_copy(out=st[:, :], in_=ps[:, :]), reads=[bps], writes=[bst])
                    sc.dma("pool", [(ex_k(exs[t], 2 * j + hh), st[hh * 64:(hh + 1) * 64, :]) for hh in range(2)], reads=[bst], writes=[b_exs[t]])
        for j in range(2):
            wt, bw = wget(f"{l}.v{j}")
            wv = wt[:, 0:2048].rearrange("p (k n) -> p k n", k=KC)
            for s in range(4):
                ps, bps = psr.get()
                for k in range(KC):
                    sc.op("pe", lambda e, k=k, ps=ps, wv=wv, s=s: e.matmul(ps[:, 0:256], hT[:, k, s * 128:(s + 1) * 128], wv[:, k, :],
                                                                           start=(k == 0), stop=(k == KC - 1)),
                          reads=[bw, b_hT[k]], writes=[bps], inc=(k == KC - 1))
                en = alt_eng()
                copy_op(en, vst[:, 4 * j:4 * j + 4, s, 0:64], ps[:, 0:256].rearrange("p (h d) -> p h d", h=4), [bps], [b_vst])
        sc.dma("pool", [(ex_v(exs[t]).rearrange("h p k d -> p h k d"), vst[:, :, :, :])], reads=[b_vst], writes=[b_exs[t]])
        wt, bw = wget(f"{l}.zf")
        wv = wt[:, 0:64].rearrange("p (k n) -> p k n", k=KC)
        ps, bps = psr.get()
        for k in range(KC):
            sc.op("pe", lambda e, k=k, ps=ps, wv=wv: e.matmul(ps[0:8, :], wv[:, k, :], hT[:, k, :], start=(k == 0), stop=(k == KC - 1)),
                  reads=[bw, b_hT[k]], writes=[bps], inc=(k == KC - 1))
        sc.op("act", lambda e, ps=ps: e.activation(out=ex8[:, :], in_=ps[0:8, :], func=AF.Exp, bias=nfb[:, l:l + 1], scale=-1.0),
              reads=[bps, b_nfb], writes=[b_ex8])
        sc.op("act", lambda e: e.activation(out=lt8[:, :], in_=ex8[:, :], func=AF.Ln, bias=1.0, scale=1.0), reads=[b_ex8], writes=[b_lt8])
        sc.dma("pool", [(lsrc[t * 8:(t + 1) * 8, :], lt8[:, :])], reads=[b_lt8], writes=[b_lsrc])
        for c in range(2):
            wt, bw = wget(f"{l}.pool{c}")
            wv = wt[:, 0:1024].rearrange("p (k n) -> p k n", k=KC)
            ps, bps = psr.get()
            for k in range(KC):
                sc.op("pe", lambda e, k=k, ps=ps, wv=wv: e.matmul(ps[:, :], wv[:, k, :], hT[:, k, :], start=(k == 0), stop=(k == KC - 1)),
                      reads=[bw, b_hT[k]], writes=[bps], inc=(k == KC - 1))
            st, bst = next_stg()
            copy_op("act", st[:, :], ps[:, :], [bps], [bst])
            sc.dma("pool", [(ex_up(exs[t], c), st[:, :])], reads=[bst], writes=[b_exs[t]])
        for c in range(2):
            wta, bwa = wget(f"{l}.a{c}")
            wva = wta[:, 0:1024].rearrange("p (k n) -> p k n", k=KC)
            psa, bpa = psr.get()
            for k in range(KC):
                sc.op("pe", lambda e, k=k, psa=psa, wva=wva: e.matmul(psa[:, :], wva[:, k, :], hT[:, k, :], start=(k == 0), stop=(k == KC - 1)),
                      reads=[bwa, b_hT[k]], writes=[bpa], inc=(k == KC - 1))
            wtg, bwg = wget(f"{l}.g{c}")
            wvg = wtg[:, 0:1024].rearrange("p (k n) -> p k n", k=KC)
            psg, bpg = psr.get()
            for k in range(KC):
                sc.op("pe", lambda e, k=k, psg=psg, wvg=wvg: e.matmul(psg[:, :], wvg[:, k, :], hT[:, k, :], start=(k == 0), stop=(k == KC - 1)),
                      reads=[bwg, b_hT[k]], writes=[bpg], inc=(k == KC - 1))
            sc.op("act", lambda e, psg=psg: e.activation(out=sgm[:, :], in_=psg[:, :], func=AF.Sigmoid), reads=[bpg], writes=[b_sgm])
            st, bst = next_stg()
            sc.op("dve", lambda e, psa=psa, st=st: e.tensor_tensor(out=st[:, :], in0=psa[:, :], in1=sgm[:, :], op=ALU.mult),
                  reads=[bpa, b_sgm], writes=[bst])
            sc.dma("pool", [(ex_u(exs[t], c), st[:, :])], reads=[bst], writes=[b_exs[t]])

    def load_x(src_ap, src_bufs, t):
        slot = load_x.n % NX
        load_x.n += 1
        sc.dma("sp", [(xring[slot][:, :, :], xview(src_ap, t))], reads=src_bufs, writes=b_x[slot])
        return xring[slot], b_x[slot]
    load_x.n = 0

    def loop_a(l):
        src = xin if l == 0 else xs
        sc.op("dve", lambda e: e.memset(vst[:, :, :, :], 1.0), writes=[b_vst])
        layer_setup_b(l)
        nxt = load_x(src, [] if l == 0 else [b_xs[0]], 0)
        for t in range(NT):
            xt, bx = nxt
            if t + 1 < NT:
                nxt = load_x(src, [] if l == 0 else [b_xs[t + 1]], t + 1)
            if l > 0:
                ffn(l - 1, 2, xt, bx)
            ffn(l, 1, xt, bx)
            sc.dma("pool", [(xview(xs, t), xt[:, :, :])], reads=bx, writes=[b_xs[t]])
            mixin(l, t, xt, bx)
            if l == 0:
                convert_some(PER_TILE - 7)
            sc.collective(exs_t[t].ap().opt(), exd_t[t].ap().opt(), b_exs[t], b_exd[t])
        sc.collective(lsrc_t.ap().opt(), ldst_t.ap().opt(), b_lsrc, b_ldst)
        lall = ldst.rearrange("(q j h) t -> h j q t", q=2, j=NT, h=8)
        for hh in range(2):
            sc.dma("sp", [(cin[hh][0:8, :].rearrange("h (j q t) -> h j q t", j=NT // 2, q=2)[:, :, q_, :],
                           lall[:, (NT // 2) * hh:(NT // 2) * (hh + 1), q_, :]) for q_ in range(2)],
                   reads=[b_ldst], writes=[b_cin[hh]])
        for g in range(NB):
            hh, bi = g // 8, g % 8
            seg = cin[hh][0:8, bi * T:(bi + 1) * T]
            if g == 0:
                sc.op("dve", lambda e, seg=seg: e.tensor_tensor_scan(out=seg, data0=ones8[:, :], data1=seg, initial=0.0, op0=ALU.mult, op1=ALU.add),
                      reads=[b_cin[hh], b_ones], writes=[b_cin[hh]])
            else:
                ph, pb = (g - 1) // 8, (g - 1) % 8
                carry = cin[ph][0:8, (pb + 1) * T - 1:(pb + 1) * T]
                sc.op("dve", lambda e, seg=seg, carry=carry: e.tensor_tensor_scan(out=seg, data0=ones8[:, :], data1=seg, initial=carry,
                                                                                op0=ALU.mult, op1=ALU.add),
                      reads=[b_cin[hh], b_cin[ph], b_ones], writes=[b_cin[hh]])
        sc.dma("pool", [(gdrow[:, 1 + hh * 4096:1 + (hh + 1) * 4096], cin[hh][0:8, :]) for hh in range(2)], reads=b_cin, writes=[b_gd])
        ps, bps = psr.get()
        for kt in range(S // 128):
            hh, off = kt // 32, (kt % 32) * 128
            sc.op("pe", lambda e, kt=kt, hh=hh, off=off: e.transpose(ps[:, kt * 8:(kt + 1) * 8], cin[hh][0:8, off:off + 128], cs[0:8, CS_ID:CS_ID + 8]),
                  reads=[b_cin[hh], b_cs], writes=[bps], inc=(kt == S // 128 - 1))
        sc.op("dve", lambda e: e.tensor_copy(out=gkall[:, :, :], in_=ps[:, :].rearrange("p (k h) -> p k h", h=8)), reads=[bps], writes=[b_gkall])
        sc.dma("pool", [(gdtok[1:1 + S, :].rearrange("(k p) h -> p k h", p=128), gkall[:, :, :])], reads=[b_gkall], writes=[b_gd])

    def layer_setup_b(l):
        base = l * PPL
        wt, bw = wget(f"{l}.pbd")
        sc.op("dve", lambda e: e.tensor_copy(out=poolbd[:, :, :], in_=wt[:, 0:256].rearrange("p (c n) -> p c n", c=2)),
              reads=[bw], writes=[b_poolbd])
        for c in range(2):
            for j in range(31):
                col = base + PP_CW + c * 31 + j
                sc.op("dve", lambda e, c=c, j=j, col=col: e.tensor_scalar(out=convdiag[:, c, j, :], in0=cs[:, CS_ID:CS_ID + 128],
                                                                            scalar1=pp[:, col:col + 1], scalar2=None, op0=ALU.mult),
                      reads=[b_cs, b_pp], writes=[b_convdiag])

    def prologue_stages(l, i):
        base = l * PPL
        par = i % 2
        t0, t1 = i * T, (i + 1) * T
        nkt = 8 * (i + 1)
        qaug, b_qaug, biasT, b_biasT = qaugs[par], b_qaugs[par], biasTs[par], b_biasTs[par]
        mixP, mixC, b_mixP, b_mixC = mixPs[par], mixCs[par], b_mixPs[par], b_mixCs[par]
        W = T + 16
        st = {}
        m0c = pp[:, PP_M0:PP_M0 + 1]
        m1c = pp[:, PP_M1:PP_M1 + 1]

        def s0():
            gA, gB = 2 * i, 2 * i + 1
            sc.dma("sp", [(qaug[0:64, :, :], qs.rearrange("h d t -> d h t")[:, :, t0:t1])], reads=[b_qs[i]], writes=[b_qaug])
            sc.dma("sp", [(grow[:, :], gdrow[:, gA * T:gA * T + T + 1])], reads=[b_gd, b_pad], writes=[b_grow])
            sc.dma("sp", [(growB[:, :], gdrow[:, gB * T:gB * T + T + 1])], reads=[b_gd, b_pad], writes=[b_growB])
            sc.dma("sp", [(grefbc[:, :], gdtok[gA * T:gA * T + 1, :].partition_broadcast(128).squeeze(1))], reads=[b_gd, b_pad], writes=[b_grefbc])
            sc.dma("sp", [(grefB[:, :], gdtok[gB * T:gB * T + 1, :].partition_broadcast(128).squeeze(1))], reads=[b_gd, b_pad], writes=[b_grefB])
            sc.op("dve", lambda e: e.tensor_scalar(out=growB[:, :], in0=growB[:, :], scalar1=m1c[0:8, :], scalar2=None, op0=ALU.mult),
                  reads=[b_growB, b_pp], writes=[b_growB])
            sc.op("dve", lambda e: e.scalar_tensor_tensor(out=grow[:, :], in0=grow[:, :], scalar=m0c[0:8, :], in1=growB[:, :], op0=ALU.mult, op1=ALU.add),
                  reads=[b_grow, b_growB, b_pp], writes=[b_grow])
            sc.op("dve", lambda e: e.tensor_scalar(out=grefB[:, :], in0=grefB[:, :], scalar1=m1c, scalar2=None, op0=ALU.mult),
                  reads=[b_grefB, b_pp], writes=[b_grefB])
            sc.op("dve", lambda e: e.scalar_tensor_tensor(out=grefbc[:, :], in0=grefbc[:, :], scalar=m0c, in1=grefB[:, :], op0=ALU.mult, op1=ALU.add),
                  reads=[b_grefbc, b_grefB, b_pp], writes=[b_grefbc])
            sc.dma("sp", [(uph[:, c, 16:T + 16], ex_up(exs[i], c)) for c in range(2)] + [(uh[:, c, 32:T + 32], ex_u(exs[i], c)) for c in range(2)],
                   reads=[b_exs[i]], writes=[b_uph, b_uh])
            if i > 0:
                prv = exd_rank(i - 1, 1)
                sc.dma("sp", [(haloA[:, c, :], ex_u(prv, c)[:, T - 32:T]) for c in range(2)] + [(phaloA[:, c, :], ex_up(prv, c)[:, T - 16:T]) for c in range(2)],
                       reads=[b_exd[i - 1]], writes=[b_halo])
            else:
                sc.op("dve", lambda e: e.memset(haloA[:, :, :], 0.0), writes=[b_halo])
                sc.op("dve", lambda e: e.memset(phaloA[:, :, :], 0.0), writes=[b_halo])
            cur0 = exd_rank(i, 0)
            sc.dma("sp", [(haloB[:, c, :], ex_u(cur0, c)[:, T - 32:T]) for c in range(2)] + [(phaloB[:, c, :], ex_up(cur0, c)[:, T - 16:T]) for c in range(2)],
                   reads=[b_exd[i]], writes=[b_halo])
            for (hA, hB, dst, w_) in ((haloA, haloB, uh, 32), (phaloA, phaloB, uph, 16)):
                sc.op("dve", lambda e, hB=hB: e.tensor_scalar(out=hB[:, :, :], in0=hB[:, :, :], scalar1=m1c, scalar2=None, op0=ALU.mult),
                      reads=[b_halo, b_pp], writes=[b_halo])
                sc.op("dve", lambda e, hA=hA, hB=hB, dst=dst, w_=w_: e.scalar_tensor_tensor(out=dst[:, :, 0:w_], in0=hA[:, :, :], scalar=m0c, in1=hB[:, :, :],
                                                                                          op0=ALU.mult, op1=ALU.add),
                      reads=[b_halo, b_pp], writes=[b_uph, b_uh])
            sc.op("dve", lambda e: e.tensor_scalar(out=xq[:, :], in0=grow[:, 1:T + 1], scalar1=grow[:, 0:1], scalar2=-1.0, op0=ALU.subtract, op1=ALU.mult),
                  reads=[b_grow], writes=[b_xq])
            sc.op("dve", lambda e: e.tensor_copy(out=xhi[:, :], in_=xq[:, :]), reads=[b_xq], writes=[b_xhi])
            sc.op("dve", lambda e: e.tensor_copy(out=x32[:, :], in_=xhi[:, :]), reads=[b_xhi], writes=[b_x32])
            sc.op("dve", lambda e: e.tensor_tensor(out=xr1[:, :], in0=xq[:, :], in1=x32[:, :], op=ALU.subtract), reads=[b_xq, b_x32], writes=[b_xr1])
            sc.op("dve", lambda e: e.tensor_copy(out=xmid[:, :], in_=xr1[:, :]), reads=[b_xr1], writes=[b_xmid])
            sc.op("dve", lambda e: e.tensor_copy(out=x32[:, :], in_=xmid[:, :]), reads=[b_xmid], writes=[b_x32])
            sc.op("dve", lambda e: e.tensor_tensor(out=xr1[:, :], in0=xr1[:, :], in1=x32[:, :], op=ALU.subtract), reads=[b_xr1, b_x32], writes=[b_xr1])
            sc.op("dve", lambda e: e.tensor_copy(out=xlo[:, :], in_=xr1[:, :]), reads=[b_xr1], writes=[b_xlo])
            sc.dma("pool", [(xqd[0, :, :], xhi[:, :]), (xqd[1, :, :], xmid[:, :]), (xqd[2, :, :], xlo[:, :])],
                   reads=[b_xhi, b_xmid, b_xlo], writes=[b_xqd])
            sc.dma("pool", [(qaug[64:67, :, :], xqd[:, :, :])], reads=[b_xqd], writes=[b_qaug])
            sc.op("dve", lambda e: e.tensor_tensor(out=biasT[:, 0:nkt, :], in0=gkall[:, 0:nkt, :],
                                                   in1=grefbc[:, :].unsqueeze(1).to_broadcast([128, nkt, 8]), op=ALU.subtract),
                  reads=[b_gkall, b_grefbc], writes=[b_biasT])
            bown, b_bown = bowns[par], b_bowns[par]
            sc.op("dve", lambda e: e.tensor_scalar(out=bown[:, :, :], in0=biasT[:, 8 * i + 4:8 * i + 8, :], scalar1=m1c, scalar2=None, op0=ALU.mult),
                  reads=[b_biasT, b_pp], writes=[b_bown])
            sc.op("dve", lambda e: e.scalar_tensor_tensor(out=bown[:, :, :], in0=biasT[:, 8 * i:8 * i + 4, :], scalar=m0c, in1=bown[:, :, :],
                                                          op0=ALU.mult, op1=ALU.add), reads=[b_biasT, b_bown, b_pp], writes=[b_bown])
            sc.op("dve", lambda e: e.tensor_scalar(out=biasT[:, 8 * i:8 * i + 4, :], in0=biasT[:, 8 * i:8 * i + 4, :],
                                                   scalar1=pp[:, PP_M0N:PP_M0N + 1], scalar2=None, op0=ALU.add),
                  reads=[b_biasT, b_bown, b_pp], writes=[b_biasT])
            sc.op("dve", lambda e: e.tensor_tensor(out=sA[:, :, 1:W], in0=uph[:, :, 1:W], in1=uph[:, :, 0:W - 1], op=ALU.add),
                  reads=[b_uph], writes=[b_sA])
            sc.op("dve", lambda e: e.tensor_tensor(out=sB[:, :, 3:W], in0=sA[:, :, 3:W], in1=sA[:, :, 1:W - 2], op=ALU.add),
                  reads=[b_sA], writes=[b_sB])
            sc.op("dve", lambda e: e.tensor_tensor(out=sA[:, 1, 7:W], in0=sB[:, 1, 7:W], in1=sB[:, 1, 3:W - 4], op=ALU.add),
                  reads=[b_sB], writes=[b_sA])
            sc.op("dve", lambda e: e.tensor_tensor(out=sB[64:128, 1, 15:W], in0=sA[64:128, 1, 15:W], in1=sA[64:128, 1, 7:W - 8], op=ALU.add),
                  reads=[b_sA], writes=[b_sB])
            srcs = [(sA, 0, 0), (sB, 0, 1), (sA, 1, 0), (sB, 1, 1)]
            for (stile, c, half) in srcs:
                p0, p1 = half * 64, (half + 1) * 64
                sc.op("dve", lambda e, stile=stile, c=c, p0=p0, p1=p1: e.scalar_tensor_tensor(
                    out=pooledb[p0:p1, c, :], in0=stile[p0:p1, c, 16:W], scalar=cs[p0:p1, CS_INVW + c:CS_INVW + c + 1],
                    in1=uph[p0:p1, c, 16:W], op0=ALU.mult, op1=ALU.subtract), reads=[b_sA, b_sB, b_uph, b_cs], writes=[b_pooledb])
            if i == 0:
                for (stile, c, half) in srcs:
                    p0, p1 = half * 64, (half + 1) * 64
                    sc.op("dve", lambda e, stile=stile, c=c, p0=p0, p1=p1: e.tensor_tensor(
                        out=t0tmp[p0:p1, c, :], in0=stile[p0:p1, c, 16:32], in1=cs[p0:p1, CS_T0 + c * 16:CS_T0 + (c + 1) * 16], op=ALU.mult),
                        reads=[b_sA, b_sB, b_cs], writes=[b_t0tmp])
                    sc.op("dve", lambda e, c=c, p0=p0, p1=p1: e.tensor_tensor(
                        out=pooledb[p0:p1, c, 0:16], in0=t0tmp[p0:p1, c, :], in1=uph[p0:p1, c, 16:32], op=ALU.subtract),
                        reads=[b_t0tmp, b_uph], writes=[b_pooledb])

        def s1():
            for c in range(2):
                ps, bps = psr.get()
                sc.op("pe", lambda e, c=c, ps=ps: e.matmul(ps[:, :], poolbd[:, c, :], pooledb[:, c, :], start=True, stop=True),
                      reads=[b_poolbd, b_pooledb], writes=[bps])
                col = base + PP_PS + c
                sc.op("act", lambda e, c=c, ps=ps, col=col: e.activation(out=mixP[:, c, :], in_=ps[:, :], func=AF.Identity, scale=pp[:, col:col + 1]),
                      reads=[bps, b_pp], writes=[b_mixP[c]])
            st["pc"] = []
            for c in range(2):
                ps, bps = psr.get(hold=True)
                for j in range(31):
                    sc.op("pe", lambda e, c=c, j=j, ps=ps: e.matmul(ps[:, :], convdiag[:, c, j, :], uh[:, c, 2 + j:2 + j + T], start=(j == 0), stop=(j == 30)),
                          reads=[b_convdiag, b_uh], writes=[bps], inc=(j == 30))
                st["pc"].append((ps, bps))

        def s2():
            for c in range(2):
                ps, bps = st["pc"][c]
                col = base + PP_CB + c
                sc.op("act", lambda e, c=c, ps=ps, col=col: e.activation(out=yconv[:, c, :], in_=ps[:, :], func=AF.Identity, bias=pp[:, col:col + 1], scale=1.0),
                      reads=[bps, b_pp], writes=[b_yconv])
                psr.release(bps)
            sc.op("dve", lambda e: e.tensor_tensor(out=ysq[:, :, :], in0=yconv[:, :, :], in1=yconv[:, :, :], op=ALU.mult), reads=[b_yconv], writes=[b_ysq])

        def s3():
            ps1, bp1 = psr.get(hold=True)
            ps2, bp2 = psr.get(hold=True)
            for c in range(2):
                sc.op("pe", lambda e, c=c: e.matmul(ps1[:, :], onesf[:, :], yconv[:, c, :], start=(c == 0), stop=(c == 1)),
                      reads=[b_ones, b_yconv], writes=[bp1], inc=(c == 1))
            for c in range(2):
                sc.op("pe", lambda e, c=c: e.matmul(ps2[:, :], onesf[:, :], ysq[:, c, :], start=(c == 0), stop=(c == 1)),
                      reads=[b_ones, b_ysq], writes=[bp2], inc=(c == 1))
            st["ln"] = (ps1, bp1, ps2, bp2)

        def s4():
            ps1, bp1, ps2, bp2 = st["ln"]
            sc.op("dve", lambda e: e.tensor_scalar(out=mstat[:, :], in0=ps1[:, :], scalar1=1.0 / 256.0, scalar2=None, op0=ALU.mult), reads=[bp1], writes=[b_mstat])
            sc.op("dve", lambda e: e.tensor_tensor(out=msq[:, :], in0=mstat[:, :], in1=mstat[:, :], op=ALU.mult), reads=[b_mstat], writes=[b_msq])
            sc.op("dve", lambda e: e.scalar_tensor_tensor(out=rstd[:, :], in0=ps2[:, :], scalar=1.0 / 256.0, in1=msq[:, :], op0=ALU.mult, op1=ALU.subtract),
                  reads=[bp2, b_msq], writes=[b_rstd])
            psr.release(bp1)
            psr.release(bp2)
            for c, (tt, bt) in enumerate(((tmpn, b_tmpn), (tmpn2, b_tmpn2))):
                sc.op("dve", lambda e, c=c, tt=tt: e.tensor_tensor(out=tt[:, :], in0=yconv[:, c, :], in1=mstat[:, :], op=ALU.subtract),
                      reads=[b_yconv, b_mstat], writes=[bt])

        def s5():
            rsqrt_eps(rstd[:, :], rstd[:, :], [b_rstd], [b_rstd])
            for c, (tt, bt) in enumerate(((tmpn, b_tmpn), (tmpn2, b_tmpn2))):
                sc.op("dve", lambda e, tt=tt: e.tensor_tensor(out=tt[:, :], in0=tt[:, :], in1=rstd[:, :], op=ALU.mult), reads=[bt, b_rstd], writes=[bt])
            for c, (tt, bt) in enumerate(((tmpn, b_tmpn), (tmpn2, b_tmpn2))):
                cg, cb = base + PP_LG + c, base + PP_LB + c
                sc.op("act", lambda e, c=c, cg=cg, cb=cb, tt=tt: e.activation(out=mixC[:, c, :], in_=tt[:, :], func=AF.Silu, bias=pp[:, cb:cb + 1], scale=pp[:, cg:cg + 1]),
                      reads=[bt, b_pp], writes=[b_mixC[c]])

        return [s0, s1, s2, s3, s4, s5]

    LOOKAHEAD = 2

    def attention_layer(l, slot_hooks):
        work_all = []
        for i in range(NT):
            nkg = 8 * i + 4
            npc = (nkg + 15) // 16
            for h in range(H):
                for pc in range(npc):
                    work_all.append((i, h, pc))
                work_all.append((i, h, "own"))
        loaded = {"n": 0}

        def issue_loads(upto):
            while loaded["n"] < min(upto, len(work_all)):
                n = loaded["n"]
                i, h, pc = work_all[n]
                slot = n % NKP
                if pc == "own":
                    sc.dma("sp", [(kps[slot][0:64, 0:T], ex_k(exs[i], h))], reads=[b_exs[i]], writes=[b_kp[slot]])
                    sc.dma("sp", [(vps[slot][:, 0:4, 0:65], ex_v(exs[i])[h])], reads=[b_exs[i]], writes=[b_vp[slot]])
                    loaded["n"] += 1
                    continue
                kpairs, vpairs, rds = [], [], []
                for jj in range(2):
                    j = 2 * pc + jj
                    if j > i:
                        continue
                    rds.append(b_exd[j])
                    if j < i:
                        both = exd[j].rearrange("(q r) c -> r q c", q=2)
                        kpairs.append((kps[slot][0:64, jj * 1024:(jj + 1) * 1024].rearrange("p (q c) -> p q c", q=2), both[h * 64:(h + 1) * 64, :, :]))
                        qs_ = (0, 1)
                    else:
                        kpairs.append((kps[slot][0:64, jj * 1024:jj * 1024 + T], ex_k(exd_rank(j, 0), h)))
                        qs_ = (0,)
                    for q_ in qs_:
                        vsrc = ex_v(exd_rank(j, q_))[h]
                        vpairs.append((vps[slot][:, jj * 8 + q_ * 4:jj * 8 + q_ * 4 + 4, 0:65], vsrc))
                sc.dma("sp", kpairs, reads=rds, writes=[b_kp[slot]])
                sc.dma("sp", vpairs, reads=rds, writes=[b_vp[slot]])
                loaded["n"] += 1

        def qk(h, j, c0, tri, kp, bkp, vp, bvp, qaug, b_qaug, bias_ap, b_bias, first, last):
            pss, bpss = psr.get()
            if tri:
                sc.op("pe", lambda e: e.matmul(pss[:, c0:c0 + 128], kp[:, j * 128:(j + 1) * 128], qaug[:, h, c0:c0 + 128], start=True, stop=False),
                      reads=[bkp, b_qaug], writes=[bpss], inc=False)
                sc.op("pe", lambda e: e.matmul(pss[:, c0:c0 + 128], identb[:, :], trib[:, :], start=False, stop=True),
                      reads=[b_identb, b_trib], writes=[bpss], inc=(c0 + 128 >= T))
                if c0 + 128 < T:
                    sc.op("pe", lambda e: e.matmul(pss[:, c0 + 128:T], kp[:, j * 128:(j + 1) * 128], qaug[:, h, c0 + 128:T], start=True, stop=True),
                          reads=[bkp, b_qaug], writes=[bpss])
            else:
                sc.op("pe", lambda e: e.matmul(pss[:, :], kp[:, j * 128:(j + 1) * 128], qaug[:, h, :], start=True, stop=True),
                      reads=[bkp, b_qaug], writes=[bpss])
            return (j, c0, pss, bpss, vp, bvp, bias_ap, b_bias, first, last)

        def exp_pv(item, pso_t, bpo):
            j, c0, pss, bpss, vp, bvp, bias_ap, b_bias, first, last = item
            pi = rot["pt"] % NP
            rot["pt"] += 1
            pt, bpt = pts[pi], b_pt[pi]
            sc.op("act", lambda e: e.activation(out=pt[:, c0:T], in_=pss[:, c0:T], func=AF.Exp, bias=bias_ap, scale=1.0),
                  reads=[bpss, b_bias], writes=[bpt])
            sc.op("pe", lambda e: e.matmul(pso_t[:, c0:T], vp[:, j, :], pt[:, c0:T], start=first, stop=last, skip_group_check=True),
                  reads=[bvp, bpt], writes=[bpo], inc=last)

        def fin_a(h, pso_t, bpo):
            k = h % 2
            sc.op("dve", lambda e: e.reciprocal(out=r1ts[k][64:65, :], in_=pso_t[64:65, :]), reads=[bpo], writes=[b_r1t[k]])
            sc.dma("pool", [(rdd[k:k + 1, :], r1ts[k][64:65, :])], reads=[b_r1t[k]], writes=[b_rdd])
            sc.dma("pool", [(bcss[k][:, :], rdd[k:k + 1, :].partition_broadcast(64).squeeze(1))], reads=[b_rdd], writes=[b_bcs[k]])

        def fin_b(i, h, pso_t, bpo):
            k = h % 2
            yb, byb = ybTs[i % 2], b_ybTs[i % 2]
            sc.op("dve", lambda e: e.tensor_tensor(out=yb[:, h, :], in0=pso_t[0:64, :], in1=bcss[k][:, :], op=ALU.mult),
                  reads=[bpo, b_bcs[k]], writes=[byb[h]])

        pending_fin = None
        n = 0
        for i in range(NT):
            par = i % 2
            qaug, b_qaug, biasT, b_biasT = qaugs[par], b_qaugs[par], biasTs[par], b_biasTs[par]
            bown, b_bown = bowns[par], b_bowns[par]
            nkg = 8 * i + 4
            npc = (nkg + 15) // 16
            hooks = slot_hooks(i)
            for h in range(H):
                cur_o = pso.get()
                pend = []
                cnt = 0
                for pc in list(range(npc)) + ["own"]:
                    issue_loads(n + 2)
                    slot = n % NKP
                    n += 1
                    if pc == "own":
                        tiles = [(j, j * 128, True, bown[:, j, h:h + 1], b_bown, False, j == 3) for j in range(4)]
                    else:
                        nk = min(16, nkg - pc * 16)
                        tiles = [(j, 0, False, biasT[:, pc * 16 + j, h:h + 1], b_biasT, (pc == 0 and j == 0), False) for j in range(nk)]
                    for (j, c0, tri, bias_ap, b_bias, first, last) in tiles:
                        pend.append(qk(h, j, c0, tri, kps[slot], b_kp[slot], vps[slot], b_vp[slot], qaug, b_qaug, bias_ap, b_bias, first, last))
                        if len(pend) > LOOKAHEAD:
                            exp_pv(pend.pop(0), cur_o[0], cur_o[1])
                        cnt += 1
                        if cnt == 6 and pending_fin is not None:
                            fin_b(*pending_fin)
                            pending_fin = None
                while pend:
                    exp_pv(pend.pop(0), cur_o[0], cur_o[1])
                if pending_fin is not None:
                    fin_b(*pending_fin)
                fin_a(h, cur_o[0], cur_o[1])
                pending_fin = (i, h, cur_o[0], cur_o[1])
                if h in hooks:
                    hooks[h]()
            if "end" in hooks:
                hooks["end"]()
        fin_b(*pending_fin)

    def w_out(l, i, xt, bx):
        par = i % 2
        mixP, mixC, b_mixP, b_mixC = mixPs[par], mixCs[par], b_mixPs[par], b_mixCs[par]
        yb, byb = ybTs[par], b_ybTs[par]
        for c in range(KC):
            wt, bw = wget(f"{l}.out{c}")
            wv = wt[:, 0:1536].rearrange("p (k n) -> p k n", k=12)
            ps, bps = psr.get()
            ops = []
            for k in range(2):
                ops.append((wv[:, k, :], mixP[:, k, :], b_mixP[k]))
            for h in range(H):
                ops.append((wv[0:64, 2 + h, :], yb[:, h, :], byb[h]))
            for k in range(2):
                ops.append((wv[:, 10 + k, :], mixC[:, k, :], b_mixC[k]))
            for n, (lh, rh, br) in enumerate(ops):
                sc.op("pe", lambda e, lh=lh, rh=rh, n=n, ps=ps: e.matmul(ps[:, :], lh, rh, start=(n == 0), stop=(n == 11)),
                      reads=[bw, br], writes=[bps], inc=(n == 11))
            sc.op("dve", lambda e, c=c, ps=ps: e.tensor_tensor(out=xt[:, c, :], in0=ps[:, :], in1=xt[:, c, :], op=ALU.add),
                  reads=[bps, bx[c]], writes=[bx[c]])

    def loop_b(l):
        for i in range(NKP):
            sc.op("dve", lambda e, i=i: e.memset(kps[i][64:128, :], 0.0), writes=[b_kp[i]])
            sc.op("dve", lambda e, i=i: e.memset(kps[i][64:67, :], 1.0), writes=[b_kp[i]])
            sc.op("dve", lambda e, i=i: e.memset(vps[i][:, :, :], 1.0), writes=[b_vp[i]])
        for i in range(2):
            sc.op("dve", lambda e, i=i: e.memset(qaugs[i][64:128, :, :], 0.0), writes=[b_qaugs[i]])
        sc.dma("sp", [(mk[:, :, :], mkd)], writes=[b_mk])
        xtiles = {}
        xtiles[0] = load_x(xs, [b_xs[0]], 0)
        for s in prologue_stages(l, 0):
            s()

        def finish_slot(i):
            xt, bx = xtiles.pop(i)
            w_out(l, i, xt, bx)
            sc.dma("pool", [(xview(xs, i), xt[:, :, :])], reads=bx, writes=[b_xs[i]])

        def slot_hooks(i):
            hooks = {}
            stages = prologue_stages(l, i + 1) if i + 1 < NT else None

            def h0():
                if i > 0:
                    finish_slot(i - 1)
                if i + 1 < NT:
                    stages[0]()
            hooks[0] = h0
            if stages is not None:
                def h2():
                    xtiles[i + 1] = load_x(xs, [b_xs[i + 1]], i + 1)
                hooks.update({1: stages[1], 2: h2, 3: stages[2], 4: stages[3], 6: stages[4], "end": stages[5]})
            return hooks

        attention_layer(l, slot_hooks)
        finish_slot(NT - 1)

    def final_norm(xt, bx):
        ps, bps = psr.get()
        for c in range(KC):
            sc.op("act", lambda e, c=c: e.activation(out=sq[:, c, :], in_=xt[:, c, :], func=AF.Square, scale=1.0 / 32.0),
                  reads=[bx[c]], writes=[b_sq[c]])
        for c in range(KC):
            sc.op("pe", lambda e, c=c: e.matmul(ps[:, :], onesb[:, :], sq[:, c, :], start=(c == 0), stop=(c == KC - 1)),
                  reads=[b_sq[c], b_ones], writes=[bps], inc=(c == KC - 1))
        rsqrt_eps(rr[:, :], ps[:, :], [bps], [b_rr])
        for c in range(KC):
            sc.op("dve", lambda e, c=c: e.scalar_tensor_tensor(out=xt[:, c, :], in0=xt[:, c, :], scalar=pp[:, PP_FIN + c:PP_FIN + c + 1],
                                                                in1=rr[:, :], op0=ALU.mult, op1=ALU.mult),
                  reads=[bx[c], b_rr, b_pp], writes=[bx[c]])

    def loop_c():
        l = NL - 1
        cur = {"nxt": load_x(xs, [b_xs[0]], 0)}
        for t in range(NT):
            xt, bx = cur["nxt"]

            def mid(t=t):
                if t + 1 < NT:
                    cur["nxt"] = load_x(xs, [b_xs[t + 1]], t + 1)
            ffn(l, 2, xt, bx, mid=mid)
            final_norm(xt, bx)
            sc.dma("pool", [(xview(yout, t), xt[:, :, :])], reads=bx, writes=[b_y])

    def dump():
        sc.barrier()
        b_dbg = B("dbg")
        pairs = [(yout[c], xs[c]) for c in range(KC)]
        sc.dma("pool", pairs, writes=[b_dbg])
        sc.barrier()

    stage = 0
    done = False
    for l in range(NL):
        loop_a(l)
        sc.barrier()
        stage += 1
        if stop == stage:
            dump(); done = True; break
        loop_b(l)
        sc.barrier()
        stage += 1
        if stop == stage:
            dump(); done = True; break
    if not done:
        loop_c()
        sc.barrier()

    with es:
        sems = [es.enter_context(nc.semaphore(f"s{i}")) for i in range(sc.nsem)]
        es.enter_context(nc.allow_non_contiguous_dma(reason="tiny per-row scalars"))
        block = es.enter_context(nc.Block())

        def emit(eng_name):
            def body(e):
                for rec in sc.eng[eng_name].ops:
                    if rec[0] == "wait":
                        e.wait_ge(sems[rec[1]], rec[2])
                    else:
                        ins = rec[1](e)
                        if rec[2] is not None:
                            ins.then_inc(sems[rec[2]], rec[3])
            return body

        block.sync(emit("sp"))
        block.tensor(emit("pe"))
        block.scalar(emit("act"))
        block.vector(emit("dve"))
        block.gpsimd(emit("pool"))
    return nc


_CACHE = {}


def kernel(**inputs):
    inp = {k: np.asarray(v) for k, v in inputs.items()}
    x = inp["x"].astype(np.float32, copy=False)
    if "nc" not in _CACHE:
        _CACHE["nc"] = build_program()
    nc = _CACHE["nc"]
    Wf = pack_weights(inp)
    pps = [pack_params(inp, r) for r in range(2)]
    css = [make_consts(r) for r in range(2)]
    mks = [make_masks(r) for r in range(2)]
    in_maps = []
    for c in range(NCORES):
        b, r = c // 2, c % 2
        xl = x[b].reshape(NB, T, D)[r::2].reshape(SL, D)
        xT = np.ascontiguousarray(xl.T).reshape(KC, 128, SL)
        in_maps.append({"xin": xT, "wf": Wf, "pp": pps[r], "cs": css[r], "mk": mks[r]})
    res = run_bass_kernel_spmd(nc, in_maps, core_ids=list(range(NCORES)))
    out = np.empty((NCORES // 2, S, D), np.float32)
    for c in range(NCORES):
        b, r = c // 2, c % 2
        yl = res.results[c]["y"].reshape(D, SL).T.reshape(NT, T, D)
        out[b].reshape(NB, T, D)[r::2] = yl
    return out
```

```python
import contextlib
import numpy as np
import concourse.bass as bass
import concourse.mybir as mybir
from concourse.bass_utils import run_bass_kernel_spmd

F32 = mybir.dt.float32
BF16 = mybir.dt.bfloat16
AF = mybir.ActivationFunctionType
ALU = mybir.AluOpType

NCORES = 8
S = 8192
SL = 4096
NB = 16
RX = 1544
GROUPS = [[0, 1], [2, 3], [4, 5], [6, 7]]
D = 1024
DFF = 2816
NL = 2
T = 512
NT = SL // T
KC = 8
FC = 22
H = 8
EPS = 1e-6
NEG = -30000.0
CVT = 4096
UPAD = 32


def wlayout():
    items = []
    for l in range(NL):
        for ffn in (1, 2):
            if ffn == 2:
                continue
            for f in range(FC):
                items.append((f"{l}.{ffn}.gu{f}", 2048))
            for c in range(KC):
                items.append((f"{l}.{ffn}.dn{c}", FC * 128))
        for c in range(2):
            items.append((f"{l}.pool{c}", 1024))
        for j in range(4):
            items.append((f"{l}.q{j}", 1024))
        for j in range(4):
            items.append((f"{l}.k{j}", 1024))
        for j in range(2):
            items.append((f"{l}.v{j}", 2048))
        items.append((f"{l}.zf", 64))
        for c in range(2):
            items.append((f"{l}.a{c}", 1024))
        for c in range(2):
            items.append((f"{l}.g{c}", 1024))
        items.append((f"{l}.pbd", 256))
        for c in range(KC):
            items.append((f"{l}.out{c}", 12 * 128))
        for f in range(FC):
            items.append((f"{l}.2.gu{f}", 2048))
        for c in range(KC):
            items.append((f"{l}.2.dn{c}", FC * 128))
    off = {}
    o = 0
    for n, w in items:
        off[n] = (o, w)
        o += w
    return items, off, o


WITEMS, WOFF, WTOT = wlayout()

PP_F1, PP_MX, PP_F2, PP_PS, PP_CB, PP_LG, PP_LB, PP_CW, PP_FB = 0, 8, 16, 24, 26, 28, 30, 32, 94
PPL = 96
PP_FIN = NL * PPL
PP_M0 = PP_FIN + 8
PP_M1 = PP_FIN + 9
PP_M0N = PP_FIN + 10
PPW = PP_FIN + 11
CS_ID, CS_TRI, CS_INVW, CS_T0 = 0, 128, 256, 258
CSW = 258 + 32


def _colchunk(W, c0, n):
    return np.ascontiguousarray(W[:, c0:c0 + n].reshape(KC, 128, n).transpose(1, 0, 2)).reshape(128, KC * n)


def pack_weights(inp):
    Wf = np.zeros((128, WTOT), np.float32)

    def put(name, a):
        o, w = WOFF[name]
        assert a.shape == (128, w), (name, a.shape, w)
        Wf[:, o:o + w] = a

    for l in range(NL):
        for ffn in (1, 2):
            wg = inp[f"ffn{ffn}_w_gate"][l]
            wu = inp[f"ffn{ffn}_w_up"][l]
            wd = inp[f"ffn{ffn}_w_down"][l]
            for f in range(FC):
                put(f"{l}.{ffn}.gu{f}", np.concatenate([_colchunk(wg, f * 128, 128), _colchunk(wu, f * 128, 128)], axis=1))
            for c in range(KC):
                a = wd[:, c * 128:(c + 1) * 128].reshape(FC, 128, 128).transpose(1, 0, 2).reshape(128, FC * 128)
                put(f"{l}.{ffn}.dn{c}", a)
        wi = inp["w_in"][l]
        for c in range(2):
            put(f"{l}.pool{c}", _colchunk(wi, c * 128, 128))
        for j in range(4):
            put(f"{l}.q{j}", _colchunk(wi, 256 + j * 128, 128))
            put(f"{l}.k{j}", _colchunk(wi, 768 + j * 128, 128))
        for j in range(2):
            put(f"{l}.v{j}", _colchunk(wi, 1280 + j * 256, 256))
        put(f"{l}.zf", _colchunk(wi, 1792, 8))
        for c in range(2):
            put(f"{l}.a{c}", _colchunk(wi, 1800 + c * 128, 128))
            put(f"{l}.g{c}", _colchunk(wi, 2056 + c * 128, 128))
        pw = inp["pool_w"][l]
        bd = np.zeros((128, 2, 128), np.float32)
        for c in range(2):
            bd[0:64, c, 0:64] = pw[2 * c]
            bd[64:128, c, 64:128] = pw[2 * c + 1]
        put(f"{l}.pbd", bd.reshape(128, 256))
        wo = inp["w_out"][l]
        for c in range(KC):
            a = np.zeros((128, 12, 128), np.float32)
            for k in range(2):
                a[:, k, :] = wo[k * 128:(k + 1) * 128, c * 128:(c + 1) * 128]
                a[:, 10 + k, :] = wo[768 + k * 128:768 + (k + 1) * 128, c * 128:(c + 1) * 128]
            for h in range(H):
                a[0:64, 2 + h, :] = wo[256 + h * 64:256 + (h + 1) * 64, c * 128:(c + 1) * 128]
            put(f"{l}.out{c}", a.reshape(128, 12 * 128))
    return Wf


def pack_params(inp, rank):
    pp = np.zeros((128, PPW), np.float32)
    pp[:, PP_M0] = 1.0 if rank == 0 else 0.0
    pp[:, PP_M1] = 1.0 if rank == 1 else 0.0
    pp[:, PP_M0N] = NEG if rank == 0 else 0.0

    def colv(v, n):
        return np.ascontiguousarray(np.asarray(v).reshape(n, 128).T)

    for l in range(NL):
        b = l * PPL
        pp[:, b + PP_F1:b + PP_F1 + 8] = colv(inp["ffn1_norm"][l], 8)
        pp[:, b + PP_MX:b + PP_MX + 8] = colv(inp["mix_norm"][l], 8)
        pp[:, b + PP_F2:b + PP_F2 + 8] = colv(inp["ffn2_norm"][l], 8)
        pp[:, b + PP_PS:b + PP_PS + 2] = colv(inp["pool_scale"][l], 2)
        pp[:, b + PP_CB:b + PP_CB + 2] = colv(inp["conv_b"][l], 2)
        pp[:, b + PP_LG:b + PP_LG + 2] = colv(inp["conv_ln_g"][l], 2)
        pp[:, b + PP_LB:b + PP_LB + 2] = colv(inp["conv_ln_b"][l], 2)
        cw = np.asarray(inp["conv_w"][l])
        pp[:, b + PP_CW:b + PP_CW + 62] = cw.T.reshape(2, 128, 31).transpose(1, 0, 2).reshape(128, 62)
        pp[0:8, b + PP_FB] = np.asarray(inp["forget_bias"][l])
    pp[:, PP_FIN:PP_FIN + 8] = colv(inp["final_norm"], 8)
    return pp


def make_consts(rank):
    cs = np.zeros((128, CSW), np.float32)
    cs[:, CS_ID:CS_ID + 128] = np.eye(128, dtype=np.float32)
    s = np.arange(128)[:, None]
    t = np.arange(128)[None, :]
    cs[:, CS_TRI:CS_TRI + 128] = np.where(s > t, NEG, 0.0)
    wins = (2, 4, 8, 16)
    for c in range(2):
        for half in range(2):
            w = wins[2 * c + half]
            cs[half * 64:(half + 1) * 64, CS_INVW + c] = 1.0 / w
            for tt in range(16):
                cs[half * 64:(half + 1) * 64, CS_T0 + c * 16 + tt] = 1.0 / (min(tt + 1, w) if rank == 0 else w)
    return cs


def make_masks(rank):
    import ml_dtypes
    s = np.arange(128)[:, None]
    t = np.arange(512)[None, :]
    mk = np.zeros((128, 8, 512), np.float32)
    for j in range(4):
        tri = np.where((j * 128 + s) > t, NEG, 0.0)
        if rank == 0:
            mk[:, j, :] = tri
            mk[:, 4 + j, :] = NEG
        else:
            mk[:, j, :] = 0.0
            mk[:, 4 + j, :] = tri
    return mk.astype(ml_dtypes.bfloat16)


class Buf:
    __slots__ = ("name", "lw", "rd", "dsem", "dcnt")

    def __init__(self, name):
        self.name = name
        self.lw = None
        self.rd = {}
        self.dsem = None
        self.dcnt = 0


class Eng:
    def __init__(self, name, sem):
        self.name = name
        self.sem = sem
        self.tick = 0
        self.ops = []
        self.seen = {}


class Sched:
    def __init__(self):
        self.nsem = 0
        self.eng = {}
        for n in ("pe", "act", "dve", "pool", "sp"):
            self.eng[n] = Eng(n, self.newsem())
        self.bufs = []

    def newsem(self):
        self.nsem += 1
        return self.nsem - 1

    def buf(self, name):
        b = Buf(name)
        self.bufs.append(b)
        return b

    def _wait(self, eng, k, v):
        if eng.seen.get(k, 0) < v:
            eng.ops.append(("wait", k, v))
            eng.seen[k] = v

    def _deps(self, eng, reads, writes):
        w = {}

        def add(k, v):
            if v > w.get(k, 0):
                w[k] = v

        for b in reads:
            if b.lw:
                add(*b.lw)
        for b in writes:
            if b.lw:
                add(*b.lw)
            for k, v in b.rd.items():
                if k != eng.sem:
                    add(k, v)
        if eng.name == "pe":
            w.pop(eng.sem, None)
        for k, v in w.items():
            self._wait(eng, k, v)

    def op(self, en, fn, reads=(), writes=(), inc=True):
        eng = self.eng[en]
        self._deps(eng, reads, writes)
        if inc:
            eng.tick += 1
            eng.ops.append(("op", fn, eng.sem, 1))
            tick = eng.tick
        else:
            eng.ops.append(("op", fn, None, 0))
            tick = eng.tick + 1
        for b in reads:
            b.rd[eng.sem] = tick
        for b in writes:
            b.lw = (eng.sem, tick)
            b.rd = {}

    def dma(self, qn, pairs, reads=(), writes=()):
        q = self.eng[qn]
        self._deps(q, reads, writes)
        prim = writes[0] if writes else reads[0]
        if prim.dsem is None:
            prim.dsem = self.newsem()
        for (o, i) in pairs:
            prim.dcnt += 16
            q.ops.append(("op", (lambda e, o=o, i=i: e.dma_start(out=o, in_=i)), prim.dsem, 16))
        for b in reads:
            b.rd[prim.dsem] = prim.dcnt
        for b in writes:
            b.lw = (prim.dsem, prim.dcnt)
            b.rd = {}

    def collective(self, src_ap, dst_ap, bsrc, bdst):
        q = self.eng["pool"]
        self._deps(q, [bsrc], [bdst])
        if bdst.dsem is None:
            bdst.dsem = self.newsem()
        bdst.dcnt += 1
        q.ops.append(("op", (lambda e: e.collective_compute("AllGather", ALU.bypass, replica_groups=GROUPS,
                                                             ins=[src_ap], outs=[dst_ap])), bdst.dsem, 1))
        bsrc.rd[bdst.dsem] = bdst.dcnt
        bdst.lw = (bdst.dsem, bdst.dcnt)
        bdst.rd = {}

    def barrier(self):
        for e in self.eng.values():
            for e2 in self.eng.values():
                if e2.tick > 0:
                    self._wait(e, e2.sem, e2.tick)
            for b in self.bufs:
                if b.dsem is not None and b.dcnt > 0:
                    self._wait(e, b.dsem, b.dcnt)


def build_program(stop=None, debug=False):
    nc = bass.Bass("TRN2", target_bir_lowering=False)
    sc = Sched()
    es = contextlib.ExitStack()

    xin = nc.dram_tensor("xin", [KC, 128, SL], F32, kind="ExternalInput").ap()
    mkd = nc.dram_tensor("mk", [128, 8, T], BF16, kind="ExternalInput").ap()
    wf = nc.dram_tensor("wf", [128, WTOT], F32, kind="ExternalInput").ap()
    ppd = nc.dram_tensor("pp", [128, PPW], F32, kind="ExternalInput").ap()
    csd = nc.dram_tensor("cs", [128, CSW], F32, kind="ExternalInput").ap()
    yout = nc.dram_tensor("y", [KC, 128, SL], F32, kind="ExternalOutput").ap()
    wb = nc.dram_tensor("wb", [128, WTOT], BF16).ap()
    xs = nc.dram_tensor("xs", [KC, 128, SL], F32).ap()
    qs = nc.dram_tensor("qs", [H, 64, SL], BF16).ap()
    exs_t = [nc.dram_tensor(f"exs{i}", [RX, T], BF16) for i in range(NT)]
    exd_t = [nc.dram_tensor(f"exd{i}", [2 * RX, T], BF16) for i in range(NT)]
    exs = [t_.ap() for t_ in exs_t]
    exd = [t_.ap() for t_ in exd_t]
    lsrc_t = nc.dram_tensor("lsrc", [NT * 8, T], F32)
    ldst_t = nc.dram_tensor("ldst", [2 * NT * 8, T], F32)
    lsrc, ldst = lsrc_t.ap(), ldst_t.ap()

    def ex_k(ap, h):
        return ap[h * 64:(h + 1) * 64, :]

    def ex_v(ap):
        return ap[512:1032, :].rearrange("r c -> (r c)").rearrange("(h p k d) -> h p k d", h=H, p=128, k=4)

    def ex_u(ap, c):
        return ap[1032 + c * 128:1032 + (c + 1) * 128, :]

    def ex_up(ap, c):
        return ap[1288 + c * 128:1288 + (c + 1) * 128, :]

    def exd_rank(j, q):
        return exd[j][q * RX:(q + 1) * RX, :]
    gdrow = nc.dram_tensor("gdrow", [8, S + 32], F32).ap()
    gdtok = nc.dram_tensor("gdtok", [S + 32, 8], F32).ap()
    xqd = nc.dram_tensor("xqd", [3, H, T], BF16).ap()
    rdd = nc.dram_tensor("rdd", [2, T], F32).ap()

    def xview(ap, t):
        return ap.rearrange("c p t -> p c t")[:, :, t * T:(t + 1) * T]

    def sb(name, shape, dt):
        return es.enter_context(nc.sbuf_tensor(name, shape, dt))

    pp = sb("pp_sb", [128, PPW], F32)
    cs = sb("cs_sb", [128, CSW], F32)
    trib = sb("trib", [128, 128], BF16)
    identb = sb("identb", [128, 128], BF16)
    onesb = sb("onesb", [128, 128], BF16)
    onesf = sb("onesf", [128, 128], F32)
    ones8 = sb("ones8", [8, T], F32)
    epsc = sb("epsc", [128, 1], F32)
    zt = sb("zt", [128, 64], F32)
    ztb = sb("ztb", [128, 64], BF16)
    nfb = sb("nfb", [8, 2], F32)
    convdiag = sb("convdiag", [128, 2, 31, 128], BF16)
    poolbd = sb("poolbd", [128, 2, 128], BF16)
    gkall = sb("gkall", [128, S // 128, 8], F32)
    NX = 2
    xring = [sb(f"xr{i}", [128, KC, T], F32) for i in range(NX)]
    NW = 4
    wring = [sb(f"wr{i}", [128, FC * 128], BF16) for i in range(NW)]
    ARENA16 = 63 * 1024
    arena = sb("arena", [128, ARENA16], BF16)
    cv = {"off": 0}

    def carve(shape, dt):
        n = int(np.prod(shape[1:]))
        n16 = n * (2 if dt == F32 else 1)
        o = cv["off"]
        cv["off"] = o + (n16 + 15) // 16 * 16
        assert cv["off"] <= ARENA16, (cv["off"], ARENA16)
        v = arena[0:shape[0], o:o + n16]
        if dt == F32:
            v = v.bitcast(F32)
        if len(shape) == 3:
            v = v.rearrange("p (a b) -> p a b", a=shape[1])
        elif len(shape) == 4:
            v = v.rearrange("p (a b c) -> p a b c", a=shape[1], b=shape[2])
        return v

    cv["off"] = 0
    hT = carve([128, KC, T], BF16)
    G = carve([128, FC, T], BF16)
    sq = carve([128, KC, T], BF16)
    rr = carve([128, T], F32)
    sgs = [carve([128, T], BF16) for i in range(2)]
    NSTG = 4
    stg = [carve([128, T], BF16) for i in range(NSTG)]
    vst = carve([128, H, 4, 65], BF16)
    sgm = carve([128, T], F32)
    ex8 = carve([8, T], F32)
    lt8 = carve([8, T], F32)
    gts = [carve([8, T + 1], F32) for i in range(2)]
    cin = [carve([128, CVT], F32) for i in range(2)]
    cout = [carve([128, CVT], BF16) for i in range(2)]
    cv["off"] = 0
    NKP = 3
    kps = [carve([128, 2048], BF16) for i in range(NKP)]
    vps = [carve([128, 16, 128], BF16) for i in range(NKP)]
    qaugs = [carve([128, H, T], BF16) for i in range(2)]
    NP = 3
    pts = [carve([128, T], BF16) for i in range(NP)]
    grow = carve([8, T + 1], F32)
    growB = carve([8, T + 1], F32)
    grefbc = carve([128, 8], F32)
    grefB = carve([128, 8], F32)
    mk = carve([128, 8, T], BF16)
    haloA = carve([128, 2, 32], BF16)
    haloB = carve([128, 2, 32], BF16)
    phaloA = carve([128, 2, 16], BF16)
    phaloB = carve([128, 2, 16], BF16)
    biasTs = [carve([128, S // 128, 8], F32) for i in range(2)]
    bowns = [carve([128, 4, 8], F32) for i in range(2)]
    _o = cv["off"]
    r1ts = [carve([128, T], F32) for i in range(2)]
    cv["off"] = _o
    xq = carve([8, T], F32)
    xr1 = carve([8, T], F32)
    x32 = carve([8, T], F32)
    xhi = carve([8, T], BF16)
    xmid = carve([8, T], BF16)
    xlo = carve([8, T], BF16)
    mixPs = [carve([128, 2, T], BF16) for i in range(2)]
    mixCs = [carve([128, 2, T], BF16) for i in range(2)]
    ybTs = [carve([64, H, T], BF16) for i in range(2)]
    uh = carve([128, 2, T + 32], BF16)
    uph = carve([128, 2, T + 16], BF16)
    sA = carve([128, 2, T + 16], F32)
    sB = carve([128, 2, T + 16], F32)
    pooledb = carve([128, 2, T], BF16)
    t0tmp = carve([128, 2, 16], F32)
    yconv = carve([128, 2, T], F32)
    ysq = sA[:, :, 0:T]
    mstat = carve([128, T], F32)
    msq = carve([128, T], F32)
    rstd = carve([128, T], F32)
    tmpn = carve([128, T], F32)
    tmpn2 = carve([128, T], F32)
    bcss = [carve([64, T], F32) for i in range(2)]

    psum = [es.enter_context(nc.psum_tensor(f"ps{i}", [128, T], F32)) for i in range(8)]

    B = sc.buf
    b_pp, b_cs, b_trib, b_identb, b_ones, b_zt, b_nfb = B("pp"), B("cs"), B("trib"), B("identb"), B("ones"), B("zt"), B("nfb")
    b_convdiag, b_poolbd, b_gkall = B("convdiag"), B("poolbd"), B("gkall")
    b_x = [[B(f"x{i}.{c}") for c in range(KC)] for i in range(NX)]
    b_w = [B(f"w{i}") for i in range(NW)]
    b_hT = [B(f"hT{c}") for c in range(KC)]
    b_G = [B(f"G{f}") for f in range(FC)]
    b_sq = [B(f"sq{c}") for c in range(KC)]
    b_rr = B("rr")
    b_sg = [B(f"sg{i}") for i in range(2)]
    b_stg = [B(f"stg{i}") for i in range(NSTG)]
    b_vst, b_sgm, b_ex8, b_lt8 = B("vst"), B("sgm"), B("ex8"), B("lt8")
    b_gt = [B("gt0"), B("gt1")]
    b_kp = [B(f"kp{i}") for i in range(NKP)]
    b_vp = [B(f"vp{i}") for i in range(NKP)]
    b_qaugs, b_grow, b_grefbc, b_biasTs = [B("qaug0"), B("qaug1")], B("grow"), B("grefbc"), [B("biasT0"), B("biasT1")]
    b_pt = [B(f"pt{i}") for i in range(NP)]
    b_xq, b_xr1, b_x32, b_xhi, b_xmid, b_xlo = B("xq"), B("xr1"), B("x32"), B("xhi"), B("xmid"), B("xlo")
    b_mixPs = [[B(f"mixP{i}{c}") for c in range(2)] for i in range(2)]
    b_mixCs = [[B(f"mixC{i}{c}") for c in range(2)] for i in range(2)]
    b_ybTs = [[B(f"ybT{i}{h}") for h in range(H)] for i in range(2)]
    b_uh, b_uph, b_sA, b_sB, b_pooledb, b_t0tmp = B("uh"), B("uph"), B("sA"), B("sB"), B("pooledb"), B("t0tmp")
    b_yconv, b_ysq_unused, b_mstat, b_msq, b_rstd, b_tmpn, b_r1t, b_bcs = (B("yconv"), B("ysq"), B("mstat"), B("msq"),
                                                                    B("rstd"), B("tmpn"), [B("r1t0"), B("r1t1")], [B("bcs0"), B("bcs1")])
    b_tmpn2 = B("tmpn2")
    b_growB, b_grefB, b_halo = B("growB"), B("grefB"), B("halo")
    b_bowns = [B("bown0"), B("bown1")]
    b_rdd = B("rdd")
    b_ysq = b_sA
    b_cin = [B("cin0"), B("cin1")]
    b_cout = [B("cout0"), B("cout1")]
    b_ps = [B(f"ps{i}") for i in range(8)]
    NBLK = (WTOT + CVT - 1) // CVT
    EARLY_COLS = WOFF["0.pbd"][0]
    NEARLY = (EARLY_COLS + CVT - 1) // CVT
    PER_TILE = (NBLK - NEARLY + NT - 1) // NT
    def blk_group(bi):
        return 0 if bi < NEARLY else 1 + (bi - NEARLY) // PER_TILE
    b_wbg = [B(f"wb{g}") for g in range(2 + (NBLK - NEARLY) // PER_TILE)]
    def wb_bufs(o, w):
        return sorted({blk_group(bi) for bi in range(o // CVT, (o + w - 1) // CVT + 1)})
    b_xs = [B(f"xs{t}") for t in range(NT)]
    b_qs = [B(f"qs{t}") for t in range(NT)]
    b_exs = [B(f"exs{t}") for t in range(NT)]
    b_exd = [B(f"exd{t}") for t in range(NT)]
    b_lsrc, b_ldst, b_gd, b_mk = B("lsrc"), B("ldst"), B("gd"), B("mk")
    b_pad = B("pad")
    b_y = B("y")
    b_xqd = B("xqd")

    class Rot:
        def __init__(self, idx):
            self.idx = idx
            self.i = 0
            self.held = set()

        def get(self, hold=False):
            while True:
                k = self.idx[self.i % len(self.idx)]
                self.i += 1
                if k not in self.held:
                    break
            if hold:
                self.held.add(k)
            return psum[k], b_ps[k]

        def release(self, bps):
            self.held.discard(b_ps.index(bps))

    psr = Rot([0, 1, 2, 3, 4, 5])
    pso = Rot([6, 7])
    rot = {"stg": 0, "alt": 0, "kp": 0, "pt": 0, "sg": 0}

    def alt_eng():
        rot["alt"] += 1
        return "act" if rot["alt"] % 2 else "dve"

    def rsqrt_eps(out, in_, reads, writes):
        sc.op("act", lambda e: e.activation(out=out, in_=in_, func=AF.Sqrt, bias=epsc[:, 0:1], scale=1.0), reads=list(reads) + [b_ones], writes=writes)
        sc.op("dve", lambda e: e.reciprocal(out=out, in_=out), reads=writes, writes=writes)

    def copy_op(en, out, in_, reads, writes):
        if en == "act":
            sc.op("act", lambda e: e.activation(out=out, in_=in_, func=AF.Copy), reads, writes)
        else:
            sc.op(en, lambda e: e.tensor_copy(out=out, in_=in_), reads, writes)

    wstate = {"issued": 0, "used": 0, "order": []}

    def wplan(name):
        wstate["order"].append(name)

    def wissue_upto(n):
        while wstate["issued"] < min(n, len(wstate["order"])):
            i = wstate["issued"]
            o, w = WOFF[wstate["order"][i]]
            slot = i % NW
            sc.dma("sp", [(wring[slot][:, 0:w], wb[:, o:o + w])], reads=[b_wbg[g] for g in wb_bufs(o, w)], writes=[b_w[slot]])
            wstate["issued"] += 1

    def wget(name):
        i = wstate["used"]
        assert wstate["order"][i] == name, (wstate["order"][i], name)
        wissue_upto(i + NW - 1)
        wstate["used"] += 1
        slot = i % NW
        return wring[slot], b_w[slot]

    def plan_ffn(l, ffn):
        for f in range(FC):
            wplan(f"{l}.{ffn}.gu{f}")
        for c in range(KC):
            wplan(f"{l}.{ffn}.dn{c}")

    def plan_mixin(l):
        for j in range(4):
            wplan(f"{l}.q{j}")
        for j in range(4):
            wplan(f"{l}.k{j}")
        for j in range(2):
            wplan(f"{l}.v{j}")
        wplan(f"{l}.zf")
        for c in range(2):
            wplan(f"{l}.pool{c}")
        for c in range(2):
            wplan(f"{l}.a{c}")
            wplan(f"{l}.g{c}")

    for l in range(NL):
        wplan(f"{l}.pbd")
        for t in range(NT):
            if l > 0:
                plan_ffn(l - 1, 2)
            plan_ffn(l, 1)
            plan_mixin(l)
        for t in range(NT):
            for c in range(KC):
                wplan(f"{l}.out{c}")
    for t in range(NT):
        plan_ffn(NL - 1, 2)

    sc.dma("sp", [(pp[:, :], ppd)], writes=[b_pp])
    sc.dma("sp", [(cs[:, :], csd)], writes=[b_cs])
    sc.op("dve", lambda e: e.memset(zt[:, :], 0.0), writes=[b_zt])
    sc.op("dve", lambda e: e.memset(ztb[:, :], 0.0), writes=[b_zt])
    sc.op("dve", lambda e: e.memset(onesb[:, :], 1.0), writes=[b_ones])
    sc.op("dve", lambda e: e.memset(onesf[:, :], 1.0), writes=[b_ones])
    sc.op("dve", lambda e: e.memset(ones8[:, :], 1.0), writes=[b_ones])
    sc.op("dve", lambda e: e.memset(epsc[:, :], EPS), writes=[b_ones])
    sc.op("dve", lambda e: e.tensor_copy(out=trib[:, :], in_=cs[:, CS_TRI:CS_TRI + 128]), reads=[b_cs], writes=[b_trib])
    sc.op("dve", lambda e: e.tensor_copy(out=identb[:, :], in_=cs[:, CS_ID:CS_ID + 128]), reads=[b_cs], writes=[b_identb])
    for l in range(NL):
        sc.op("dve", lambda e, l=l: e.tensor_scalar(out=nfb[:, l:l + 1], in0=pp[0:8, l * PPL + PP_FB:l * PPL + PP_FB + 1],
                                                     scalar1=-1.0, scalar2=None, op0=ALU.mult), reads=[b_pp], writes=[b_nfb])
    sc.dma("pool", [(gdrow[:, 0:1], zt[0:8, 0:1]), (gdtok[0:1, :], zt[0:1, 0:8])], reads=[b_zt], writes=[b_pad])

    cvt_engs = ["dve", "act"]

    cvt_state = {"next": 0, "loaded": 0}

    def cvt_load(i):
        c0 = i * CVT
        w = min(CVT, WTOT - c0)
        sc.dma("sp", [(cin[i % 2][:, 0:w], wf[:, c0:c0 + w])], writes=[b_cin[i % 2]])

    def cvt_conv(i):
        c0 = i * CVT
        w = min(CVT, WTOT - c0)
        k = i % 2
        copy_op(cvt_engs[i % 2], cout[k][:, 0:w], cin[k][:, 0:w], [b_cin[k]], [b_cout[k]])
        sc.dma("pool", [(wb[:, c0:c0 + w], cout[k][:, 0:w])], reads=[b_cout[k]], writes=[b_wbg[blk_group(i)]])

    cvt_limit = {"v": NBLK}

    def convert_some(n, limit=None):
        lim = cvt_limit["v"] if limit is None else limit
        for _ in range(n):
            i = cvt_state["next"]
            if i >= lim:
                return
            while cvt_state["loaded"] < min(i + 2, lim):
                cvt_load(cvt_state["loaded"])
                cvt_state["loaded"] += 1
            cvt_conv(i)
            cvt_state["next"] += 1

    convert_some(NEARLY, NEARLY)

    def rmsnorm_to_hT(xt, bx, gcol):
        ps, bps = psr.get()
        for c in range(KC):
            sc.op("act", lambda e, c=c: e.activation(out=sq[:, c, :], in_=xt[:, c, :], func=AF.Square, scale=1.0 / 32.0),
                  reads=[bx[c]], writes=[b_sq[c]])
        for c in range(KC):
            sc.op("pe", lambda e, c=c: e.matmul(ps[:, :], onesb[:, :], sq[:, c, :], start=(c == 0), stop=(c == KC - 1)),
                  reads=[b_sq[c], b_ones], writes=[bps], inc=(c == KC - 1))
        rsqrt_eps(rr[:, :], ps[:, :], [bps], [b_rr])
        for c in range(KC):
            sc.op("dve", lambda e, c=c: e.scalar_tensor_tensor(out=hT[:, c, :], in0=xt[:, c, :], scalar=pp[:, gcol + c:gcol + c + 1],
                                                                in1=rr[:, :], op0=ALU.mult, op1=ALU.mult),
                  reads=[bx[c], b_rr, b_pp], writes=[b_hT[c]])

    dbgflag = {"first": debug}

    cvt_budget = {"n": 0}

    def ffn(l, which, xt, bx, mid=None):
        gcol = l * PPL + (PP_F1 if which == 1 else PP_F2)
        rmsnorm_to_hT(xt, bx, gcol)
        first = False
        for f in range(FC):
            wt, bw = wget(f"{l}.{which}.gu{f}")
            wv = wt[:, 0:2048].rearrange("p (g k n) -> p g k n", g=2, k=KC)
            psg, bpg = psr.get()
            psu, bpu = psr.get()
            for k in range(KC):
                sc.op("pe", lambda e, k=k, psg=psg, wv=wv: e.matmul(psg[:, :], wv[:, 0, k, :], hT[:, k, :], start=(k == 0), stop=(k == KC - 1)),
                      reads=[bw, b_hT[k]], writes=[bpg], inc=(k == KC - 1))
            for k in range(KC):
                sc.op("pe", lambda e, k=k, psu=psu, wv=wv: e.matmul(psu[:, :], wv[:, 1, k, :], hT[:, k, :], start=(k == 0), stop=(k == KC - 1)),
                      reads=[bw, b_hT[k]], writes=[bpu], inc=(k == KC - 1))
            si = rot["sg"] % 2
            rot["sg"] += 1
            sc.op("act", lambda e, psg=psg, si=si: e.activation(out=sgs[si][:, :], in_=psg[:, :], func=AF.Silu),
                  reads=[bpg], writes=[b_sg[si]])
            sc.op("dve", lambda e, f=f, psu=psu, si=si: e.tensor_tensor(out=G[:, f, :], in0=psu[:, :], in1=sgs[si][:, :], op=ALU.mult),
                  reads=[bpu, b_sg[si]], writes=[b_G[f]])
            if which == 1 and f % 3 == 2 and cvt_budget["n"] > 0:
                convert_some(1)
                cvt_budget["n"] -= 1
        if mid is not None:
            mid()
        for c in range(KC):
            wt, bw = wget(f"{l}.{which}.dn{c}")
            wv = wt[:, 0:FC * 128].rearrange("p (k n) -> p k n", k=FC)
            psd, bpd = psr.get()
            for f in range(FC):
                sc.op("pe", lambda e, f=f, psd=psd, wv=wv: e.matmul(psd[:, :], wv[:, f, :], G[:, f, :], start=(f == 0), stop=(f == FC - 1)),
                      reads=[bw, b_G[f]], writes=[bpd], inc=(f == FC - 1))
            sc.op("dve", lambda e, c=c, psd=psd: e.scalar_tensor_tensor(out=xt[:, c, :], in0=psd[:, :], scalar=0.5, in1=xt[:, c, :],
                                                                        op0=ALU.mult, op1=ALU.add),
                  reads=[bpd, bx[c]], writes=[bx[c]])

    def next_stg():
        i = rot["stg"] % NSTG
        rot["stg"] += 1
        return stg[i], b_stg[i]

    def mixin(l, t, xt, bx):
        base = l * PPL
        rmsnorm_to_hT(xt, bx, base + PP_MX)
        t0, t1 = t * T, (t + 1) * T
        for which in ("q", "k"):
            for j in range(4):
                wt, bw = wget(f"{l}.{which}{j}")
                wv = wt[:, 0:1024].rearrange("p (k n) -> p k n", k=KC)
                ps, bps = psr.get()
                for k in range(KC):
                    sc.op("pe", lambda e, k=k, ps=ps, wv=wv: e.matmul(ps[:, :], wv[:, k, :], hT[:, k, :], start=(k == 0), stop=(k == KC - 1)),
                          reads=[bw, b_hT[k]], writes=[bps], inc=(k == KC - 1))
                st, bst = next_stg()
                if which == "q":
                    sc.op("act", lambda e, ps=ps, st=st: e.activation(out=st[:, :], in_=ps[:, :], func=AF.Copy, scale=0.125),
                          reads=[bps], writes=[bst])
                    sc.dma("pool", [(qs[2 * j + hh, :, t0:t1], st[hh * 64:(hh + 1) * 64, :]) for hh in range(2)], reads=[bst], writes=[b_qs[t]])
                else:
                    sc.op("dve", lambda e, ps=ps, st=st: e.tensor_copy(out=st[:, :], in_=ps[:, :]), reads=[bps], writes=[bst])
                    sc.dma("pool", [(ex_k(exs[t], 2 * j + hh), st[hh * 64:(hh + 1) * 64, :]) for hh in range(2)], reads=[bst], writes=[b_exs[t]])
        for j in range(2):
            wt, bw = wget(f"{l}.v{j}")
            wv = wt[:, 0:2048].rearrange("p (k n) -> p k n", k=KC)
            for s in range(4):
                ps, bps = psr.get()
                for k in range(KC):
                    sc.op("pe", lambda e, k=k, ps=ps, wv=wv, s=s: e.matmul(ps[:, 0:256], hT[:, k, s * 128:(s + 1) * 128], wv[:, k, :],
                                                                           start=(k == 0), stop=(k == KC - 1)),
                          reads=[bw, b_hT[k]], writes=[bps], inc=(k == KC - 1))
                en = alt_eng()
                copy_op(en, vst[:, 4 * j:4 * j + 4, s, 0:64], ps[:, 0:256].rearrange("p (h d) -> p h d", h=4), [bps], [b_vst])
        sc.dma("pool", [(ex_v(exs[t]).rearrange("h p k d -> p h k d"), vst[:, :, :, :])], reads=[b_vst], writes=[b_exs[t]])
        wt, bw = wget(f"{l}.zf")
        wv = wt[:, 0:64].rearrange("p (k n) -> p k n", k=KC)
        ps, bps = psr.get()
        for k in range(KC):
            sc.op("pe", lambda e, k=k, ps=ps, wv=wv: e.matmul(ps[0:8, :], wv[:, k, :], hT[:, k, :], start=(k == 0), stop=(k == KC - 1)),
                  reads=[bw, b_hT[k]], writes=[bps], inc=(k == KC - 1))
        sc.op("act", lambda e, ps=ps: e.activation(out=ex8[:, :], in_=ps[0:8, :], func=AF.Exp, bias=nfb[:, l:l + 1], scale=-1.0),
              reads=[bps, b_nfb], writes=[b_ex8])
        sc.op("act", lambda e: e.activation(out=lt8[:, :], in_=ex8[:, :], func=AF.Ln, bias=1.0, scale=1.0), reads=[b_ex8], writes=[b_lt8])
        sc.dma("pool", [(lsrc[t * 8:(t + 1) * 8, :], lt8[:, :])], reads=[b_lt8], writes=[b_lsrc])
        for c in range(2):
            wt, bw = wget(f"{l}.pool{c}")
            wv = wt[:, 0:1024].rearrange("p (k n) -> p k n", k=KC)
            ps, bps = psr.get()
            for k in range(KC):
                sc.op("pe", lambda e, k=k, ps=ps, wv=wv: e.matmul(ps[:, :], wv[:, k, :], hT[:, k, :], start=(k == 0), stop=(k == KC - 1)),
                      reads=[bw, b_hT[k]], writes=[bps], inc=(k == KC - 1))
            st, bst = next_stg()
            copy_op("act", st[:, :], ps[:, :], [bps], [bst])
            sc.dma("pool", [(ex_up(exs[t], c), st[:, :])], reads=[bst], writes=[b_exs[t]])
        for c in range(2):
            wta, bwa = wget(f"{l}.a{c}")
            wva = wta[:, 0:1024].rearrange("p (k n) -> p k n", k=KC)
            psa, bpa = psr.get()
            for k in range(KC):
                sc.op("pe", lambda e, k=k, psa=psa, wva=wva: e.matmul(psa[:, :], wva[:, k, :], hT[:, k, :], start=(k == 0), stop=(k == KC - 1)),
                      reads=[bwa, b_hT[k]], writes=[bpa], inc=(k == KC - 1))
            wtg, bwg = wget(f"{l}.g{c}")
            wvg = wtg[:, 0:1024].rearrange("p (k n) -> p k n", k=KC)
            psg, bpg = psr.get()
            for k in range(KC):
                sc.op("pe", lambda e, k=k, psg=psg, wvg=wvg: e.matmul(psg[:, :], wvg[:, k, :], hT[:, k, :], start=(k == 0), stop=(k == KC - 1)),
                      reads=[bwg, b_hT[k]], writes=[bpg], inc=(k == KC - 1))
            sc.op("act", lambda e, psg=psg: e.activation(out=sgm[:, :], in_=psg[:, :], func=AF.Sigmoid), reads=[bpg], writes=[b_sgm])
            st, bst = next_stg()
            sc.op("dve", lambda e, psa=psa, st=st: e.tensor_tensor(out=st[:, :], in0=psa[:, :], in1=sgm[:, :], op=ALU.mult),
                  reads=[bpa, b_sgm], writes=[bst])
            sc.dma("pool", [(ex_u(exs[t], c), st[:, :])], reads=[bst], writes=[b_exs[t]])

    def load_x(src_ap, src_bufs, t):
        slot = load_x.n % NX
        load_x.n += 1
        sc.dma("sp", [(xring[slot][:, :, :], xview(src_ap, t))], reads=src_bufs, writes=b_x[slot])
        return xring[slot], b_x[slot]
    load_x.n = 0

    def loop_a(l):
        src = xin if l == 0 else xs
        sc.op("dve", lambda e: e.memset(vst[:, :, :, :], 1.0), writes=[b_vst])
        cvt_limit["v"] = NEARLY + 5 * NT if l == 0 else NBLK
        layer_setup_b(l)
        nxt = load_x(src, [] if l == 0 else [b_xs[0]], 0)
        for t in range(NT):
            xt, bx = nxt
            if t + 1 < NT:
                nxt = load_x(src, [] if l == 0 else [b_xs[t + 1]], t + 1)
            cvt_budget["n"] = 5 if l == 0 else 3
            if l > 0:
                ffn(l - 1, 2, xt, bx)
            ffn(l, 1, xt, bx)
            sc.dma("pool", [(xview(xs, t), xt[:, :, :])], reads=bx, writes=[b_xs[t]])
            mixin(l, t, xt, bx)
            convert_some(cvt_budget["n"])
            cvt_budget["n"] = 0
            sc.collective(exs_t[t].ap().opt(), exd_t[t].ap().opt(), b_exs[t], b_exd[t])
        sc.collective(lsrc_t.ap().opt(), ldst_t.ap().opt(), b_lsrc, b_ldst)
        lall = ldst.rearrange("(q j h) t -> h j q t", q=2, j=NT, h=8)
        for hh in range(2):
            sc.dma("sp", [(cin[hh][0:8, :].rearrange("h (j q t) -> h j q t", j=NT // 2, q=2)[:, :, q_, :],
                           lall[:, (NT // 2) * hh:(NT // 2) * (hh + 1), q_, :]) for q_ in range(2)],
                   reads=[b_ldst], writes=[b_cin[hh]])
        for g in range(NB):
            hh, bi = g // 8, g % 8
            seg = cin[hh][0:8, bi * T:(bi + 1) * T]
            if g == 0:
                sc.op("dve", lambda e, seg=seg: e.tensor_tensor_scan(out=seg, data0=ones8[:, :], data1=seg, initial=0.0, op0=ALU.mult, op1=ALU.add),
                      reads=[b_cin[hh], b_ones], writes=[b_cin[hh]])
            else:
                ph, pb = (g - 1) // 8, (g - 1) % 8
                carry = cin[ph][0:8, (pb + 1) * T - 1:(pb + 1) * T]
                sc.op("dve", lambda e, seg=seg, carry=carry: e.tensor_tensor_scan(out=seg, data0=ones8[:, :], data1=seg, initial=carry,
                                                                                op0=ALU.mult, op1=ALU.add),
                      reads=[b_cin[hh], b_cin[ph], b_ones], writes=[b_cin[hh]])
        sc.dma("pool", [(gdrow[:, 1 + hh * 4096:1 + (hh + 1) * 4096], cin[hh][0:8, :]) for hh in range(2)], reads=b_cin, writes=[b_gd])
        ps, bps = psr.get()
        for kt in range(S // 128):
            hh, off = kt // 32, (kt % 32) * 128
            sc.op("pe", lambda e, kt=kt, hh=hh, off=off: e.transpose(ps[:, kt * 8:(kt + 1) * 8], cin[hh][0:8, off:off + 128], cs[0:8, CS_ID:CS_ID + 8]),
                  reads=[b_cin[hh], b_cs], writes=[bps], inc=(kt == S // 128 - 1))
        sc.op("dve", lambda e: e.tensor_copy(out=gkall[:, :, :], in_=ps[:, :].rearrange("p (k h) -> p k h", h=8)), reads=[bps], writes=[b_gkall])
        sc.dma("pool", [(gdtok[1:1 + S, :].rearrange("(k p) h -> p k h", p=128), gkall[:, :, :])], reads=[b_gkall], writes=[b_gd])

    def layer_setup_b(l):
        base = l * PPL
        wt, bw = wget(f"{l}.pbd")
        sc.op("dve", lambda e: e.tensor_copy(out=poolbd[:, :, :], in_=wt[:, 0:256].rearrange("p (c n) -> p c n", c=2)),
              reads=[bw], writes=[b_poolbd])
        for c in range(2):
            for j in range(31):
                col = base + PP_CW + c * 31 + j
                sc.op("dve", lambda e, c=c, j=j, col=col: e.tensor_scalar(out=convdiag[:, c, j, :], in0=cs[:, CS_ID:CS_ID + 128],
                                                                            scalar1=pp[:, col:col + 1], scalar2=None, op0=ALU.mult),
                      reads=[b_cs, b_pp], writes=[b_convdiag])

    def prologue_stages(l, i):
        base = l * PPL
        par = i % 2
        t0, t1 = i * T, (i + 1) * T
        nkt = 8 * (i + 1)
        qaug, b_qaug, biasT, b_biasT = qaugs[par], b_qaugs[par], biasTs[par], b_biasTs[par]
        mixP, mixC, b_mixP, b_mixC = mixPs[par], mixCs[par], b_mixPs[par], b_mixCs[par]
        W = T + 16
        st = {}
        m0c = pp[:, PP_M0:PP_M0 + 1]
        m1c = pp[:, PP_M1:PP_M1 + 1]

        def s0():
            gA, gB = 2 * i, 2 * i + 1
            sc.dma("sp", [(qaug[0:64, :, :], qs.rearrange("h d t -> d h t")[:, :, t0:t1])], reads=[b_qs[i]], writes=[b_qaug])
            sc.dma("sp", [(grow[:, :], gdrow[:, gA * T:gA * T + T + 1])], reads=[b_gd, b_pad], writes=[b_grow])
            sc.dma("sp", [(growB[:, :], gdrow[:, gB * T:gB * T + T + 1])], reads=[b_gd, b_pad], writes=[b_growB])
            sc.dma("sp", [(grefbc[:, :], gdtok[gA * T:gA * T + 1, :].partition_broadcast(128).squeeze(1))], reads=[b_gd, b_pad], writes=[b_grefbc])
            sc.dma("sp", [(grefB[:, :], gdtok[gB * T:gB * T + 1, :].partition_broadcast(128).squeeze(1))], reads=[b_gd, b_pad], writes=[b_grefB])
            sc.op("dve", lambda e: e.tensor_scalar(out=growB[:, :], in0=growB[:, :], scalar1=m1c[0:8, :], scalar2=None, op0=ALU.mult),
                  reads=[b_growB, b_pp], writes=[b_growB])
            sc.op("dve", lambda e: e.scalar_tensor_tensor(out=grow[:, :], in0=grow[:, :], scalar=m0c[0:8, :], in1=growB[:, :], op0=ALU.mult, op1=ALU.add),
                  reads=[b_grow, b_growB, b_pp], writes=[b_grow])
            sc.op("dve", lambda e: e.tensor_scalar(out=grefB[:, :], in0=grefB[:, :], scalar1=m1c, scalar2=None, op0=ALU.mult),
                  reads=[b_grefB, b_pp], writes=[b_grefB])
            sc.op("dve", lambda e: e.scalar_tensor_tensor(out=grefbc[:, :], in0=grefbc[:, :], scalar=m0c, in1=grefB[:, :], op0=ALU.mult, op1=ALU.add),
                  reads=[b_grefbc, b_grefB, b_pp], writes=[b_grefbc])
            sc.dma("sp", [(uph[:, c, 16:T + 16], ex_up(exs[i], c)) for c in range(2)] + [(uh[:, c, 32:T + 32], ex_u(exs[i], c)) for c in range(2)],
                   reads=[b_exs[i]], writes=[b_uph, b_uh])
            if i > 0:
                prv = exd_rank(i - 1, 1)
                sc.dma("sp", [(haloA[:, c, :], ex_u(prv, c)[:, T - 32:T]) for c in range(2)] + [(phaloA[:, c, :], ex_up(prv, c)[:, T - 16:T]) for c in range(2)],
                       reads=[b_exd[i - 1]], writes=[b_halo])
            else:
                sc.op("dve", lambda e: e.memset(haloA[:, :, :], 0.0), writes=[b_halo])
                sc.op("dve", lambda e: e.memset(phaloA[:, :, :], 0.0), writes=[b_halo])
            cur0 = exd_rank(i, 0)
            sc.dma("sp", [(haloB[:, c, :], ex_u(cur0, c)[:, T - 32:T]) for c in range(2)] + [(phaloB[:, c, :], ex_up(cur0, c)[:, T - 16:T]) for c in range(2)],
                   reads=[b_exd[i]], writes=[b_halo])
            for (hA, hB, dst, w_) in ((haloA, haloB, uh, 32), (phaloA, phaloB, uph, 16)):
                sc.op("dve", lambda e, hB=hB: e.tensor_scalar(out=hB[:, :, :], in0=hB[:, :, :], scalar1=m1c, scalar2=None, op0=ALU.mult),
                      reads=[b_halo, b_pp], writes=[b_halo])
                sc.op("dve", lambda e, hA=hA, hB=hB, dst=dst, w_=w_: e.scalar_tensor_tensor(out=dst[:, :, 0:w_], in0=hA[:, :, :], scalar=m0c, in1=hB[:, :, :],
                                                                                          op0=ALU.mult, op1=ALU.add),
                      reads=[b_halo, b_pp], writes=[b_uph, b_uh])
            sc.op("dve", lambda e: e.tensor_scalar(out=xq[:, :], in0=grow[:, 1:T + 1], scalar1=grow[:, 0:1], scalar2=-1.0, op0=ALU.subtract, op1=ALU.mult),
                  reads=[b_grow], writes=[b_xq])
            sc.op("dve", lambda e: e.tensor_copy(out=xhi[:, :], in_=xq[:, :]), reads=[b_xq], writes=[b_xhi])
            sc.op("dve", lambda e: e.tensor_copy(out=x32[:, :], in_=xhi[:, :]), reads=[b_xhi], writes=[b_x32])
            sc.op("dve", lambda e: e.tensor_tensor(out=xr1[:, :], in0=xq[:, :], in1=x32[:, :], op=ALU.subtract), reads=[b_xq, b_x32], writes=[b_xr1])
            sc.op("dve", lambda e: e.tensor_copy(out=xmid[:, :], in_=xr1[:, :]), reads=[b_xr1], writes=[b_xmid])
            sc.op("dve", lambda e: e.tensor_copy(out=x32[:, :], in_=xmid[:, :]), reads=[b_xmid], writes=[b_x32])
            sc.op("dve", lambda e: e.tensor_tensor(out=xr1[:, :], in0=xr1[:, :], in1=x32[:, :], op=ALU.subtract), reads=[b_xr1, b_x32], writes=[b_xr1])
            sc.op("dve", lambda e: e.tensor_copy(out=xlo[:, :], in_=xr1[:, :]), reads=[b_xr1], writes=[b_xlo])
            sc.dma("pool", [(xqd[0, :, :], xhi[:, :]), (xqd[1, :, :], xmid[:, :]), (xqd[2, :, :], xlo[:, :])],
                   reads=[b_xhi, b_xmid, b_xlo], writes=[b_xqd])
            sc.dma("pool", [(qaug[64:67, :, :], xqd[:, :, :])], reads=[b_xqd], writes=[b_qaug])
            sc.op("dve", lambda e: e.tensor_tensor(out=biasT[:, 0:nkt, :], in0=gkall[:, 0:nkt, :],
                                                   in1=grefbc[:, :].unsqueeze(1).to_broadcast([128, nkt, 8]), op=ALU.subtract),
                  reads=[b_gkall, b_grefbc], writes=[b_biasT])
            bown, b_bown = bowns[par], b_bowns[par]
            sc.op("dve", lambda e: e.tensor_scalar(out=bown[:, :, :], in0=biasT[:, 8 * i + 4:8 * i + 8, :], scalar1=m1c, scalar2=None, op0=ALU.mult),
                  reads=[b_biasT, b_pp], writes=[b_bown])
            sc.op("dve", lambda e: e.scalar_tensor_tensor(out=bown[:, :, :], in0=biasT[:, 8 * i:8 * i + 4, :], scalar=m0c, in1=bown[:, :, :],
                                                          op0=ALU.mult, op1=ALU.add), reads=[b_biasT, b_bown, b_pp], writes=[b_bown])
            sc.op("dve", lambda e: e.tensor_scalar(out=biasT[:, 8 * i:8 * i + 4, :], in0=biasT[:, 8 * i:8 * i + 4, :],
                                                   scalar1=pp[:, PP_M0N:PP_M0N + 1], scalar2=None, op0=ALU.add),
                  reads=[b_biasT, b_bown, b_pp], writes=[b_biasT])
            sc.op("dve", lambda e: e.tensor_tensor(out=sA[:, :, 1:W], in0=uph[:, :, 1:W], in1=uph[:, :, 0:W - 1], op=ALU.add),
                  reads=[b_uph], writes=[b_sA])
            sc.op("dve", lambda e: e.tensor_tensor(out=sB[:, :, 3:W], in0=sA[:, :, 3:W], in1=sA[:, :, 1:W - 2], op=ALU.add),
                  reads=[b_sA], writes=[b_sB])
            sc.op("dve", lambda e: e.tensor_tensor(out=sA[:, 1, 7:W], in0=sB[:, 1, 7:W], in1=sB[:, 1, 3:W - 4], op=ALU.add),
                  reads=[b_sB], writes=[b_sA])
            sc.op("dve", lambda e: e.tensor_tensor(out=sB[64:128, 1, 15:W], in0=sA[64:128, 1, 15:W], in1=sA[64:128, 1, 7:W - 8], op=ALU.add),
                  reads=[b_sA], writes=[b_sB])
            srcs = [(sA, 0, 0), (sB, 0, 1), (sA, 1, 0), (sB, 1, 1)]
            for (stile, c, half) in srcs:
                p0, p1 = half * 64, (half + 1) * 64
                sc.op("dve", lambda e, stile=stile, c=c, p0=p0, p1=p1: e.scalar_tensor_tensor(
                    out=pooledb[p0:p1, c, :], in0=stile[p0:p1, c, 16:W], scalar=cs[p0:p1, CS_INVW + c:CS_INVW + c + 1],
                    in1=uph[p0:p1, c, 16:W], op0=ALU.mult, op1=ALU.subtract), reads=[b_sA, b_sB, b_uph, b_cs], writes=[b_pooledb])
            if i == 0:
                for (stile, c, half) in srcs:
                    p0, p1 = half * 64, (half + 1) * 64
                    sc.op("dve", lambda e, stile=stile, c=c, p0=p0, p1=p1: e.tensor_tensor(
                        out=t0tmp[p0:p1, c, :], in0=stile[p0:p1, c, 16:32], in1=cs[p0:p1, CS_T0 + c * 16:CS_T0 + (c + 1) * 16], op=ALU.mult),
                        reads=[b_sA, b_sB, b_cs], writes=[b_t0tmp])
                    sc.op("dve", lambda e, c=c, p0=p0, p1=p1: e.tensor_tensor(
                        out=pooledb[p0:p1, c, 0:16], in0=t0tmp[p0:p1, c, :], in1=uph[p0:p1, c, 16:32], op=ALU.subtract),
                        reads=[b_t0tmp, b_uph], writes=[b_pooledb])

        def s1():
            for c in range(2):
                ps, bps = psr.get()
                sc.op("pe", lambda e, c=c, ps=ps: e.matmul(ps[:, :], poolbd[:, c, :], pooledb[:, c, :], start=True, stop=True),
                      reads=[b_poolbd, b_pooledb], writes=[bps])
                col = base + PP_PS + c
                sc.op("act", lambda e, c=c, ps=ps, col=col: e.activation(out=mixP[:, c, :], in_=ps[:, :], func=AF.Identity, scale=pp[:, col:col + 1]),
                      reads=[bps, b_pp], writes=[b_mixP[c]])
            st["pc"] = []
            for c in range(2):
                ps, bps = psr.get(hold=True)
                for j in range(31):
                    sc.op("pe", lambda e, c=c, j=j, ps=ps: e.matmul(ps[:, :], convdiag[:, c, j, :], uh[:, c, 2 + j:2 + j + T], start=(j == 0), stop=(j == 30)),
                          reads=[b_convdiag, b_uh], writes=[bps], inc=(j == 30))
                st["pc"].append((ps, bps))

        def s2():
            for c in range(2):
                ps, bps = st["pc"][c]
                col = base + PP_CB + c
                sc.op("act", lambda e, c=c, ps=ps, col=col: e.activation(out=yconv[:, c, :], in_=ps[:, :], func=AF.Identity, bias=pp[:, col:col + 1], scale=1.0),
                      reads=[bps, b_pp], writes=[b_yconv])
                psr.release(bps)
            sc.op("dve", lambda e: e.tensor_tensor(out=ysq[:, :, :], in0=yconv[:, :, :], in1=yconv[:, :, :], op=ALU.mult), reads=[b_yconv], writes=[b_ysq])

        def s3():
            ps1, bp1 = psr.get(hold=True)
            ps2, bp2 = psr.get(hold=True)
            for c in range(2):
                sc.op("pe", lambda e, c=c: e.matmul(ps1[:, :], onesf[:, :], yconv[:, c, :], start=(c == 0), stop=(c == 1)),
                      reads=[b_ones, b_yconv], writes=[bp1], inc=(c == 1))
            for c in range(2):
                sc.op("pe", lambda e, c=c: e.matmul(ps2[:, :], onesf[:, :], ysq[:, c, :], start=(c == 0), stop=(c == 1)),
                      reads=[b_ones, b_ysq], writes=[bp2], inc=(c == 1))
            st["ln"] = (ps1, bp1, ps2, bp2)

        def s4():
            ps1, bp1, ps2, bp2 = st["ln"]
            sc.op("dve", lambda e: e.tensor_scalar(out=mstat[:, :], in0=ps1[:, :], scalar1=1.0 / 256.0, scalar2=None, op0=ALU.mult), reads=[bp1], writes=[b_mstat])
            sc.op("dve", lambda e: e.tensor_tensor(out=msq[:, :], in0=mstat[:, :], in1=mstat[:, :], op=ALU.mult), reads=[b_mstat], writes=[b_msq])
            sc.op("dve", lambda e: e.scalar_tensor_tensor(out=rstd[:, :], in0=ps2[:, :], scalar=1.0 / 256.0, in1=msq[:, :], op0=ALU.mult, op1=ALU.subtract),
                  reads=[bp2, b_msq], writes=[b_rstd])
            psr.release(bp1)
            psr.release(bp2)
            for c, (tt, bt) in enumerate(((tmpn, b_tmpn), (tmpn2, b_tmpn2))):
                sc.op("dve", lambda e, c=c, tt=tt: e.tensor_tensor(out=tt[:, :], in0=yconv[:, c, :], in1=mstat[:, :], op=ALU.subtract),
                      reads=[b_yconv, b_mstat], writes=[bt])

        def s5():
            rsqrt_eps(rstd[:, :], rstd[:, :], [b_rstd], [b_rstd])
            for c, (tt, bt) in enumerate(((tmpn, b_tmpn), (tmpn2, b_tmpn2))):
                sc.op("dve", lambda e, tt=tt: e.tensor_tensor(out=tt[:, :], in0=tt[:, :], in1=rstd[:, :], op=ALU.mult), reads=[bt, b_rstd], writes=[bt])
            for c, (tt, bt) in enumerate(((tmpn, b_tmpn), (tmpn2, b_tmpn2))):
                cg, cb = base + PP_LG + c, base + PP_LB + c
                sc.op("act", lambda e, c=c, cg=cg, cb=cb, tt=tt: e.activation(out=mixC[:, c, :], in_=tt[:, :], func=AF.Silu, bias=pp[:, cb:cb + 1], scale=pp[:, cg:cg + 1]),
                      reads=[bt, b_pp], writes=[b_mixC[c]])

        return [s0, s1, s2, s3, s4, s5]

    LOOKAHEAD = 2

    def attention_layer(l, slot_hooks):
        work_all = []
        for i in range(NT):
            nkg = 8 * i + 4
            npc = (nkg + 15) // 16
            for h in range(H):
                for pc in range(npc):
                    work_all.append((i, h, pc))
                work_all.append((i, h, "own"))
        loaded = {"n": 0}

        def issue_loads(upto):
            while loaded["n"] < min(upto, len(work_all)):
                n = loaded["n"]
                i, h, pc = work_all[n]
                slot = n % NKP
                if pc == "own":
                    sc.dma("sp", [(kps[slot][0:64, 0:T], ex_k(exs[i], h))], reads=[b_exs[i]], writes=[b_kp[slot]])
                    sc.dma("sp", [(vps[slot][:, 0:4, 0:65], ex_v(exs[i])[h])], reads=[b_exs[i]], writes=[b_vp[slot]])
                    loaded["n"] += 1
                    continue
                kpairs, vpairs, rds = [], [], []
                for jj in range(2):
                    j = 2 * pc + jj
                    if j > i:
                        continue
                    rds.append(b_exd[j])
                    if j < i:
                        both = exd[j].rearrange("(q r) c -> r q c", q=2)
                        kpairs.append((kps[slot][0:64, jj * 1024:(jj + 1) * 1024].rearrange("p (q c) -> p q c", q=2), both[h * 64:(h + 1) * 64, :, :]))
                        qs_ = (0, 1)
                    else:
                        kpairs.append((kps[slot][0:64, jj * 1024:jj * 1024 + T], ex_k(exd_rank(j, 0), h)))
                        qs_ = (0,)
                    for q_ in qs_:
                        vsrc = ex_v(exd_rank(j, q_))[h]
                        vpairs.append((vps[slot][:, jj * 8 + q_ * 4:jj * 8 + q_ * 4 + 4, 0:65], vsrc))
                sc.dma("sp", kpairs, reads=rds, writes=[b_kp[slot]])
                sc.dma("sp", vpairs, reads=rds, writes=[b_vp[slot]])
                loaded["n"] += 1

        def qk(h, j, c0, tri, kp, bkp, vp, bvp, qaug, b_qaug, bias_ap, b_bias, first, last):
            pss, bpss = psr.get()
            if tri:
                sc.op("pe", lambda e: e.matmul(pss[:, c0:c0 + 128], kp[:, j * 128:(j + 1) * 128], qaug[:, h, c0:c0 + 128], start=True, stop=False),
                      reads=[bkp, b_qaug], writes=[bpss], inc=False)
                sc.op("pe", lambda e: e.matmul(pss[:, c0:c0 + 128], identb[:, :], trib[:, :], start=False, stop=True),
                      reads=[b_identb, b_trib], writes=[bpss], inc=(c0 + 128 >= T))
                if c0 + 128 < T:
                    sc.op("pe", lambda e: e.matmul(pss[:, c0 + 128:T], kp[:, j * 128:(j + 1) * 128], qaug[:, h, c0 + 128:T], start=True, stop=True),
                          reads=[bkp, b_qaug], writes=[bpss])
            else:
                sc.op("pe", lambda e: e.matmul(pss[:, :], kp[:, j * 128:(j + 1) * 128], qaug[:, h, :], start=True, stop=True),
                      reads=[bkp, b_qaug], writes=[bpss])
            return (j, c0, pss, bpss, vp, bvp, bias_ap, b_bias, first, last)

        def exp_pv(item, pso_t, bpo):
            j, c0, pss, bpss, vp, bvp, bias_ap, b_bias, first, last = item
            pi = rot["pt"] % NP
            rot["pt"] += 1
            pt, bpt = pts[pi], b_pt[pi]
            sc.op("act", lambda e: e.activation(out=pt[:, c0:T], in_=pss[:, c0:T], func=AF.Exp, bias=bias_ap, scale=1.0),
                  reads=[bpss, b_bias], writes=[bpt])
            sc.op("pe", lambda e: e.matmul(pso_t[:, c0:T], vp[:, j, :], pt[:, c0:T], start=first, stop=last, skip_group_check=True),
                  reads=[bvp, bpt], writes=[bpo], inc=last)

        def fin_a(h, pso_t, bpo):
            k = h % 2
            sc.op("dve", lambda e: e.reciprocal(out=r1ts[k][64:65, :], in_=pso_t[64:65, :]), reads=[bpo], writes=[b_r1t[k]])
            sc.dma("pool", [(rdd[k:k + 1, :], r1ts[k][64:65, :])], reads=[b_r1t[k]], writes=[b_rdd])
            sc.dma("pool", [(bcss[k][:, :], rdd[k:k + 1, :].partition_broadcast(64).squeeze(1))], reads=[b_rdd], writes=[b_bcs[k]])

        def fin_b(i, h, pso_t, bpo):
            k = h % 2
            yb, byb = ybTs[i % 2], b_ybTs[i % 2]
            sc.op("dve", lambda e: e.tensor_tensor(out=yb[:, h, :], in0=pso_t[0:64, :], in1=bcss[k][:, :], op=ALU.mult),
                  reads=[bpo, b_bcs[k]], writes=[byb[h]])

        pending_fin = None
        n = 0
        for i in range(NT):
            par = i % 2
            qaug, b_qaug, biasT, b_biasT = qaugs[par], b_qaugs[par], biasTs[par], b_biasTs[par]
            bown, b_bown = bowns[par], b_bowns[par]
            nkg = 8 * i + 4
            npc = (nkg + 15) // 16
            hooks = slot_hooks(i)
            for h in range(H):
                cur_o = pso.get()
                pend = []
                cnt = 0
                for pc in list(range(npc)) + ["own"]:
                    issue_loads(n + 2)
                    slot = n % NKP
                    n += 1
                    if pc == "own":
                        tiles = [(j, j * 128, True, bown[:, j, h:h + 1], b_bown, False, j == 3) for j in range(4)]
                    else:
                        nk = min(16, nkg - pc * 16)
                        tiles = [(j, 0, False, biasT[:, pc * 16 + j, h:h + 1], b_biasT, (pc == 0 and j == 0), False) for j in range(nk)]
                    for (j, c0, tri, bias_ap, b_bias, first, last) in tiles:
                        pend.append(qk(h, j, c0, tri, kps[slot], b_kp[slot], vps[slot], b_vp[slot], qaug, b_qaug, bias_ap, b_bias, first, last))
                        if len(pend) > LOOKAHEAD:
                            exp_pv(pend.pop(0), cur_o[0], cur_o[1])
                        cnt += 1
                        if cnt == 6 and pending_fin is not None:
                            fin_b(*pending_fin)
                            pending_fin = None
                while pend:
                    exp_pv(pend.pop(0), cur_o[0], cur_o[1])
                if pending_fin is not None:
                    fin_b(*pending_fin)
                fin_a(h, cur_o[0], cur_o[1])
                pending_fin = (i, h, cur_o[0], cur_o[1])
                if h in hooks:
                    hooks[h]()
            if "end" in hooks:
                hooks["end"]()
        fin_b(*pending_fin)

    def w_out(l, i, xt, bx):
        par = i % 2
        mixP, mixC, b_mixP, b_mixC = mixPs[par], mixCs[par], b_mixPs[par], b_mixCs[par]
        yb, byb = ybTs[par], b_ybTs[par]
        for c in range(KC):
            wt, bw = wget(f"{l}.out{c}")
            wv = wt[:, 0:1536].rearrange("p (k n) -> p k n", k=12)
            ps, bps = psr.get()
            ops = []
            for k in range(2):
                ops.append((wv[:, k, :], mixP[:, k, :], b_mixP[k]))
            for h in range(H):
                ops.append((wv[0:64, 2 + h, :], yb[:, h, :], byb[h]))
            for k in range(2):
                ops.append((wv[:, 10 + k, :], mixC[:, k, :], b_mixC[k]))
            for n, (lh, rh, br) in enumerate(ops):
                sc.op("pe", lambda e, lh=lh, rh=rh, n=n, ps=ps: e.matmul(ps[:, :], lh, rh, start=(n == 0), stop=(n == 11)),
                      reads=[bw, br], writes=[bps], inc=(n == 11))
            sc.op("dve", lambda e, c=c, ps=ps: e.tensor_tensor(out=xt[:, c, :], in0=ps[:, :], in1=xt[:, c, :], op=ALU.add),
                  reads=[bps, bx[c]], writes=[bx[c]])

    def loop_b(l):
        for i in range(NKP):
            sc.op("dve", lambda e, i=i: e.memset(kps[i][64:128, :], 0.0), writes=[b_kp[i]])
            sc.op("dve", lambda e, i=i: e.memset(kps[i][64:67, :], 1.0), writes=[b_kp[i]])
            sc.op("dve", lambda e, i=i: e.memset(vps[i][:, :, :], 1.0), writes=[b_vp[i]])
        for i in range(2):
            sc.op("dve", lambda e, i=i: e.memset(qaugs[i][64:128, :, :], 0.0), writes=[b_qaugs[i]])
        sc.dma("sp", [(mk[:, :, :], mkd)], writes=[b_mk])
        xtiles = {}
        xtiles[0] = load_x(xs, [b_xs[0]], 0)
        for s in prologue_stages(l, 0):
            s()

        def finish_slot(i):
            xt, bx = xtiles.pop(i)
            w_out(l, i, xt, bx)
            sc.dma("pool", [(xview(xs, i), xt[:, :, :])], reads=bx, writes=[b_xs[i]])

        def slot_hooks(i):
            hooks = {}
            stages = prologue_stages(l, i + 1) if i + 1 < NT else None

            def h0():
                if i > 0:
                    finish_slot(i - 1)
                if i + 1 < NT:
                    stages[0]()
            hooks[0] = h0
            if stages is not None:
                def h2():
                    xtiles[i + 1] = load_x(xs, [b_xs[i + 1]], i + 1)
                hooks.update({1: stages[1], 2: h2, 3: stages[2], 4: stages[3], 6: stages[4], "end": stages[5]})
            return hooks

        attention_layer(l, slot_hooks)
        finish_slot(NT - 1)

    def final_norm(xt, bx):
        ps, bps = psr.get()
        for c in range(KC):
            sc.op("act", lambda e, c=c: e.activation(out=sq[:, c, :], in_=xt[:, c, :], func=AF.Square, scale=1.0 / 32.0),
                  reads=[bx[c]], writes=[b_sq[c]])
        for c in range(KC):
            sc.op("pe", lambda e, c=c: e.matmul(ps[:, :], onesb[:, :], sq[:, c, :], start=(c == 0), stop=(c == KC - 1)),
                  reads=[b_sq[c], b_ones], writes=[bps], inc=(c == KC - 1))
        rsqrt_eps(rr[:, :], ps[:, :], [bps], [b_rr])
        for c in range(KC):
            sc.op("dve", lambda e, c=c: e.scalar_tensor_tensor(out=xt[:, c, :], in0=xt[:, c, :], scalar=pp[:, PP_FIN + c:PP_FIN + c + 1],
                                                                in1=rr[:, :], op0=ALU.mult, op1=ALU.mult),
                  reads=[bx[c], b_rr, b_pp], writes=[bx[c]])

    def loop_c():
        l = NL - 1
        cur = {"nxt": load_x(xs, [b_xs[0]], 0)}
        for t in range(NT):
            xt, bx = cur["nxt"]

            def mid(t=t):
                if t + 1 < NT:
                    cur["nxt"] = load_x(xs, [b_xs[t + 1]], t + 1)
            ffn(l, 2, xt, bx, mid=mid)
            final_norm(xt, bx)
            sc.dma("pool", [(xview(yout, t), xt[:, :, :])], reads=bx, writes=[b_y])

    def dump():
        sc.barrier()
        b_dbg = B("dbg")
        pairs = [(yout[c], xs[c]) for c in range(KC)]
        sc.dma("pool", pairs, writes=[b_dbg])
        sc.barrier()

    stage = 0
    done = False
    for l in range(NL):
        loop_a(l)
        sc.barrier()
        stage += 1
        if stop == stage:
            dump(); done = True; break
        loop_b(l)
        sc.barrier()
        stage += 1
        if stop == stage:
            dump(); done = True; break
    if not done:
        loop_c()
        sc.barrier()

    with es:
        sems = [es.enter_context(nc.semaphore(f"s{i}")) for i in range(sc.nsem)]
        es.enter_context(nc.allow_non_contiguous_dma(reason="tiny per-row scalars"))
        block = es.enter_context(nc.Block())

        def emit(eng_name):
            def body(e):
                for rec in sc.eng[eng_name].ops:
                    if rec[0] == "wait":
                        e.wait_ge(sems[rec[1]], rec[2])
                    else:
                        ins = rec[1](e)
                        if rec[2] is not None:
                            ins.then_inc(sems[rec[2]], rec[3])
            return body

        block.sync(emit("sp"))
        block.tensor(emit("pe"))
        block.scalar(emit("act"))
        block.vector(emit("dve"))
        block.gpsimd(emit("pool"))
    return nc


_CACHE = {}


def kernel(**inputs):
    inp = {k: np.asarray(v) for k, v in inputs.items()}
    x = inp["x"].astype(np.float32, copy=False)
    if "nc" not in _CACHE:
        _CACHE["nc"] = build_program()
    nc = _CACHE["nc"]
    Wf = pack_weights(inp)
    pps = [pack_params(inp, r) for r in range(2)]
    css = [make_consts(r) for r in range(2)]
    mks = [make_masks(r) for r in range(2)]
    in_maps = []
    for c in range(NCORES):
        b, r = c // 2, c % 2
        xl = x[b].reshape(NB, T, D)[r::2].reshape(SL, D)
        xT = np.ascontiguousarray(xl.T).reshape(KC, 128, SL)
        in_maps.append({"xin": xT, "wf": Wf, "pp": pps[r], "cs": css[r], "mk": mks[r]})
    res = run_bass_kernel_spmd(nc, in_maps, core_ids=list(range(NCORES)))
    out = np.empty((NCORES // 2, S, D), np.float32)
    for c in range(NCORES):
        b, r = c // 2, c % 2
        yl = res.results[c]["y"].reshape(D, SL).T.reshape(NT, T, D)
        out[b].reshape(NB, T, D)[r::2] = yl
    return out
```

```python
import contextlib
import numpy as np
import concourse.bass as bass
import concourse.mybir as mybir
from concourse.bass_utils import run_bass_kernel_spmd

F32 = mybir.dt.float32
BF16 = mybir.dt.bfloat16
AF = mybir.ActivationFunctionType
ALU = mybir.AluOpType

NCORES = 8
S = 8192
SL = 4096
NB = 16
RX = 1544
GROUPS = [[0, 1], [2, 3], [4, 5], [6, 7]]
D = 1024
DFF = 2816
NL = 2
T = 512
NT = SL // T
KC = 8
FC = 22
H = 8
EPS = 1e-6
NEG = -30000.0
CVT = 4096
UPAD = 32


def wlayout():
    items = []
    for l in range(NL):
        for ffn in (1, 2):
            if ffn == 2:
                continue
            for f in range(FC):
                items.append((f"{l}.{ffn}.gu{f}", 2048))
            for c in range(KC):
                items.append((f"{l}.{ffn}.dn{c}", FC * 128))
        for c in range(2):
            items.append((f"{l}.pool{c}", 1024))
        for j in range(4):
            items.append((f"{l}.q{j}", 1024))
        for j in range(4):
            items.append((f"{l}.k{j}", 1024))
        for j in range(2):
            items.append((f"{l}.v{j}", 2048))
        items.append((f"{l}.zf", 64))
        for c in range(2):
            items.append((f"{l}.a{c}", 1024))
        for c in range(2):
            items.append((f"{l}.g{c}", 1024))
        items.append((f"{l}.pbd", 256))
        for c in range(KC):
            items.append((f"{l}.out{c}", 12 * 128))
        for f in range(FC):
            items.append((f"{l}.2.gu{f}", 2048))
        for c in range(KC):
            items.append((f"{l}.2.dn{c}", FC * 128))
    off = {}
    o = 0
    for n, w in items:
        off[n] = (o, w)
        o += w
    return items, off, o


WITEMS, WOFF, WTOT = wlayout()

PP_F1, PP_MX, PP_F2, PP_PS, PP_CB, PP_LG, PP_LB, PP_CW, PP_FB = 0, 8, 16, 24, 26, 28, 30, 32, 94
PPL = 96
PP_FIN = NL * PPL
PP_M0 = PP_FIN + 8
PP_M1 = PP_FIN + 9
PP_M0N = PP_FIN + 10
PPW = PP_FIN + 11
CS_ID, CS_TRI, CS_INVW, CS_T0 = 0, 128, 256, 258
CSW = 258 + 32


def _colchunk(W, c0, n):
    return np.ascontiguousarray(W[:, c0:c0 + n].reshape(KC, 128, n).transpose(1, 0, 2)).reshape(128, KC * n)


def pack_weights(inp):
    Wf = np.zeros((128, WTOT), np.float32)

    def put(name, a):
        o, w = WOFF[name]
        assert a.shape == (128, w), (name, a.shape, w)
        Wf[:, o:o + w] = a

    for l in range(NL):
        for ffn in (1, 2):
            wg = inp[f"ffn{ffn}_w_gate"][l]
            wu = inp[f"ffn{ffn}_w_up"][l]
            wd = inp[f"ffn{ffn}_w_down"][l]
            for f in range(FC):
                put(f"{l}.{ffn}.gu{f}", np.concatenate([_colchunk(wg, f * 128, 128), _colchunk(wu, f * 128, 128)], axis=1))
            for c in range(KC):
                a = wd[:, c * 128:(c + 1) * 128].reshape(FC, 128, 128).transpose(1, 0, 2).reshape(128, FC * 128)
                put(f"{l}.{ffn}.dn{c}", a)
        wi = inp["w_in"][l]
        for c in range(2):
            put(f"{l}.pool{c}", _colchunk(wi, c * 128, 128))
        for j in range(4):
            put(f"{l}.q{j}", _colchunk(wi, 256 + j * 128, 128))
            put(f"{l}.k{j}", _colchunk(wi, 768 + j * 128, 128))
        for j in range(2):
            put(f"{l}.v{j}", _colchunk(wi, 1280 + j * 256, 256))
        put(f"{l}.zf", _colchunk(wi, 1792, 8))
        for c in range(2):
            put(f"{l}.a{c}", _colchunk(wi, 1800 + c * 128, 128))
            put(f"{l}.g{c}", _colchunk(wi, 2056 + c * 128, 128))
        pw = inp["pool_w"][l]
        bd = np.zeros((128, 2, 128), np.float32)
        for c in range(2):
            bd[0:64, c, 0:64] = pw[2 * c]
            bd[64:128, c, 64:128] = pw[2 * c + 1]
        put(f"{l}.pbd", bd.reshape(128, 256))
        wo = inp["w_out"][l]
        for c in range(KC):
            a = np.zeros((128, 12, 128), np.float32)
            for k in range(2):
                a[:, k, :] = wo[k * 128:(k + 1) * 128, c * 128:(c + 1) * 128]
                a[:, 10 + k, :] = wo[768 + k * 128:768 + (k + 1) * 128, c * 128:(c + 1) * 128]
            for h in range(H):
                a[0:64, 2 + h, :] = wo[256 + h * 64:256 + (h + 1) * 64, c * 128:(c + 1) * 128]
            put(f"{l}.out{c}", a.reshape(128, 12 * 128))
    return Wf


def pack_params(inp, rank):
    pp = np.zeros((128, PPW), np.float32)
    pp[:, PP_M0] = 1.0 if rank == 0 else 0.0
    pp[:, PP_M1] = 1.0 if rank == 1 else 0.0
    pp[:, PP_M0N] = NEG if rank == 0 else 0.0

    def colv(v, n):
        return np.ascontiguousarray(np.asarray(v).reshape(n, 128).T)

    for l in range(NL):
        b = l * PPL
        pp[:, b + PP_F1:b + PP_F1 + 8] = colv(inp["ffn1_norm"][l], 8)
        pp[:, b + PP_MX:b + PP_MX + 8] = colv(inp["mix_norm"][l], 8)
        pp[:, b + PP_F2:b + PP_F2 + 8] = colv(inp["ffn2_norm"][l], 8)
        pp[:, b + PP_PS:b + PP_PS + 2] = colv(inp["pool_scale"][l], 2)
        pp[:, b + PP_CB:b + PP_CB + 2] = colv(inp["conv_b"][l], 2)
        pp[:, b + PP_LG:b + PP_LG + 2] = colv(inp["conv_ln_g"][l], 2)
        pp[:, b + PP_LB:b + PP_LB + 2] = colv(inp["conv_ln_b"][l], 2)
        cw = np.asarray(inp["conv_w"][l])
        pp[:, b + PP_CW:b + PP_CW + 62] = cw.T.reshape(2, 128, 31).transpose(1, 0, 2).reshape(128, 62)
        pp[0:8, b + PP_FB] = np.asarray(inp["forget_bias"][l])
    pp[:, PP_FIN:PP_FIN + 8] = colv(inp["final_norm"], 8)
    return pp


def make_consts(rank):
    cs = np.zeros((128, CSW), np.float32)
    cs[:, CS_ID:CS_ID + 128] = np.eye(128, dtype=np.float32)
    s = np.arange(128)[:, None]
    t = np.arange(128)[None, :]
    cs[:, CS_TRI:CS_TRI + 128] = np.where(s > t, NEG, 0.0)
    wins = (2, 4, 8, 16)
    for c in range(2):
        for half in range(2):
            w = wins[2 * c + half]
            cs[half * 64:(half + 1) * 64, CS_INVW + c] = 1.0 / w
            for tt in range(16):
                cs[half * 64:(half + 1) * 64, CS_T0 + c * 16 + tt] = 1.0 / (min(tt + 1, w) if rank == 0 else w)
    return cs


def make_masks(rank):
    import ml_dtypes
    s = np.arange(128)[:, None]
    t = np.arange(512)[None, :]
    mk = np.zeros((128, 8, 512), np.float32)
    for j in range(4):
        tri = np.where((j * 128 + s) > t, NEG, 0.0)
        if rank == 0:
            mk[:, j, :] = tri
            mk[:, 4 + j, :] = NEG
        else:
            mk[:, j, :] = 0.0
            mk[:, 4 + j, :] = tri
    return mk.astype(ml_dtypes.bfloat16)


class Buf:
    __slots__ = ("name", "lw", "rd", "dsem", "dcnt")

    def __init__(self, name):
        self.name = name
        self.lw = None
        self.rd = {}
        self.dsem = None
        self.dcnt = 0


class Eng:
    def __init__(self, name, sem):
        self.name = name
        self.sem = sem
        self.tick = 0
        self.ops = []
        self.seen = {}


class Sched:
    def __init__(self):
        self.nsem = 0
        self.eng = {}
        for n in ("pe", "act", "dve", "pool", "sp"):
            self.eng[n] = Eng(n, self.newsem())
        self.bufs = []

    def newsem(self):
        self.nsem += 1
        return self.nsem - 1

    def buf(self, name):
        b = Buf(name)
        self.bufs.append(b)
        return b

    def _wait(self, eng, k, v):
        if eng.seen.get(k, 0) < v:
            eng.ops.append(("wait", k, v))
            eng.seen[k] = v

    def _deps(self, eng, reads, writes):
        w = {}

        def add(k, v):
            if v > w.get(k, 0):
                w[k] = v

        for b in reads:
            if b.lw:
                add(*b.lw)
        for b in writes:
            if b.lw:
                add(*b.lw)
            for k, v in b.rd.items():
                if k != eng.sem:
                    add(k, v)
        if eng.name == "pe":
            w.pop(eng.sem, None)
        for k, v in w.items():
            self._wait(eng, k, v)

    def op(self, en, fn, reads=(), writes=(), inc=True):
        eng = self.eng[en]
        self._deps(eng, reads, writes)
        if inc:
            eng.tick += 1
            eng.ops.append(("op", fn, eng.sem, 1))
            tick = eng.tick
        else:
            eng.ops.append(("op", fn, None, 0))
            tick = eng.tick + 1
        for b in reads:
            b.rd[eng.sem] = tick
        for b in writes:
            b.lw = (eng.sem, tick)
            b.rd = {}

    def dma(self, qn, pairs, reads=(), writes=()):
        q = self.eng[qn]
        self._deps(q, reads, writes)
        prim = writes[0] if writes else reads[0]
        if prim.dsem is None:
            prim.dsem = self.newsem()
        for (o, i) in pairs:
            prim.dcnt += 16
            q.ops.append(("op", (lambda e, o=o, i=i: e.dma_start(out=o, in_=i)), prim.dsem, 16))
        for b in reads:
            b.rd[prim.dsem] = prim.dcnt
        for b in writes:
            b.lw = (prim.dsem, prim.dcnt)
            b.rd = {}

    def collective(self, src_ap, dst_ap, bsrc, bdst):
        q = self.eng["pool"]
        self._deps(q, [bsrc], [bdst])
        if bdst.dsem is None:
            bdst.dsem = self.newsem()
        bdst.dcnt += 1
        q.ops.append(("op", (lambda e: e.collective_compute("AllGather", ALU.bypass, replica_groups=GROUPS,
                                                             ins=[src_ap], outs=[dst_ap])), bdst.dsem, 1))
        bsrc.rd[bdst.dsem] = bdst.dcnt
        bdst.lw = (bdst.dsem, bdst.dcnt)
        bdst.rd = {}

    def barrier(self):
        for e in self.eng.values():
            for e2 in self.eng.values():
                if e2.tick > 0:
                    self._wait(e, e2.sem, e2.tick)
            for b in self.bufs:
                if b.dsem is not None and b.dcnt > 0:
                    self._wait(e, b.dsem, b.dcnt)


def build_program(stop=None, debug=False):
    nc = bass.Bass("TRN2", target_bir_lowering=False)
    sc = Sched()
    es = contextlib.ExitStack()

    xin = nc.dram_tensor("xin", [KC, 128, SL], F32, kind="ExternalInput").ap()
    mkd = nc.dram_tensor("mk", [128, 8, T], BF16, kind="ExternalInput").ap()
    wf = nc.dram_tensor("wf", [128, WTOT], F32, kind="ExternalInput").ap()
    ppd = nc.dram_tensor("pp", [128, PPW], F32, kind="ExternalInput").ap()
    csd = nc.dram_tensor("cs", [128, CSW], F32, kind="ExternalInput").ap()
    yout = nc.dram_tensor("y", [KC, 128, SL], F32, kind="ExternalOutput").ap()
    wb = nc.dram_tensor("wb", [128, WTOT], BF16).ap()
    xs = nc.dram_tensor("xs", [KC, 128, SL], F32).ap()
    qs = nc.dram_tensor("qs", [H, 64, SL], BF16).ap()
    exs_t = [nc.dram_tensor(f"exs{i}", [RX, T], BF16) for i in range(NT)]
    exd_t = [nc.dram_tensor(f"exd{i}", [2 * RX, T], BF16) for i in range(NT)]
    exs = [t_.ap() for t_ in exs_t]
    exd = [t_.ap() for t_ in exd_t]
    lsrc_t = nc.dram_tensor("lsrc", [NT * 8, T], F32)
    ldst_t = nc.dram_tensor("ldst", [2 * NT * 8, T], F32)
    lsrc, ldst = lsrc_t.ap(), ldst_t.ap()

    def ex_k(ap, h):
        return ap[h * 64:(h + 1) * 64, :]

    def ex_v(ap):
        return ap[512:1032, :].rearrange("r c -> (r c)").rearrange("(h p k d) -> h p k d", h=H, p=128, k=4)

    def ex_u(ap, c):
        return ap[1032 + c * 128:1032 + (c + 1) * 128, :]

    def ex_up(ap, c):
        return ap[1288 + c * 128:1288 + (c + 1) * 128, :]

    def exd_rank(j, q):
        return exd[j][q * RX:(q + 1) * RX, :]
    gdrow = nc.dram_tensor("gdrow", [8, S + 32], F32).ap()
    gdtok = nc.dram_tensor("gdtok", [S + 32, 8], F32).ap()
    xqd = nc.dram_tensor("xqd", [3, H, T], BF16).ap()
    rdd = nc.dram_tensor("rdd", [2, T], F32).ap()

    def xview(ap, t):
        return ap.rearrange("c p t -> p c t")[:, :, t * T:(t + 1) * T]

    def sb(name, shape, dt):
        return es.enter_context(nc.sbuf_tensor(name, shape, dt))

    pp = sb("pp_sb", [128, PPW], F32)
    cs = sb("cs_sb", [128, CSW], F32)
    trib = sb("trib", [128, 128], BF16)
    identb = sb("identb", [128, 128], BF16)
    onesb = sb("onesb", [128, 128], BF16)
    onesf = sb("onesf", [128, 128], F32)
    ones8 = sb("ones8", [8, T], F32)
    epsc = sb("epsc", [128, 1], F32)
    zt = sb("zt", [128, 64], F32)
    ztb = sb("ztb", [128, 64], BF16)
    nfb = sb("nfb", [8, 2], F32)
    convdiag = sb("convdiag", [128, 2, 31, 128], BF16)
    poolbd = sb("poolbd", [128, 2, 128], BF16)
    gkall = sb("gkall", [128, S // 128, 8], F32)
    NX = 2
    xring = [sb(f"xr{i}", [128, KC, T], F32) for i in range(NX)]
    NW = 4
    wring = [sb(f"wr{i}", [128, FC * 128], BF16) for i in range(NW)]
    ARENA16 = 63 * 1024
    arena = sb("arena", [128, ARENA16], BF16)
    cv = {"off": 0}

    def carve(shape, dt):
        n = int(np.prod(shape[1:]))
        n16 = n * (2 if dt == F32 else 1)
        o = cv["off"]
        cv["off"] = o + (n16 + 15) // 16 * 16
        assert cv["off"] <= ARENA16, (cv["off"], ARENA16)
        v = arena[0:shape[0], o:o + n16]
        if dt == F32:
            v = v.bitcast(F32)
        if len(shape) == 3:
            v = v.rearrange("p (a b) -> p a b", a=shape[1])
        elif len(shape) == 4:
            v = v.rearrange("p (a b c) -> p a b c", a=shape[1], b=shape[2])
        return v

    cv["off"] = 0
    hT = carve([128, KC, T], BF16)
    G = carve([128, FC, T], BF16)
    sq = carve([128, KC, T], BF16)
    rr = carve([128, T], F32)
    sgs = [carve([128, T], BF16) for i in range(2)]
    NSTG = 4
    stg = [carve([128, T], BF16) for i in range(NSTG)]
    vst = carve([128, H, 4, 65], BF16)
    sgm = carve([128, T], F32)
    ex8 = carve([8, T], F32)
    lt8 = carve([8, T], F32)
    gts = [carve([8, T + 1], F32) for i in range(2)]
    cin = [carve([128, CVT], F32) for i in range(2)]
    cout = [carve([128, CVT], BF16) for i in range(2)]
    cv["off"] = 0
    NKP = 3
    kps = [carve([128, 2048], BF16) for i in range(NKP)]
    vps = [carve([128, 16, 128], BF16) for i in range(NKP)]
    qaugs = [carve([128, H, T], BF16) for i in range(2)]
    NP = 3
    pts = [carve([128, T], BF16) for i in range(NP)]
    grow = carve([8, T + 1], F32)
    growB = carve([8, T + 1], F32)
    grefbc = carve([128, 8], F32)
    grefB = carve([128, 8], F32)
    mk = carve([128, 8, T], BF16)
    haloA = carve([128, 2, 32], BF16)
    haloB = carve([128, 2, 32], BF16)
    phaloA = carve([128, 2, 16], BF16)
    phaloB = carve([128, 2, 16], BF16)
    biasTs = [carve([128, S // 128, 8], F32) for i in range(2)]
    bowns = [carve([128, 4, 8], F32) for i in range(2)]
    _o = cv["off"]
    r1ts = [carve([128, T], F32) for i in range(2)]
    cv["off"] = _o
    xq = carve([8, T], F32)
    xr1 = carve([8, T], F32)
    x32 = carve([8, T], F32)
    xhi = carve([8, T], BF16)
    xmid = carve([8, T], BF16)
    xlo = carve([8, T], BF16)
    mixPs = [carve([128, 2, T], BF16) for i in range(2)]
    mixCs = [carve([128, 2, T], BF16) for i in range(2)]
    ybTs = [carve([64, H, T], BF16) for i in range(2)]
    uh = carve([128, 2, T + 32], BF16)
    uph = carve([128, 2, T + 16], BF16)
    sA = carve([128, 2, T + 16], F32)
    sB = carve([128, 2, T + 16], F32)
    pooledb = carve([128, 2, T], BF16)
    t0tmp = carve([128, 2, 16], F32)
    yconv = carve([128, 2, T], F32)
    ysq = sA[:, :, 0:T]
    mstat = carve([128, T], F32)
    msq = carve([128, T], F32)
    rstd = carve([128, T], F32)
    tmpn = carve([128, T], F32)
    tmpn2 = carve([128, T], F32)
    bcss = [carve([64, T], F32) for i in range(2)]

    psum = [es.enter_context(nc.psum_tensor(f"ps{i}", [128, T], F32)) for i in range(8)]

    B = sc.buf
    b_pp, b_cs, b_trib, b_identb, b_ones, b_zt, b_nfb = B("pp"), B("cs"), B("trib"), B("identb"), B("ones"), B("zt"), B("nfb")
    b_convdiag, b_poolbd, b_gkall = B("convdiag"), B("poolbd"), B("gkall")
    b_x = [[B(f"x{i}.{c}") for c in range(KC)] for i in range(NX)]
    b_w = [B(f"w{i}") for i in range(NW)]
    b_hT = [B(f"hT{c}") for c in range(KC)]
    b_G = [B(f"G{f}") for f in range(FC)]
    b_sq = [B(f"sq{c}") for c in range(KC)]
    b_rr = B("rr")
    b_sg = [B(f"sg{i}") for i in range(2)]
    b_stg = [B(f"stg{i}") for i in range(NSTG)]
    b_vst, b_sgm, b_ex8, b_lt8 = B("vst"), B("sgm"), B("ex8"), B("lt8")
    b_gt = [B("gt0"), B("gt1")]
    b_kp = [B(f"kp{i}") for i in range(NKP)]
    b_vp = [B(f"vp{i}") for i in range(NKP)]
    b_qaugs, b_grow, b_grefbc, b_biasTs = [B("qaug0"), B("qaug1")], B("grow"), B("grefbc"), [B("biasT0"), B("biasT1")]
    b_pt = [B(f"pt{i}") for i in range(NP)]
    b_xq, b_xr1, b_x32, b_xhi, b_xmid, b_xlo = B("xq"), B("xr1"), B("x32"), B("xhi"), B("xmid"), B("xlo")
    b_mixPs = [[B(f"mixP{i}{c}") for c in range(2)] for i in range(2)]
    b_mixCs = [[B(f"mixC{i}{c}") for c in range(2)] for i in range(2)]
    b_ybTs = [[B(f"ybT{i}{h}") for h in range(H)] for i in range(2)]
    b_uh, b_uph, b_sA, b_sB, b_pooledb, b_t0tmp = B("uh"), B("uph"), B("sA"), B("sB"), B("pooledb"), B("t0tmp")
    b_yconv, b_ysq_unused, b_mstat, b_msq, b_rstd, b_tmpn, b_r1t, b_bcs = (B("yconv"), B("ysq"), B("mstat"), B("msq"),
                                                                    B("rstd"), B("tmpn"), [B("r1t0"), B("r1t1")], [B("bcs0"), B("bcs1")])
    b_tmpn2 = B("tmpn2")
    b_growB, b_grefB, b_halo = B("growB"), B("grefB"), B("halo")
    b_bowns = [B("bown0"), B("bown1")]
    b_rdd = B("rdd")
    b_ysq = b_sA
    b_cin = [B("cin0"), B("cin1")]
    b_cout = [B("cout0"), B("cout1")]
    b_ps = [B(f"ps{i}") for i in range(8)]
    NBLK = (WTOT + CVT - 1) // CVT
    EARLY_COLS = WOFF["0.pbd"][0]
    NEARLY = (EARLY_COLS + CVT - 1) // CVT
    PER_TILE = (NBLK - NEARLY + NT - 1) // NT
    def blk_group(bi):
        return 0 if bi < NEARLY else 1 + (bi - NEARLY) // PER_TILE
    b_wbg = [B(f"wb{g}") for g in range(2 + (NBLK - NEARLY) // PER_TILE)]
    def wb_bufs(o, w):
        return sorted({blk_group(bi) for bi in range(o // CVT, (o + w - 1) // CVT + 1)})
    b_xs = [B(f"xs{t}") for t in range(NT)]
    b_qs = [B(f"qs{t}") for t in range(NT)]
    b_exs = [B(f"exs{t}") for t in range(NT)]
    b_exd = [B(f"exd{t}") for t in range(NT)]
    b_lsrc, b_ldst, b_gd, b_mk = B("lsrc"), B("ldst"), B("gd"), B("mk")
    b_pad = B("pad")
    b_y = B("y")
    b_xqd = B("xqd")

    class Rot:
        def __init__(self, idx):
            self.idx = idx
            self.i = 0
            self.held = set()

        def get(self, hold=False):
            while True:
                k = self.idx[self.i % len(self.idx)]
                self.i += 1
                if k not in self.held:
                    break
            if hold:
                self.held.add(k)
            return psum[k], b_ps[k]

        def release(self, bps):
            self.held.discard(b_ps.index(bps))

    psr = Rot([0, 1, 2, 3, 4, 5])
    pso = Rot([6, 7])
    rot = {"stg": 0, "alt": 0, "kp": 0, "pt": 0, "sg": 0}

    def alt_eng():
        rot["alt"] += 1
        return "act" if rot["alt"] % 2 else "dve"

    def rsqrt_eps(out, in_, reads, writes):
        sc.op("act", lambda e: e.activation(out=out, in_=in_, func=AF.Sqrt, bias=epsc[:, 0:1], scale=1.0), reads=list(reads) + [b_ones], writes=writes)
        sc.op("dve", lambda e: e.reciprocal(out=out, in_=out), reads=writes, writes=writes)

    def copy_op(en, out, in_, reads, writes):
        if en == "act":
            sc.op("act", lambda e: e.activation(out=out, in_=in_, func=AF.Copy), reads, writes)
        else:
            sc.op(en, lambda e: e.tensor_copy(out=out, in_=in_), reads, writes)

    wstate = {"issued": 0, "used": 0, "order": []}

    def wplan(name):
        wstate["order"].append(name)

    def wissue_upto(n):
        while wstate["issued"] < min(n, len(wstate["order"])):
            i = wstate["issued"]
            o, w = WOFF[wstate["order"][i]]
            slot = i % NW
            sc.dma("sp", [(wring[slot][:, 0:w], wb[:, o:o + w])], reads=[b_wbg[g] for g in wb_bufs(o, w)], writes=[b_w[slot]])
            wstate["issued"] += 1

    def wget(name):
        i = wstate["used"]
        assert wstate["order"][i] == name, (wstate["order"][i], name)
        wissue_upto(i + NW - 1)
        wstate["used"] += 1
        slot = i % NW
        return wring[slot], b_w[slot]

    def plan_ffn(l, ffn):
        for f in range(FC):
            wplan(f"{l}.{ffn}.gu{f}")
        for c in range(KC):
            wplan(f"{l}.{ffn}.dn{c}")

    def plan_mixin(l):
        wplan(f"{l}.zf")
        for j in range(4):
            wplan(f"{l}.q{j}")
        for j in range(4):
            wplan(f"{l}.k{j}")
        for j in range(2):
            wplan(f"{l}.v{j}")
        for c in range(2):
            wplan(f"{l}.pool{c}")
        for c in range(2):
            wplan(f"{l}.a{c}")
            wplan(f"{l}.g{c}")

    for l in range(NL):
        wplan(f"{l}.pbd")
        for t in range(NT):
            if l > 0:
                plan_ffn(l - 1, 2)
            plan_ffn(l, 1)
            plan_mixin(l)
        for t in range(NT):
            for c in range(KC):
                wplan(f"{l}.out{c}")
    for t in range(NT):
        plan_ffn(NL - 1, 2)

    sc.dma("sp", [(pp[:, :], ppd)], writes=[b_pp])
    sc.dma("sp", [(cs[:, :], csd)], writes=[b_cs])
    sc.op("dve", lambda e: e.memset(zt[:, :], 0.0), writes=[b_zt])
    sc.op("dve", lambda e: e.memset(ztb[:, :], 0.0), writes=[b_zt])
    sc.op("dve", lambda e: e.memset(onesb[:, :], 1.0), writes=[b_ones])
    sc.op("dve", lambda e: e.memset(onesf[:, :], 1.0), writes=[b_ones])
    sc.op("dve", lambda e: e.memset(ones8[:, :], 1.0), writes=[b_ones])
    sc.op("dve", lambda e: e.memset(epsc[:, :], EPS), writes=[b_ones])
    sc.op("dve", lambda e: e.tensor_copy(out=trib[:, :], in_=cs[:, CS_TRI:CS_TRI + 128]), reads=[b_cs], writes=[b_trib])
    sc.op("dve", lambda e: e.tensor_copy(out=identb[:, :], in_=cs[:, CS_ID:CS_ID + 128]), reads=[b_cs], writes=[b_identb])
    for l in range(NL):
        sc.op("dve", lambda e, l=l: e.tensor_scalar(out=nfb[:, l:l + 1], in0=pp[0:8, l * PPL + PP_FB:l * PPL + PP_FB + 1],
                                                     scalar1=-1.0, scalar2=None, op0=ALU.mult), reads=[b_pp], writes=[b_nfb])
    sc.dma("pool", [(gdrow[:, 0:1], zt[0:8, 0:1]), (gdtok[0:1, :], zt[0:1, 0:8])], reads=[b_zt], writes=[b_pad])

    cvt_engs = ["dve", "act"]

    cvt_state = {"next": 0, "loaded": 0}

    def cvt_load(i):
        c0 = i * CVT
        w = min(CVT, WTOT - c0)
        sc.dma("sp", [(cin[i % 2][:, 0:w], wf[:, c0:c0 + w])], writes=[b_cin[i % 2]])

    def cvt_conv(i):
        c0 = i * CVT
        w = min(CVT, WTOT - c0)
        k = i % 2
        copy_op(cvt_engs[i % 2], cout[k][:, 0:w], cin[k][:, 0:w], [b_cin[k]], [b_cout[k]])
        sc.dma("pool", [(wb[:, c0:c0 + w], cout[k][:, 0:w])], reads=[b_cout[k]], writes=[b_wbg[blk_group(i)]])

    cvt_limit = {"v": NBLK}

    def convert_some(n, limit=None):
        lim = cvt_limit["v"] if limit is None else limit
        for _ in range(n):
            i = cvt_state["next"]
            if i >= lim:
                return
            while cvt_state["loaded"] < min(i + 2, lim):
                cvt_load(cvt_state["loaded"])
                cvt_state["loaded"] += 1
            cvt_conv(i)
            cvt_state["next"] += 1

    convert_some(NEARLY, NEARLY)

    def rmsnorm_to_hT(xt, bx, gcol):
        ps, bps = psr.get()
        for c in range(KC):
            sc.op("act", lambda e, c=c: e.activation(out=sq[:, c, :], in_=xt[:, c, :], func=AF.Square, scale=1.0 / 32.0),
                  reads=[bx[c]], writes=[b_sq[c]])
        for c in range(KC):
            sc.op("pe", lambda e, c=c: e.matmul(ps[:, :], onesb[:, :], sq[:, c, :], start=(c == 0), stop=(c == KC - 1)),
                  reads=[b_sq[c], b_ones], writes=[bps], inc=(c == KC - 1))
        rsqrt_eps(rr[:, :], ps[:, :], [bps], [b_rr])
        for c in range(KC):
            sc.op("dve", lambda e, c=c: e.scalar_tensor_tensor(out=hT[:, c, :], in0=xt[:, c, :], scalar=pp[:, gcol + c:gcol + c + 1],
                                                                in1=rr[:, :], op0=ALU.mult, op1=ALU.mult),
                  reads=[bx[c], b_rr, b_pp], writes=[b_hT[c]])

    dbgflag = {"first": debug}

    cvt_budget = {"n": 0}

    def ffn(l, which, xt, bx, mid=None):
        gcol = l * PPL + (PP_F1 if which == 1 else PP_F2)
        rmsnorm_to_hT(xt, bx, gcol)
        first = False
        for f in range(FC):
            wt, bw = wget(f"{l}.{which}.gu{f}")
            wv = wt[:, 0:2048].rearrange("p (g k n) -> p g k n", g=2, k=KC)
            psg, bpg = psr.get()
            psu, bpu = psr.get()
            for k in range(KC):
                sc.op("pe", lambda e, k=k, psg=psg, wv=wv: e.matmul(psg[:, :], wv[:, 0, k, :], hT[:, k, :], start=(k == 0), stop=(k == KC - 1)),
                      reads=[bw, b_hT[k]], writes=[bpg], inc=(k == KC - 1))
            for k in range(KC):
                sc.op("pe", lambda e, k=k, psu=psu, wv=wv: e.matmul(psu[:, :], wv[:, 1, k, :], hT[:, k, :], start=(k == 0), stop=(k == KC - 1)),
                      reads=[bw, b_hT[k]], writes=[bpu], inc=(k == KC - 1))
            si = rot["sg"] % 2
            rot["sg"] += 1
            sc.op("act", lambda e, psg=psg, si=si: e.activation(out=sgs[si][:, :], in_=psg[:, :], func=AF.Silu),
                  reads=[bpg], writes=[b_sg[si]])
            sc.op("dve", lambda e, f=f, psu=psu, si=si: e.tensor_tensor(out=G[:, f, :], in0=psu[:, :], in1=sgs[si][:, :], op=ALU.mult),
                  reads=[bpu, b_sg[si]], writes=[b_G[f]])
            if which == 1 and f % 3 == 2 and cvt_budget["n"] > 0:
                convert_some(1)
                cvt_budget["n"] -= 1
        if mid is not None:
            mid()
        for c in range(KC):
            wt, bw = wget(f"{l}.{which}.dn{c}")
            wv = wt[:, 0:FC * 128].rearrange("p (k n) -> p k n", k=FC)
            psd, bpd = psr.get()
            for f in range(FC):
                sc.op("pe", lambda e, f=f, psd=psd, wv=wv: e.matmul(psd[:, :], wv[:, f, :], G[:, f, :], start=(f == 0), stop=(f == FC - 1)),
                      reads=[bw, b_G[f]], writes=[bpd], inc=(f == FC - 1))
            sc.op("dve", lambda e, c=c, psd=psd: e.scalar_tensor_tensor(out=xt[:, c, :], in0=psd[:, :], scalar=0.5, in1=xt[:, c, :],
                                                                        op0=ALU.mult, op1=ALU.add),
                  reads=[bpd, bx[c]], writes=[bx[c]])

    def next_stg():
        i = rot["stg"] % NSTG
        rot["stg"] += 1
        return stg[i], b_stg[i]

    def mixin(l, t, xt, bx, after_zf=None):
        base = l * PPL
        rmsnorm_to_hT(xt, bx, base + PP_MX)
        t0, t1 = t * T, (t + 1) * T
        wt, bw = wget(f"{l}.zf")
        wv = wt[:, 0:64].rearrange("p (k n) -> p k n", k=KC)
        ps, bps = psr.get()
        for k in range(KC):
            sc.op("pe", lambda e, k=k, ps=ps, wv=wv: e.matmul(ps[0:8, :], wv[:, k, :], hT[:, k, :], start=(k == 0), stop=(k == KC - 1)),
                  reads=[bw, b_hT[k]], writes=[bps], inc=(k == KC - 1))
        sc.op("act", lambda e, ps=ps: e.activation(out=ex8[:, :], in_=ps[0:8, :], func=AF.Exp, bias=nfb[:, l:l + 1], scale=-1.0),
              reads=[bps, b_nfb], writes=[b_ex8])
        sc.op("act", lambda e: e.activation(out=lt8[:, :], in_=ex8[:, :], func=AF.Ln, bias=1.0, scale=1.0), reads=[b_ex8], writes=[b_lt8])
        sc.dma("pool", [(lsrc[t * 8:(t + 1) * 8, :], lt8[:, :])], reads=[b_lt8], writes=[b_lsrc])
        if after_zf is not None:
            after_zf()
        for which in ("q", "k"):
            for j in range(4):
                wt, bw = wget(f"{l}.{which}{j}")
                wv = wt[:, 0:1024].rearrange("p (k n) -> p k n", k=KC)
                ps, bps = psr.get()
                for k in range(KC):
                    sc.op("pe", lambda e, k=k, ps=ps, wv=wv: e.matmul(ps[:, :], wv[:, k, :], hT[:, k, :], start=(k == 0), stop=(k == KC - 1)),
                          reads=[bw, b_hT[k]], writes=[bps], inc=(k == KC - 1))
                st, bst = next_stg()
                if which == "q":
                    sc.op("act", lambda e, ps=ps, st=st: e.activation(out=st[:, :], in_=ps[:, :], func=AF.Copy, scale=0.125),
                          reads=[bps], writes=[bst])
                    sc.dma("pool", [(qs[2 * j + hh, :, t0:t1], st[hh * 64:(hh + 1) * 64, :]) for hh in range(2)], reads=[bst], writes=[b_qs[t]])
                else:
                    sc.op("dve", lambda e, ps=ps, st=st: e.tensor_copy(out=st[:, :], in_=ps[:, :]), reads=[bps], writes=[bst])
                    sc.dma("pool", [(ex_k(exs[t], 2 * j + hh), st[hh * 64:(hh + 1) * 64, :]) for hh in range(2)], reads=[bst], writes=[b_exs[t]])
        for j in range(2):
            wt, bw = wget(f"{l}.v{j}")
            wv = wt[:, 0:2048].rearrange("p (k n) -> p k n", k=KC)
            for s in range(4):
                ps, bps = psr.get()
                for k in range(KC):
                    sc.op("pe", lambda e, k=k, ps=ps, wv=wv, s=s: e.matmul(ps[:, 0:256], hT[:, k, s * 128:(s + 1) * 128], wv[:, k, :],
                                                                           start=(k == 0), stop=(k == KC - 1)),
                          reads=[bw, b_hT[k]], writes=[bps], inc=(k == KC - 1))
                en = alt_eng()
                copy_op(en, vst[:, 4 * j:4 * j + 4, s, 0:64], ps[:, 0:256].rearrange("p (h d) -> p h d", h=4), [bps], [b_vst])
        sc.dma("pool", [(ex_v(exs[t]).rearrange("h p k d -> p h k d"), vst[:, :, :, :])], reads=[b_vst], writes=[b_exs[t]])
        for c in range(2):
            wt, bw = wget(f"{l}.pool{c}")
            wv = wt[:, 0:1024].rearrange("p (k n) -> p k n", k=KC)
            ps, bps = psr.get()
            for k in range(KC):
                sc.op("pe", lambda e, k=k, ps=ps, wv=wv: e.matmul(ps[:, :], wv[:, k, :], hT[:, k, :], start=(k == 0), stop=(k == KC - 1)),
                      reads=[bw, b_hT[k]], writes=[bps], inc=(k == KC - 1))
            st, bst = next_stg()
            copy_op("act", st[:, :], ps[:, :], [bps], [bst])
            sc.dma("pool", [(ex_up(exs[t], c), st[:, :])], reads=[bst], writes=[b_exs[t]])
        for c in range(2):
            wta, bwa = wget(f"{l}.a{c}")
            wva = wta[:, 0:1024].rearrange("p (k n) -> p k n", k=KC)
            psa, bpa = psr.get()
            for k in range(KC):
                sc.op("pe", lambda e, k=k, psa=psa, wva=wva: e.matmul(psa[:, :], wva[:, k, :], hT[:, k, :], start=(k == 0), stop=(k == KC - 1)),
                      reads=[bwa, b_hT[k]], writes=[bpa], inc=(k == KC - 1))
            wtg, bwg = wget(f"{l}.g{c}")
            wvg = wtg[:, 0:1024].rearrange("p (k n) -> p k n", k=KC)
            psg, bpg = psr.get()
            for k in range(KC):
                sc.op("pe", lambda e, k=k, psg=psg, wvg=wvg: e.matmul(psg[:, :], wvg[:, k, :], hT[:, k, :], start=(k == 0), stop=(k == KC - 1)),
                      reads=[bwg, b_hT[k]], writes=[bpg], inc=(k == KC - 1))
            sc.op("act", lambda e, psg=psg: e.activation(out=sgm[:, :], in_=psg[:, :], func=AF.Sigmoid), reads=[bpg], writes=[b_sgm])
            st, bst = next_stg()
            sc.op("dve", lambda e, psa=psa, st=st: e.tensor_tensor(out=st[:, :], in0=psa[:, :], in1=sgm[:, :], op=ALU.mult),
                  reads=[bpa, b_sgm], writes=[bst])
            sc.dma("pool", [(ex_u(exs[t], c), st[:, :])], reads=[bst], writes=[b_exs[t]])

    def load_x(src_ap, src_bufs, t):
        slot = load_x.n % NX
        load_x.n += 1
        sc.dma("sp", [(xring[slot][:, :, :], xview(src_ap, t))], reads=src_bufs, writes=b_x[slot])
        return xring[slot], b_x[slot]
    load_x.n = 0

    def loop_a(l):
        src = xin if l == 0 else xs
        sc.op("dve", lambda e: e.memset(vst[:, :, :, :], 1.0), writes=[b_vst])
        cvt_limit["v"] = NEARLY + 5 * NT if l == 0 else NBLK
        layer_setup_b(l)
        nxt = load_x(src, [] if l == 0 else [b_xs[0]], 0)
        for t in range(NT):
            xt, bx = nxt
            if t + 1 < NT:
                nxt = load_x(src, [] if l == 0 else [b_xs[t + 1]], t + 1)
            cvt_budget["n"] = 5 if l == 0 else 3
            if l > 0:
                ffn(l - 1, 2, xt, bx)
            ffn(l, 1, xt, bx)
            sc.dma("pool", [(xview(xs, t), xt[:, :, :])], reads=bx, writes=[b_xs[t]])
            mixin(l, t, xt, bx, after_zf=((lambda: sc.collective(lsrc_t.ap().opt(), ldst_t.ap().opt(), b_lsrc, b_ldst))
                                          if t == NT - 1 else None))
            convert_some(cvt_budget["n"])
            cvt_budget["n"] = 0
            sc.collective(exs_t[t].ap().opt(), exd_t[t].ap().opt(), b_exs[t], b_exd[t])
        lall = ldst.rearrange("(q j h) t -> h j q t", q=2, j=NT, h=8)
        for hh in range(2):
            sc.dma("sp", [(cin[hh][0:8, :].rearrange("h (j q t) -> h j q t", j=NT // 2, q=2)[:, :, q_, :],
                           lall[:, (NT // 2) * hh:(NT // 2) * (hh + 1), q_, :]) for q_ in range(2)],
                   reads=[b_ldst], writes=[b_cin[hh]])
        for g in range(NB):
            hh, bi = g // 8, g % 8
            seg = cin[hh][0:8, bi * T:(bi + 1) * T]
            if g == 0:
                sc.op("dve", lambda e, seg=seg: e.tensor_tensor_scan(out=seg, data0=ones8[:, :], data1=seg, initial=0.0, op0=ALU.mult, op1=ALU.add),
                      reads=[b_cin[hh], b_ones], writes=[b_cin[hh]])
            else:
                ph, pb = (g - 1) // 8, (g - 1) % 8
                carry = cin[ph][0:8, (pb + 1) * T - 1:(pb + 1) * T]
                sc.op("dve", lambda e, seg=seg, carry=carry: e.tensor_tensor_scan(out=seg, data0=ones8[:, :], data1=seg, initial=carry,
                                                                                op0=ALU.mult, op1=ALU.add),
                      reads=[b_cin[hh], b_cin[ph], b_ones], writes=[b_cin[hh]])
        sc.dma("pool", [(gdrow[:, 1 + hh * 4096:1 + (hh + 1) * 4096], cin[hh][0:8, :]) for hh in range(2)], reads=b_cin, writes=[b_gd])
        ps, bps = psr.get()
        for kt in range(S // 128):
            hh, off = kt // 32, (kt % 32) * 128
            sc.op("pe", lambda e, kt=kt, hh=hh, off=off: e.transpose(ps[:, kt * 8:(kt + 1) * 8], cin[hh][0:8, off:off + 128], cs[0:8, CS_ID:CS_ID + 8]),
                  reads=[b_cin[hh], b_cs], writes=[bps], inc=(kt == S // 128 - 1))
        sc.op("dve", lambda e: e.tensor_copy(out=gkall[:, :, :], in_=ps[:, :].rearrange("p (k h) -> p k h", h=8)), reads=[bps], writes=[b_gkall])
        sc.dma("pool", [(gdtok[1:1 + S, :].rearrange("(k p) h -> p k h", p=128), gkall[:, :, :])], reads=[b_gkall], writes=[b_gd])

    def layer_setup_b(l):
        base = l * PPL
        wt, bw = wget(f"{l}.pbd")
        sc.op("dve", lambda e: e.tensor_copy(out=poolbd[:, :, :], in_=wt[:, 0:256].rearrange("p (c n) -> p c n", c=2)),
              reads=[bw], writes=[b_poolbd])
        for c in range(2):
            for j in range(31):
                col = base + PP_CW + c * 31 + j
                sc.op("dve", lambda e, c=c, j=j, col=col: e.tensor_scalar(out=convdiag[:, c, j, :], in0=cs[:, CS_ID:CS_ID + 128],
                                                                            scalar1=pp[:, col:col + 1], scalar2=None, op0=ALU.mult),
                      reads=[b_cs, b_pp], writes=[b_convdiag])

    def prologue_stages(l, i):
        base = l * PPL
        par = i % 2
        t0, t1 = i * T, (i + 1) * T
        nkt = 8 * (i + 1)
        qaug, b_qaug, biasT, b_biasT = qaugs[par], b_qaugs[par], biasTs[par], b_biasTs[par]
        mixP, mixC, b_mixP, b_mixC = mixPs[par], mixCs[par], b_mixPs[par], b_mixCs[par]
        W = T + 16
        st = {}
        m0c = pp[:, PP_M0:PP_M0 + 1]
        m1c = pp[:, PP_M1:PP_M1 + 1]

        def s0():
            gA, gB = 2 * i, 2 * i + 1
            sc.dma("sp", [(qaug[0:64, :, :], qs.rearrange("h d t -> d h t")[:, :, t0:t1])], reads=[b_qs[i]], writes=[b_qaug])
            sc.dma("sp", [(grow[:, :], gdrow[:, gA * T:gA * T + T + 1])], reads=[b_gd, b_pad], writes=[b_grow])
            sc.dma("sp", [(growB[:, :], gdrow[:, gB * T:gB * T + T + 1])], reads=[b_gd, b_pad], writes=[b_growB])
            sc.dma("sp", [(grefbc[:, :], gdtok[gA * T:gA * T + 1, :].partition_broadcast(128).squeeze(1))], reads=[b_gd, b_pad], writes=[b_grefbc])
            sc.dma("sp", [(grefB[:, :], gdtok[gB * T:gB * T + 1, :].partition_broadcast(128).squeeze(1))], reads=[b_gd, b_pad], writes=[b_grefB])
            sc.op("dve", lambda e: e.tensor_scalar(out=growB[:, :], in0=growB[:, :], scalar1=m1c[0:8, :], scalar2=None, op0=ALU.mult),
                  reads=[b_growB, b_pp], writes=[b_growB])
            sc.op("dve", lambda e: e.scalar_tensor_tensor(out=grow[:, :], in0=grow[:, :], scalar=m0c[0:8, :], in1=growB[:, :], op0=ALU.mult, op1=ALU.add),
                  reads=[b_grow, b_growB, b_pp], writes=[b_grow])
            sc.op("dve", lambda e: e.tensor_scalar(out=grefB[:, :], in0=grefB[:, :], scalar1=m1c, scalar2=None, op0=ALU.mult),
                  reads=[b_grefB, b_pp], writes=[b_grefB])
            sc.op("dve", lambda e: e.scalar_tensor_tensor(out=grefbc[:, :], in0=grefbc[:, :], scalar=m0c, in1=grefB[:, :], op0=ALU.mult, op1=ALU.add),
                  reads=[b_grefbc, b_grefB, b_pp], writes=[b_grefbc])
            sc.dma("sp", [(uph[:, c, 16:T + 16], ex_up(exs[i], c)) for c in range(2)] + [(uh[:, c, 32:T + 32], ex_u(exs[i], c)) for c in range(2)],
                   reads=[b_exs[i]], writes=[b_uph, b_uh])
            if i > 0:
                prv = exd_rank(i - 1, 1)
                sc.dma("sp", [(haloA[:, c, :], ex_u(prv, c)[:, T - 32:T]) for c in range(2)] + [(phaloA[:, c, :], ex_up(prv, c)[:, T - 16:T]) for c in range(2)],
                       reads=[b_exd[i - 1]], writes=[b_halo])
            else:
                sc.op("dve", lambda e: e.memset(haloA[:, :, :], 0.0), writes=[b_halo])
                sc.op("dve", lambda e: e.memset(phaloA[:, :, :], 0.0), writes=[b_halo])
            cur0 = exd_rank(i, 0)
            sc.dma("sp", [(haloB[:, c, :], ex_u(cur0, c)[:, T - 32:T]) for c in range(2)] + [(phaloB[:, c, :], ex_up(cur0, c)[:, T - 16:T]) for c in range(2)],
                   reads=[b_exd[i]], writes=[b_halo])
            for (hA, hB, dst, w_) in ((haloA, haloB, uh, 32), (phaloA, phaloB, uph, 16)):
                sc.op("dve", lambda e, hB=hB: e.tensor_scalar(out=hB[:, :, :], in0=hB[:, :, :], scalar1=m1c, scalar2=None, op0=ALU.mult),
                      reads=[b_halo, b_pp], writes=[b_halo])
                sc.op("dve", lambda e, hA=hA, hB=hB, dst=dst, w_=w_: e.scalar_tensor_tensor(out=dst[:, :, 0:w_], in0=hA[:, :, :], scalar=m0c, in1=hB[:, :, :],
                                                                                          op0=ALU.mult, op1=ALU.add),
                      reads=[b_halo, b_pp], writes=[b_uph, b_uh])
            sc.op("dve", lambda e: e.tensor_scalar(out=xq[:, :], in0=grow[:, 1:T + 1], scalar1=grow[:, 0:1], scalar2=-1.0, op0=ALU.subtract, op1=ALU.mult),
                  reads=[b_grow], writes=[b_xq])
            sc.op("dve", lambda e: e.tensor_copy(out=xhi[:, :], in_=xq[:, :]), reads=[b_xq], writes=[b_xhi])
            sc.op("dve", lambda e: e.tensor_copy(out=x32[:, :], in_=xhi[:, :]), reads=[b_xhi], writes=[b_x32])
            sc.op("dve", lambda e: e.tensor_tensor(out=xr1[:, :], in0=xq[:, :], in1=x32[:, :], op=ALU.subtract), reads=[b_xq, b_x32], writes=[b_xr1])
            sc.op("dve", lambda e: e.tensor_copy(out=xmid[:, :], in_=xr1[:, :]), reads=[b_xr1], writes=[b_xmid])
            sc.op("dve", lambda e: e.tensor_copy(out=x32[:, :], in_=xmid[:, :]), reads=[b_xmid], writes=[b_x32])
            sc.op("dve", lambda e: e.tensor_tensor(out=xr1[:, :], in0=xr1[:, :], in1=x32[:, :], op=ALU.subtract), reads=[b_xr1, b_x32], writes=[b_xr1])
            sc.op("dve", lambda e: e.tensor_copy(out=xlo[:, :], in_=xr1[:, :]), reads=[b_xr1], writes=[b_xlo])
            sc.dma("pool", [(xqd[0, :, :], xhi[:, :]), (xqd[1, :, :], xmid[:, :]), (xqd[2, :, :], xlo[:, :])],
                   reads=[b_xhi, b_xmid, b_xlo], writes=[b_xqd])
            sc.dma("pool", [(qaug[64:67, :, :], xqd[:, :, :])], reads=[b_xqd], writes=[b_qaug])
            sc.op("dve", lambda e: e.tensor_tensor(out=biasT[:, 0:nkt, :], in0=gkall[:, 0:nkt, :],
                                                   in1=grefbc[:, :].unsqueeze(1).to_broadcast([128, nkt, 8]), op=ALU.subtract),
                  reads=[b_gkall, b_grefbc], writes=[b_biasT])
            bown, b_bown = bowns[par], b_bowns[par]
            sc.op("dve", lambda e: e.tensor_scalar(out=bown[:, :, :], in0=biasT[:, 8 * i + 4:8 * i + 8, :], scalar1=m1c, scalar2=None, op0=ALU.mult),
                  reads=[b_biasT, b_pp], writes=[b_bown])
            sc.op("dve", lambda e: e.scalar_tensor_tensor(out=bown[:, :, :], in0=biasT[:, 8 * i:8 * i + 4, :], scalar=m0c, in1=bown[:, :, :],
                                                          op0=ALU.mult, op1=ALU.add), reads=[b_biasT, b_bown, b_pp], writes=[b_bown])
            sc.op("dve", lambda e: e.tensor_scalar(out=biasT[:, 8 * i:8 * i + 4, :], in0=biasT[:, 8 * i:8 * i + 4, :],
                                                   scalar1=pp[:, PP_M0N:PP_M0N + 1], scalar2=None, op0=ALU.add),
                  reads=[b_biasT, b_bown, b_pp], writes=[b_biasT])
            sc.op("dve", lambda e: e.tensor_tensor(out=sA[:, :, 1:W], in0=uph[:, :, 1:W], in1=uph[:, :, 0:W - 1], op=ALU.add),
                  reads=[b_uph], writes=[b_sA])
            sc.op("dve", lambda e: e.tensor_tensor(out=sB[:, :, 3:W], in0=sA[:, :, 3:W], in1=sA[:, :, 1:W - 2], op=ALU.add),
                  reads=[b_sA], writes=[b_sB])
            sc.op("dve", lambda e: e.tensor_tensor(out=sA[:, 1, 7:W], in0=sB[:, 1, 7:W], in1=sB[:, 1, 3:W - 4], op=ALU.add),
                  reads=[b_sB], writes=[b_sA])
            sc.op("dve", lambda e: e.tensor_tensor(out=sB[64:128, 1, 15:W], in0=sA[64:128, 1, 15:W], in1=sA[64:128, 1, 7:W - 8], op=ALU.add),
                  reads=[b_sA], writes=[b_sB])
            srcs = [(sA, 0, 0), (sB, 0, 1), (sA, 1, 0), (sB, 1, 1)]
            for (stile, c, half) in srcs:
                p0, p1 = half * 64, (half + 1) * 64
                sc.op("dve", lambda e, stile=stile, c=c, p0=p0, p1=p1: e.scalar_tensor_tensor(
                    out=pooledb[p0:p1, c, :], in0=stile[p0:p1, c, 16:W], scalar=cs[p0:p1, CS_INVW + c:CS_INVW + c + 1],
                    in1=uph[p0:p1, c, 16:W], op0=ALU.mult, op1=ALU.subtract), reads=[b_sA, b_sB, b_uph, b_cs], writes=[b_pooledb])
            if i == 0:
                for (stile, c, half) in srcs:
                    p0, p1 = half * 64, (half + 1) * 64
                    sc.op("dve", lambda e, stile=stile, c=c, p0=p0, p1=p1: e.tensor_tensor(
                        out=t0tmp[p0:p1, c, :], in0=stile[p0:p1, c, 16:32], in1=cs[p0:p1, CS_T0 + c * 16:CS_T0 + (c + 1) * 16], op=ALU.mult),
                        reads=[b_sA, b_sB, b_cs], writes=[b_t0tmp])
                    sc.op("dve", lambda e, c=c, p0=p0, p1=p1: e.tensor_tensor(
                        out=pooledb[p0:p1, c, 0:16], in0=t0tmp[p0:p1, c, :], in1=uph[p0:p1, c, 16:32], op=ALU.subtract),
                        reads=[b_t0tmp, b_uph], writes=[b_pooledb])

        def s1():
            for c in range(2):
                ps, bps = psr.get()
                sc.op("pe", lambda e, c=c, ps=ps: e.matmul(ps[:, :], poolbd[:, c, :], pooledb[:, c, :], start=True, stop=True),
                      reads=[b_poolbd, b_pooledb], writes=[bps])
                col = base + PP_PS + c
                sc.op("act", lambda e, c=c, ps=ps, col=col: e.activation(out=mixP[:, c, :], in_=ps[:, :], func=AF.Identity, scale=pp[:, col:col + 1]),
                      reads=[bps, b_pp], writes=[b_mixP[c]])
            st["pc"] = []
            for c in range(2):
                ps, bps = psr.get(hold=True)
                for j in range(31):
                    sc.op("pe", lambda e, c=c, j=j, ps=ps: e.matmul(ps[:, :], convdiag[:, c, j, :], uh[:, c, 2 + j:2 + j + T], start=(j == 0), stop=(j == 30)),
                          reads=[b_convdiag, b_uh], writes=[bps], inc=(j == 30))
                st["pc"].append((ps, bps))

        def s2():
            for c in range(2):
                ps, bps = st["pc"][c]
                col = base + PP_CB + c
                sc.op("act", lambda e, c=c, ps=ps, col=col: e.activation(out=yconv[:, c, :], in_=ps[:, :], func=AF.Identity, bias=pp[:, col:col + 1], scale=1.0),
                      reads=[bps, b_pp], writes=[b_yconv])
                psr.release(bps)
            sc.op("dve", lambda e: e.tensor_tensor(out=ysq[:, :, :], in0=yconv[:, :, :], in1=yconv[:, :, :], op=ALU.mult), reads=[b_yconv], writes=[b_ysq])

        def s3():
            ps1, bp1 = psr.get(hold=True)
            ps2, bp2 = psr.get(hold=True)
            for c in range(2):
                sc.op("pe", lambda e, c=c: e.matmul(ps1[:, :], onesf[:, :], yconv[:, c, :], start=(c == 0), stop=(c == 1)),
                      reads=[b_ones, b_yconv], writes=[bp1], inc=(c == 1))
            for c in range(2):
                sc.op("pe", lambda e, c=c: e.matmul(ps2[:, :], onesf[:, :], ysq[:, c, :], start=(c == 0), stop=(c == 1)),
                      reads=[b_ones, b_ysq], writes=[bp2], inc=(c == 1))
            st["ln"] = (ps1, bp1, ps2, bp2)

        def s4():
            ps1, bp1, ps2, bp2 = st["ln"]
            sc.op("dve", lambda e: e.tensor_scalar(out=mstat[:, :], in0=ps1[:, :], scalar1=1.0 / 256.0, scalar2=None, op0=ALU.mult), reads=[bp1], writes=[b_mstat])
            sc.op("dve", lambda e: e.tensor_tensor(out=msq[:, :], in0=mstat[:, :], in1=mstat[:, :], op=ALU.mult), reads=[b_mstat], writes=[b_msq])
            sc.op("dve", lambda e: e.scalar_tensor_tensor(out=rstd[:, :], in0=ps2[:, :], scalar=1.0 / 256.0, in1=msq[:, :], op0=ALU.mult, op1=ALU.subtract),
                  reads=[bp2, b_msq], writes=[b_rstd])
            psr.release(bp1)
            psr.release(bp2)
            for c, (tt, bt) in enumerate(((tmpn, b_tmpn), (tmpn2, b_tmpn2))):
                sc.op("dve", lambda e, c=c, tt=tt: e.tensor_tensor(out=tt[:, :], in0=yconv[:, c, :], in1=mstat[:, :], op=ALU.subtract),
                      reads=[b_yconv, b_mstat], writes=[bt])

        def s5():
            rsqrt_eps(rstd[:, :], rstd[:, :], [b_rstd], [b_rstd])
            for c, (tt, bt) in enumerate(((tmpn, b_tmpn), (tmpn2, b_tmpn2))):
                sc.op("dve", lambda e, tt=tt: e.tensor_tensor(out=tt[:, :], in0=tt[:, :], in1=rstd[:, :], op=ALU.mult), reads=[bt, b_rstd], writes=[bt])
            for c, (tt, bt) in enumerate(((tmpn, b_tmpn), (tmpn2, b_tmpn2))):
                cg, cb = base + PP_LG + c, base + PP_LB + c
                sc.op("act", lambda e, c=c, cg=cg, cb=cb, tt=tt: e.activation(out=mixC[:, c, :], in_=tt[:, :], func=AF.Silu, bias=pp[:, cb:cb + 1], scale=pp[:, cg:cg + 1]),
                      reads=[bt, b_pp], writes=[b_mixC[c]])

        return [s0, s1, s2, s3, s4, s5]

    LOOKAHEAD = 2

    def attention_layer(l, slot_hooks):
        work_all = []
        for i in range(NT):
            nkg = 8 * i + 4
            npc = (nkg + 15) // 16
            for h in range(H):
                for pc in range(npc):
                    work_all.append((i, h, pc))
                work_all.append((i, h, "own"))
        loaded = {"n": 0}

        def issue_loads(upto):
            while loaded["n"] < min(upto, len(work_all)):
                n = loaded["n"]
                i, h, pc = work_all[n]
                slot = n % NKP
                if pc == "own":
                    sc.dma("sp", [(kps[slot][0:64, 0:T], ex_k(exs[i], h))], reads=[b_exs[i]], writes=[b_kp[slot]])
                    sc.dma("sp", [(vps[slot][:, 0:4, 0:65], ex_v(exs[i])[h])], reads=[b_exs[i]], writes=[b_vp[slot]])
                    loaded["n"] += 1
                    continue
                kpairs, vpairs, rds = [], [], []
                for jj in range(2):
                    j = 2 * pc + jj
                    if j > i:
                        continue
                    rds.append(b_exd[j])
                    if j < i:
                        both = exd[j].rearrange("(q r) c -> r q c", q=2)
                        kpairs.append((kps[slot][0:64, jj * 1024:(jj + 1) * 1024].rearrange("p (q c) -> p q c", q=2), both[h * 64:(h + 1) * 64, :, :]))
                        qs_ = (0, 1)
                    else:
                        kpairs.append((kps[slot][0:64, jj * 1024:jj * 1024 + T], ex_k(exd_rank(j, 0), h)))
                        qs_ = (0,)
                    for q_ in qs_:
                        vsrc = ex_v(exd_rank(j, q_))[h]
                        vpairs.append((vps[slot][:, jj * 8 + q_ * 4:jj * 8 + q_ * 4 + 4, 0:65], vsrc))
                sc.dma("sp", kpairs, reads=rds, writes=[b_kp[slot]])
                sc.dma("sp", vpairs, reads=rds, writes=[b_vp[slot]])
                loaded["n"] += 1

        def qk(h, j, c0, tri, kp, bkp, vp, bvp, qaug, b_qaug, bias_ap, b_bias, first, last):
            pss, bpss = psr.get()
            if tri:
                sc.op("pe", lambda e: e.matmul(pss[:, c0:c0 + 128], kp[:, j * 128:(j + 1) * 128], qaug[:, h, c0:c0 + 128], start=True, stop=False),
                      reads=[bkp, b_qaug], writes=[bpss], inc=False)
                sc.op("pe", lambda e: e.matmul(pss[:, c0:c0 + 128], identb[:, :], trib[:, :], start=False, stop=True),
                      reads=[b_identb, b_trib], writes=[bpss], inc=(c0 + 128 >= T))
                if c0 + 128 < T:
                    sc.op("pe", lambda e: e.matmul(pss[:, c0 + 128:T], kp[:, j * 128:(j + 1) * 128], qaug[:, h, c0 + 128:T], start=True, stop=True),
                          reads=[bkp, b_qaug], writes=[bpss])
            else:
                sc.op("pe", lambda e: e.matmul(pss[:, :], kp[:, j * 128:(j + 1) * 128], qaug[:, h, :], start=True, stop=True),
                      reads=[bkp, b_qaug], writes=[bpss])
            return (j, c0, pss, bpss, vp, bvp, bias_ap, b_bias, first, last)

        def exp_pv(item, pso_t, bpo):
            j, c0, pss, bpss, vp, bvp, bias_ap, b_bias, first, last = item
            pi = rot["pt"] % NP
            rot["pt"] += 1
            pt, bpt = pts[pi], b_pt[pi]
            sc.op("act", lambda e: e.activation(out=pt[:, c0:T], in_=pss[:, c0:T], func=AF.Exp, bias=bias_ap, scale=1.0),
                  reads=[bpss, b_bias], writes=[bpt])
            sc.op("pe", lambda e: e.matmul(pso_t[:, c0:T], vp[:, j, :], pt[:, c0:T], start=first, stop=last, skip_group_check=True),
                  reads=[bvp, bpt], writes=[bpo], inc=last)

        def fin_a(h, pso_t, bpo):
            k = h % 2
            sc.op("dve", lambda e: e.reciprocal(out=r1ts[k][64:65, :], in_=pso_t[64:65, :]), reads=[bpo], writes=[b_r1t[k]])
            sc.dma("pool", [(rdd[k:k + 1, :], r1ts[k][64:65, :])], reads=[b_r1t[k]], writes=[b_rdd])
            sc.dma("pool", [(bcss[k][:, :], rdd[k:k + 1, :].partition_broadcast(64).squeeze(1))], reads=[b_rdd], writes=[b_bcs[k]])

        def fin_b(i, h, pso_t, bpo):
            k = h % 2
            yb, byb = ybTs[i % 2], b_ybTs[i % 2]
            sc.op("dve", lambda e: e.tensor_tensor(out=yb[:, h, :], in0=pso_t[0:64, :], in1=bcss[k][:, :], op=ALU.mult),
                  reads=[bpo, b_bcs[k]], writes=[byb[h]])

        pending_fin = None
        n = 0
        for i in range(NT):
            par = i % 2
            qaug, b_qaug, biasT, b_biasT = qaugs[par], b_qaugs[par], biasTs[par], b_biasTs[par]
            bown, b_bown = bowns[par], b_bowns[par]
            nkg = 8 * i + 4
            npc = (nkg + 15) // 16
            hooks = slot_hooks(i)
            for h in range(H):
                cur_o = pso.get()
                pend = []
                cnt = 0
                for pc in list(range(npc)) + ["own"]:
                    issue_loads(n + 2)
                    slot = n % NKP
                    n += 1
                    if pc == "own":
                        tiles = [(j, j * 128, True, bown[:, j, h:h + 1], b_bown, False, j == 3) for j in range(4)]
                    else:
                        nk = min(16, nkg - pc * 16)
                        tiles = [(j, 0, False, biasT[:, pc * 16 + j, h:h + 1], b_biasT, (pc == 0 and j == 0), False) for j in range(nk)]
                    for (j, c0, tri, bias_ap, b_bias, first, last) in tiles:
                        pend.append(qk(h, j, c0, tri, kps[slot], b_kp[slot], vps[slot], b_vp[slot], qaug, b_qaug, bias_ap, b_bias, first, last))
                        if len(pend) > LOOKAHEAD:
                            exp_pv(pend.pop(0), cur_o[0], cur_o[1])
                        cnt += 1
                        if cnt == 6 and pending_fin is not None:
                            fin_b(*pending_fin)
                            pending_fin = None
                while pend:
                    exp_pv(pend.pop(0), cur_o[0], cur_o[1])
                if pending_fin is not None:
                    fin_b(*pending_fin)
                fin_a(h, cur_o[0], cur_o[1])
                pending_fin = (i, h, cur_o[0], cur_o[1])
                if h in hooks:
                    hooks[h]()
            if "end" in hooks:
                hooks["end"]()
        fin_b(*pending_fin)

    def w_out(l, i, xt, bx):
        par = i % 2
        mixP, mixC, b_mixP, b_mixC = mixPs[par], mixCs[par], b_mixPs[par], b_mixCs[par]
        yb, byb = ybTs[par], b_ybTs[par]
        for c in range(KC):
            wt, bw = wget(f"{l}.out{c}")
            wv = wt[:, 0:1536].rearrange("p (k n) -> p k n", k=12)
            ps, bps = psr.get()
            ops = []
            for k in range(2):
                ops.append((wv[:, k, :], mixP[:, k, :], b_mixP[k]))
            for h in range(H):
                ops.append((wv[0:64, 2 + h, :], yb[:, h, :], byb[h]))
            for k in range(2):
                ops.append((wv[:, 10 + k, :], mixC[:, k, :], b_mixC[k]))
            for n, (lh, rh, br) in enumerate(ops):
                sc.op("pe", lambda e, lh=lh, rh=rh, n=n, ps=ps: e.matmul(ps[:, :], lh, rh, start=(n == 0), stop=(n == 11)),
                      reads=[bw, br], writes=[bps], inc=(n == 11))
            sc.op("dve", lambda e, c=c, ps=ps: e.tensor_tensor(out=xt[:, c, :], in0=ps[:, :], in1=xt[:, c, :], op=ALU.add),
                  reads=[bps, bx[c]], writes=[bx[c]])

    def loop_b(l):
        for i in range(NKP):
            sc.op("dve", lambda e, i=i: e.memset(kps[i][64:128, :], 0.0), writes=[b_kp[i]])
            sc.op("dve", lambda e, i=i: e.memset(kps[i][64:67, :], 1.0), writes=[b_kp[i]])
            sc.op("dve", lambda e, i=i: e.memset(vps[i][:, :, :], 1.0), writes=[b_vp[i]])
        for i in range(2):
            sc.op("dve", lambda e, i=i: e.memset(qaugs[i][64:128, :, :], 0.0), writes=[b_qaugs[i]])
        sc.dma("sp", [(mk[:, :, :], mkd)], writes=[b_mk])
        xtiles = {}
        xtiles[0] = load_x(xs, [b_xs[0]], 0)
        for s in prologue_stages(l, 0):
            s()

        def finish_slot(i):
            xt, bx = xtiles.pop(i)
            w_out(l, i, xt, bx)
            sc.dma("pool", [(xview(xs, i), xt[:, :, :])], reads=bx, writes=[b_xs[i]])

        def slot_hooks(i):
            hooks = {}
            stages = prologue_stages(l, i + 1) if i + 1 < NT else None

            def h0():
                if i > 0:
                    finish_slot(i - 1)
                if i + 1 < NT:
                    stages[0]()
            hooks[0] = h0
            if stages is not None:
                def h2():
                    xtiles[i + 1] = load_x(xs, [b_xs[i + 1]], i + 1)
                hooks.update({1: stages[1], 2: h2, 3: stages[2], 4: stages[3], 6: stages[4], "end": stages[5]})
            return hooks

        attention_layer(l, slot_hooks)
        finish_slot(NT - 1)

    def final_norm(xt, bx):
        ps, bps = psr.get()
        for c in range(KC):
            sc.op("act", lambda e, c=c: e.activation(out=sq[:, c, :], in_=xt[:, c, :], func=AF.Square, scale=1.0 / 32.0),
                  reads=[bx[c]], writes=[b_sq[c]])
        for c in range(KC):
            sc.op("pe", lambda e, c=c: e.matmul(ps[:, :], onesb[:, :], sq[:, c, :], start=(c == 0), stop=(c == KC - 1)),
                  reads=[b_sq[c], b_ones], writes=[bps], inc=(c == KC - 1))
        rsqrt_eps(rr[:, :], ps[:, :], [bps], [b_rr])
        for c in range(KC):
            sc.op("dve", lambda e, c=c: e.scalar_tensor_tensor(out=xt[:, c, :], in0=xt[:, c, :], scalar=pp[:, PP_FIN + c:PP_FIN + c + 1],
                                                                in1=rr[:, :], op0=ALU.mult, op1=ALU.mult),
                  reads=[bx[c], b_rr, b_pp], writes=[bx[c]])

    def loop_c():
        l = NL - 1
        cur = {"nxt": load_x(xs, [b_xs[0]], 0)}
        for t in range(NT):
            xt, bx = cur["nxt"]

            def mid(t=t):
                if t + 1 < NT:
                    cur["nxt"] = load_x(xs, [b_xs[t + 1]], t + 1)
            ffn(l, 2, xt, bx, mid=mid)
            final_norm(xt, bx)
            sc.dma("pool", [(xview(yout, t), xt[:, :, :])], reads=bx, writes=[b_y])

    def dump():
        sc.barrier()
        b_dbg = B("dbg")
        pairs = [(yout[c], xs[c]) for c in range(KC)]
        sc.dma("pool", pairs, writes=[b_dbg])
        sc.barrier()

    stage = 0
    done = False
    for l in range(NL):
        loop_a(l)
        sc.barrier()
        stage += 1
        if stop == stage:
            dump(); done = True; break
        loop_b(l)
        sc.barrier()
        stage += 1
        if stop == stage:
            dump(); done = True; break
    if not done:
        loop_c()
        sc.barrier()

    with es:
        sems = [es.enter_context(nc.semaphore(f"s{i}")) for i in range(sc.nsem)]
        es.enter_context(nc.allow_non_contiguous_dma(reason="tiny per-row scalars"))
        block = es.enter_context(nc.Block())

        def emit(eng_name):
            def body(e):
                for rec in sc.eng[eng_name].ops:
                    if rec[0] == "wait":
                        e.wait_ge(sems[rec[1]], rec[2])
                    else:
                        ins = rec[1](e)
                        if rec[2] is not None:
                            ins.then_inc(sems[rec[2]], rec[3])
            return body

        block.sync(emit("sp"))
        block.tensor(emit("pe"))
        block.scalar(emit("act"))
        block.vector(emit("dve"))
        block.gpsimd(emit("pool"))
    return nc


_CACHE = {}


def kernel(**inputs):
    inp = {k: np.asarray(v) for k, v in inputs.items()}
    x = inp["x"].astype(np.float32, copy=False)
    if "nc" not in _CACHE:
        _CACHE["nc"] = build_program()
    nc = _CACHE["nc"]
    Wf = pack_weights(inp)
    pps = [pack_params(inp, r) for r in range(2)]
    css = [make_consts(r) for r in range(2)]
    mks = [make_masks(r) for r in range(2)]
    in_maps = []
    for c in range(NCORES):
        b, r = c // 2, c % 2
        xl = x[b].reshape(NB, T, D)[r::2].reshape(SL, D)
        xT = np.ascontiguousarray(xl.T).reshape(KC, 128, SL)
        in_maps.append({"xin": xT, "wf": Wf, "pp": pps[r], "cs": css[r], "mk": mks[r]})
    res = run_bass_kernel_spmd(nc, in_maps, core_ids=list(range(NCORES)))
    out = np.empty((NCORES // 2, S, D), np.float32)
    for c in range(NCORES):
        b, r = c // 2, c % 2
        yl = res.results[c]["y"].reshape(D, SL).T.reshape(NT, T, D)
        out[b].reshape(NB, T, D)[r::2] = yl
    return out
```
